# Optimizing a Trainium2 kernel written in Bass

```python
import math
import jax, jax.numpy as jnp
from jax import lax
import numpy as np

D_MODEL = 1024
BATCH = 2
SEQ = 8192
DEPTH = 1
DEC_BATCH = 8
DEC_SEQ = 8192
PAST_LEN = 128

F_WIDTH = D_MODEL
F_GROUPS = 4
F_GROUP_DIM = F_WIDTH // F_GROUPS
H_WIDTH = D_MODEL
H_ORDER = 2
N_DIR = 2
SHORT_CONV = 3
FILT_EMB = 33
FILT_BANDS = (FILT_EMB - 1) // 2
FILT_HIDDEN = 64
N_BRANCH = 2
RMS_EPS = 1e-6
DECAY_TARGET = 1e-2
FAST_DECAY_PCT = 0.3
SLOW_DECAY_PCT = 1.5
PROJ_COLS = 2 * F_WIDTH + (H_ORDER + 1) * H_WIDTH + H_WIDTH + N_BRANCH * D_MODEL

kernel_name = "fourier_hyena_gated_encoder"


def rmsnorm(x, g):
    xf = x.astype(jnp.float32)
    y = xf * lax.rsqrt(jnp.mean(xf * xf, axis=-1, keepdims=True) + RMS_EPS)
    return (y * g.astype(jnp.float32)).astype(x.dtype)


def short_conv(u, w):
    L = u.shape[1]
    up = jnp.pad(u, ((0, 0), (1, 1), (0, 0)))
    return sum(up[:, k:k + L] * w[k] for k in range(SHORT_CONV))


def hyena_filters(L, w1, b1, f1, w2, b2, f2, w3, decay):
    f32 = jnp.float32
    pos = jnp.arange(L, dtype=f32)
    t = pos / (L - 1)
    bands = jnp.linspace(1e-4, FILT_BANDS - 1, FILT_BANDS, dtype=f32)
    ang = (2.0 * math.pi * pos / L)[:, None] * bands[None]
    z = jnp.concatenate([t[:, None], jnp.cos(ang), -jnp.sin(ang)], axis=-1)
    h = jnp.sin(f1.astype(f32) * (z @ w1.astype(f32) + b1.astype(f32)))
    h = jnp.sin(f2.astype(f32) * (h @ w2.astype(f32) + b2.astype(f32)))
    h = (h @ w3.astype(f32)).reshape(L, N_DIR, H_ORDER, H_WIDTH)
    h = h * jnp.exp(-t[:, None, None, None] * jnp.abs(decay.astype(f32))[None])
    h = h / jnp.sum(jnp.abs(h), axis=(0, 1), keepdims=True)
    k = jnp.concatenate([h[:, 0],
                         jnp.zeros((1, H_ORDER, H_WIDTH), f32),
                         h[:L - 1, 1][::-1]], axis=0)
    return jnp.fft.rfft(k, axis=0)


def long_conv(u, kf, d):
    L = u.shape[1]
    uf = jnp.fft.rfft(u, n=2 * L, axis=1)
    y = jnp.fft.irfft(uf * kf[None], n=2 * L, axis=1)[:, :L]
    return y + d * u


def encoder_layer(x, g_pre, w_in, w_short, filt_w1, filt_b1, filt_freq1, filt_w2,
                  filt_b2, filt_freq2, filt_w3, filt_decay, hyena_d,
                  w_fourier_out, w_hyena_out, b_merge, w_out, g_post):
    B, L, _ = x.shape
    dt = x.dtype
    xn = rmsnorm(x, g_pre)
    p = xn @ w_in
    fv, fg, hv, hg, mg = jnp.split(
        p, [F_WIDTH, 2 * F_WIDTH, 2 * F_WIDTH + 3 * H_WIDTH,
            2 * F_WIDTH + 4 * H_WIDTH], axis=-1)

    fz = fv.astype(jnp.float32).reshape(B, L, F_GROUPS, F_GROUP_DIM)
    fz = jnp.fft.fftn(fz, axes=(1, 3), norm="ortho").real
    fz = fz.reshape(B, L, F_WIDTH).astype(dt)
    y_f = (fz * jax.nn.silu(fg)) @ w_fourier_out

    hv = short_conv(hv, w_short)
    v, x1, x2 = jnp.split(hv, 3, axis=-1)
    kf = hyena_filters(L, filt_w1, filt_b1, filt_freq1, filt_w2, filt_b2,
                       filt_freq2, filt_w3, filt_decay)
    z = v.astype(jnp.float32)
    for n, gate in enumerate((x1, x2)):
        z = gate.astype(jnp.float32) * long_conv(z, kf[:, n],
                                                 hyena_d[n].astype(jnp.float32))
    y_h = (z.astype(dt) * jax.nn.silu(hg)) @ w_hyena_out

    g_f, g_h = jnp.split(jax.nn.sigmoid(mg + b_merge), 2, axis=-1)
    out = (g_f * y_f + g_h * y_h) @ w_out
    return x + rmsnorm(out, g_post)


def setup_inputs(seed: int = 0) -> dict:
    key = jax.random.key(seed)
    ks = jax.random.split(key, 20)
    f32 = jnp.float32
    nrm = lambda k, shape, s: jax.random.normal(k, shape, f32) * s
    base_decay = jnp.abs(jnp.linspace(math.log(DECAY_TARGET) / FAST_DECAY_PCT,
                                      math.log(DECAY_TARGET) / SLOW_DECAY_PCT,
                                      H_WIDTH, dtype=f32))
    decay = jnp.broadcast_to(base_decay, (DEPTH, N_DIR, H_ORDER, H_WIDTH))
    decay = decay + nrm(ks[11], (DEPTH, N_DIR, H_ORDER, H_WIDTH), 0.1)
    return {
        "x_prompt": nrm(ks[0], (BATCH, SEQ, D_MODEL), 1.0),
        "x_sample": nrm(ks[1], (DEC_BATCH, DEC_SEQ, D_MODEL), 1.0),
        "g_pre": 1.0 + nrm(ks[2], (DEPTH, D_MODEL), 0.02),
        "w_in": nrm(ks[3], (DEPTH, D_MODEL, PROJ_COLS), D_MODEL ** -0.5),
        "w_short": nrm(ks[4], (DEPTH, SHORT_CONV, 3 * H_WIDTH), SHORT_CONV ** -0.5),
        "filt_w1": nrm(ks[5], (DEPTH, FILT_EMB, FILT_HIDDEN), FILT_EMB ** -0.5),
        "filt_b1": nrm(ks[6], (DEPTH, FILT_HIDDEN), 0.1),
        "filt_freq1": 1.0 + nrm(ks[7], (DEPTH, FILT_HIDDEN), 0.05),
        "filt_w2": nrm(ks[8], (DEPTH, FILT_HIDDEN, FILT_HIDDEN), FILT_HIDDEN ** -0.5),
        "filt_b2": nrm(ks[9], (DEPTH, FILT_HIDDEN), 0.1),
        "filt_freq2": 1.0 + nrm(ks[10], (DEPTH, FILT_HIDDEN), 0.05),
        "filt_w3": nrm(ks[12], (DEPTH, FILT_HIDDEN, N_DIR * H_ORDER * H_WIDTH),
                       FILT_HIDDEN ** -0.5),
        "filt_decay": decay,
        "hyena_d": nrm(ks[13], (DEPTH, H_ORDER, H_WIDTH), 0.1),
        "w_fourier_out": nrm(ks[14], (DEPTH, F_WIDTH, D_MODEL), F_WIDTH ** -0.5),
        "w_hyena_out": nrm(ks[15], (DEPTH, H_WIDTH, D_MODEL), H_WIDTH ** -0.5),
        "b_merge": nrm(ks[16], (DEPTH, N_BRANCH * D_MODEL), 0.01),
        "w_out": nrm(ks[17], (DEPTH, D_MODEL, D_MODEL), D_MODEL ** -0.5),
        "g_post": 1.0 + nrm(ks[18], (DEPTH, D_MODEL), 0.02),
    }


def reference(x_prompt, x_sample, g_pre, w_in, w_short, filt_w1, filt_b1,
              filt_freq1, filt_w2, filt_b2, filt_freq2, filt_w3, filt_decay,
              hyena_d, w_fourier_out, w_hyena_out, b_merge, w_out, g_post):
    y_prompt = x_prompt
    y_sample = x_sample
    for i in range(DEPTH):
        layer_params = (g_pre[i], w_in[i], w_short[i], filt_w1[i], filt_b1[i],
                        filt_freq1[i], filt_w2[i], filt_b2[i], filt_freq2[i],
                        filt_w3[i], filt_decay[i], hyena_d[i], w_fourier_out[i],
                        w_hyena_out[i], b_merge[i], w_out[i], g_post[i])
        y_prompt = encoder_layer(y_prompt, *layer_params)
        y_sample = encoder_layer(y_sample, *layer_params)
    return (y_prompt, y_sample)
```

```python
import contextlib
import numpy as np
import ml_dtypes
import concourse.bass as bass
import concourse.mybir as mybir
from concourse.bass_utils import run_bass_kernel_spmd

F32 = mybir.dt.float32
BF16 = mybir.dt.bfloat16
AF = mybir.ActivationFunctionType
ALU = mybir.AluOpType
AX = mybir.AxisListType

L = 8192
N2 = 16384
D = 1024
PI2 = 2.0 * np.pi
NCORES = 8
NSEQ = 2
RMS_EPS = 1e-6


class Tok:
    __slots__ = ("name", "w", "r", "multi", "wl", "prev_r")

    def __init__(self, name="", multi=False):
        self.name = name
        self.w = None
        self.r = []
        self.multi = multi
        self.wl = []
        self.prev_r = []


def MT(name=""):
    return Tok(name, multi=True)


class Op:
    __slots__ = ("eng", "fn", "deps", "marked", "val", "sem", "is_dma", "last")

    def __init__(self, eng, fn, is_dma=False):
        self.eng = eng
        self.fn = fn
        self.deps = []
        self.marked = False
        self.val = None
        self.sem = None
        self.is_dma = is_dma
        self.last = True


class Prog:
    ENGS = ("pe", "act", "dve", "pool", "sp")

    def __init__(self, nc):
        self.nc = nc
        self.ops = {e: [] for e in self.ENGS}
        self.dma_streams = {}
        self.last_op = {e: None for e in self.ENGS}
        self.pending_bar = {e: None for e in self.ENGS}
        self.open_group = {}

    def barrier(self, skip=()):
        deps = [o for o in self.last_op.values() if o is not None]
        for name, lst in self.dma_streams.items():
            if lst and name not in skip:
                deps.append(lst[-1])
        for e in self.ENGS:
            old = self.pending_bar[e] or []
            self.pending_bar[e] = old + deps

    def op(self, eng, fn, reads=(), writes=(), dma=None, last=True, after=()):
        o = Op(eng, fn, is_dma=dma is not None)
        raw = set()
        deps = []
        for t in after:
            if t.multi:
                deps.extend(t.wl)
                deps.extend(t.prev_r)
            elif t.w is not None:
                deps.append(t.w)
            deps.extend(t.r)
        for t in reads:
            if t.multi:
                deps.extend(t.wl)
            elif t.w is not None:
                deps.append(t.w)
        for t in writes:
            if t.multi:
                assert not any(t is q for q in reads), "op reads+writes a multi token: " + t.name
                if t.r:
                    t.prev_r = t.r
                    t.r = []
                    t.wl = []
                deps.extend(t.prev_r)
            else:
                if t.w is not None:
                    deps.append(t.w)
                deps.extend(t.r)
        if self.pending_bar[eng] is not None:
            for d in self.pending_bar[eng]:
                deps.append(d)
                raw.add(id(d))
            self.pending_bar[eng] = None
        seen = set()
        opengrp = self.open_group.get(dma, ()) if dma is not None else ()
        for d in deps:
            if d is o or id(d) in seen:
                continue
            seen.add(id(d))
            if any(d is g for g in opengrp):
                continue
            if not (d.is_dma or o.is_dma) and d.eng == eng:
                if eng == "pe":
                    continue
            o.deps.append(d)
        for t in reads:
            t.r.append(o)
        for t in writes:
            if t.multi:
                t.wl.append(o)
            else:
                t.w = o
                t.r = []
        self.ops[eng].append(o)
        if dma is not None:
            self.dma_streams.setdefault(dma, []).append(o)
            o.sem = dma
            o.last = last
            if last:
                self.open_group[dma] = []
            else:
                self.open_group.setdefault(dma, []).append(o)
        else:
            self.last_op[eng] = o
        return o

    def emit(self, final_streams=()):
        nc = self.nc
        for e in self.ENGS:
            for o in self.ops[e]:
                for d in o.deps:
                    d.marked = True
        with contextlib.ExitStack() as st:
            esem = {e: st.enter_context(nc.semaphore("s_" + e)) for e in self.ENGS}
            dsem = {n: st.enter_context(nc.semaphore("d_" + n)) for n in self.dma_streams}
            for e in self.ENGS:
                c = 0
                for o in self.ops[e]:
                    if o.is_dma:
                        continue
                    if o.marked:
                        c += 1
                        o.val = c
                        o.sem = esem[e]
            for name, lst in self.dma_streams.items():
                c = 0
                pend = []
                for o in lst:
                    c += 16
                    pend.append(o)
                    o.sem = dsem[name]
                    o.marked = True
                    if o.last:
                        for q in pend:
                            q.val = c
                        pend = []
                assert not pend, name
            engobj = {"pe": "tensor", "act": "scalar", "dve": "vector", "pool": "gpsimd", "sp": "sync"}
            block = st.enter_context(nc.Block())
            self.n_waits = {e: 0 for e in self.ENGS}

            def make(e):
                def body(eng):
                    known = {}
                    for o in self.ops[e]:
                        need = {}
                        for d in o.deps:
                            k = id(d.sem)
                            if known.get(k, 0) >= d.val:
                                continue
                            if k not in need or need[k][1] < d.val:
                                need[k] = (d.sem, d.val)
                        for k, (sem, val) in need.items():
                            eng.wait_ge(sem, val)
                            known[k] = val
                            self.n_waits[e] += 1
                        ins = o.fn(eng)
                        if o.is_dma:
                            ins.then_inc(o.sem, 16)
                        elif o.marked:
                            ins.then_inc(o.sem, 1)
                    if e == "sp":
                        for name in final_streams:
                            lst = self.dma_streams.get(name)
                            if lst:
                                eng.wait_ge(dsem[name], lst[-1].val)
                return body

            for e in self.ENGS:
                if not self.ops[e] and e != "sp":
                    continue
                getattr(block, engobj[e])(make(e))


def _bperm():
    p = np.arange(128)
    return 2 * (p % 64) + p // 64


def _tables():
    bf = lambda x: np.ascontiguousarray(np.asarray(x, np.float32).astype(ml_dtypes.bfloat16))
    t = {}
    t["t_ident"] = bf(np.eye(128))
    def F1(na):
        a = np.arange(na)[:, None].astype(np.float64)
        k1 = np.arange(64)[None, :].astype(np.float64)
        th = PI2 * a * (k1 + 0.5) / 128.0
        return np.concatenate([np.cos(th), -np.sin(th)], axis=1)
    f1 = F1(64)
    t["t_F1d"] = bf(np.concatenate([f1, f1], 0))
    t["t_F1k"] = bf(F1(128))
    b = _bperm()[:, None].astype(np.float64)
    k2 = np.arange(128)[None, :].astype(np.float64)
    M = np.zeros((128, 64, 3, 128))
    for k1 in range(64):
        th = PI2 * (b * (k1 + 0.5) / N2 + b * k2 / 128.0)
        M[:, k1, 0] = np.cos(th)
        M[:, k1, 1] = -np.sin(th)
        M[:, k1, 2] = np.sin(th)
    t["t_M"] = bf(M)
    k2c = np.arange(128)[:, None].astype(np.float64)
    bb = np.arange(128)[None, :].astype(np.float64)
    th = PI2 * bb * k2c / 128.0
    Gr, Gi = np.cos(th), np.sin(th)
    G = np.zeros((128, 2, 2, 64))
    for hb in range(2):
        G[:, hb, 0] = Gr[:, 64 * hb:64 * hb + 64]
        G[:, hb, 1] = Gi[:, 64 * hb:64 * hb + 64]
    t["t_G"] = bf(G.reshape(128, 256))
    k1 = np.arange(64)[:, None].astype(np.float64)
    a = np.arange(64)[None, :].astype(np.float64)
    A = np.zeros((128, 128, 2, 64))
    for bi in range(128):
        ph = PI2 * (k1 + 0.5) * (128.0 * a + bi) / N2
        Br, Bi = np.cos(ph), np.sin(ph)
        A[:64, bi, 0] = Br
        A[64:, bi, 0] = -Bi
        A[:64, bi, 1] = -Bi
        A[64:, bi, 1] = -Br
    t["t_A"] = bf(A * (2.0 / N2))
    aa = np.arange(64)[:, None].astype(np.float64)
    kk = np.arange(64)[None, :].astype(np.float64)
    th = PI2 * aa * kk / 64.0
    Fr, Fi = np.cos(th), -np.sin(th)
    ff = np.stack([np.concatenate([Fr, Fi], 1), np.concatenate([Fi, -Fr], 1)], 1)
    t["t_FF"] = bf(np.concatenate([ff, ff], 0))
    jj = np.arange(128)
    k2o = (2 * (jj % 64) + jj // 64)[None, :].astype(np.float64)
    FM = np.zeros((128, 64, 2, 128))
    for k1i in range(64):
        th = PI2 * (b * k1i / L + b * k2o / 128.0)
        FM[:, k1i, 0] = np.cos(th)
        FM[:, k1i, 1] = np.sin(th)
    t["t_FM"] = bf(FM / np.sqrt(L))
    c = np.arange(256)[:, None].astype(np.float64)
    m = np.arange(256)[None, :].astype(np.float64)
    th = PI2 * c * m / 256.0
    CS = np.stack([np.cos(th), np.sin(th)], 1) / 16.0
    t["t_CS"] = bf(CS.reshape(2, 128, 2, 256).transpose(1, 0, 2, 3))
    bj = _bperm()[:, None]
    an = np.arange(128)[None, :]
    n = (128 * an + bj).reshape(-1)
    pos = np.where(n < L, n, 2 * L - 1 - n).astype(np.float32)
    pos = np.clip(pos, 0, L - 1)
    tt = pos / np.float32(L - 1)
    bands = np.linspace(1e-4, 15, 16, dtype=np.float32)
    ang = (np.float32(PI2) * pos / np.float32(L))[:, None] * bands[None]
    z = np.concatenate([tt[:, None], np.cos(ang), -np.sin(ang)], axis=-1)
    t["t_zf"] = np.ascontiguousarray(z.T.astype(np.float32))
    t["t_tneg"] = np.ascontiguousarray((-tt.reshape(128, 128).T).astype(np.float32))
    sg = np.ones((128, 1), np.float32)
    sg[64:] = -1.0
    t["t_sgn"] = sg
    return t


TABLE_SHAPES = {
    "t_ident": ([128, 128], BF16), "t_F1d": ([128, 128], BF16), "t_F1k": ([128, 128], BF16),
    "t_M": ([128, 64, 3, 128], BF16), "t_G": ([128, 256], BF16), "t_A": ([128, 128, 2, 64], BF16),
    "t_FF": ([128, 2, 128], BF16), "t_FM": ([128, 64, 2, 128], BF16), "t_CS": ([128, 2, 2, 256], BF16),
    "t_zf": ([33, 16384], F32), "t_tneg": ([128, 128], F32), "t_sgn": ([128, 1], F32),
}

WEIGHT_SHAPES = {
    "w_in": [D, 8192], "g_pre": [1, D], "g_post": [1, D], "ws_t": [128, 72],
    "fw1": [33, 64], "fb1": [64, 1], "ff1": [64, 1], "fw2": [64, 64], "fb2": [64, 1], "ff2": [64, 1],
    "fw3": [64, 4096], "fdecay": [4, D], "hd": [2, D], "wf": [D, D], "wh": [D, D], "wo": [D, D],
    "bm_t": [128, 16],
}

ARENA_BYTES = 180 * 1024


class K:
    def __init__(self, dbg=None):
        self.dbg = dbg or {}
        self.nc = bass.Bass("TRN2", target_bir_lowering=False)
        self.st = contextlib.ExitStack()
        self.P = Prog(self.nc)
        self.dbg_outs = []
        self.rr = 0
        self.bank_i = 0

    def dram_in(self, name, shape, dt=F32):
        return self.nc.dram_tensor(name, list(shape), dt, kind="ExternalInput").ap()

    def dram_out(self, name, shape, dt=F32):
        return self.nc.dram_tensor(name, list(shape), dt, kind="ExternalOutput").ap()

    def sb(self, name, shape, dt):
        return self.st.enter_context(self.nc.sbuf_tensor(name, list(shape), dt))

    def av(self, off_bytes, shape, dt=BF16):
        n = int(np.prod(shape[1:]))
        el = 2 if dt == BF16 else 4
        assert off_bytes % 4 == 0
        assert off_bytes + n * el <= ARENA_BYTES, (off_bytes, shape)
        a = self.arena[0:shape[0], off_bytes // 2: off_bytes // 2 + n * el // 2]
        if dt != BF16:
            a = a.bitcast(dt)
        if len(shape) == 3:
            a = a.rearrange("p (a b) -> p a b", a=shape[1])
        elif len(shape) == 4:
            a = a.rearrange("p (a b c) -> p a b c", a=shape[1], b=shape[2])
        elif len(shape) == 5:
            a = a.rearrange("p (a b c d) -> p a b c d", a=shape[1], b=shape[2], c=shape[3])
        return a

    def bank(self, n=1):
        i = self.bank_i
        if i % n:
            i += n - i % n
        if i + n > 8:
            i = 0
        self.bank_i = (i + n) % 8
        return self.psum[:, 512 * i: 512 * (i + n)], self.bank_tok[i:i + n]

    def ev(self):
        self.rr += 1
        return "act" if self.rr % 2 else "dve"

    def op(self, *a, **k):
        return self.P.op(*a, **k)

    def copy(self, eng, out, in_, reads, writes, after=()):
        if eng == "act":
            return self.op("act", lambda e: e.activation(out=out, in_=in_, func=AF.Copy), reads=reads, writes=writes,
                           after=after)
        if eng == "dve":
            return self.op("dve", lambda e: e.tensor_copy(out=out, in_=in_), reads=reads, writes=writes, after=after)
        return self.op("pool", lambda e: e.tensor_copy(out=out, in_=in_), reads=reads, writes=writes, after=after)

    def dma(self, out, in_, reads, writes, stream, last=True, eng="sp"):
        return self.op(eng, lambda e: e.dma_start(out=out, in_=in_), reads=reads, writes=writes, dma=stream, last=last)

    def dump(self, name, ap, shape, dt, reads):
        o = self.dram_out("dbg_" + name, shape, dt)
        t = Tok()
        self.dma(o, ap, reads, [t], "dbg_" + name)
        self.dbg_outs.append("dbg_" + name)

    def build(self):
        nc = self.nc
        d = self.dbg
        self.xs = self.dram_in("xs", [NSEQ, L, D])
        self.W = {k: self.dram_in(k, s) for k, s in WEIGHT_SHAPES.items()}
        self.T = {k: self.dram_in(k, s, dt) for k, (s, dt) in TABLE_SHAPES.items()}
        self.y = self.dram_out("y", [NSEQ, L, D])
        self.s_xnT = self.dram_out("s_xnT", [16, 128, 4096], BF16)
        self.s_u = self.dram_out("s_u", [2, 8, 128, L], BF16)
        self.s_ka = self.dram_out("s_ka", [2, 8, 128, 16384], BF16)
        self.s_wbf = self.dram_out("s_wbf", [7, 8, 128, 1024], BF16)
        self.s_wt = self.dram_out("s_wt", [128, 8, 5120], BF16)
        self.s_rn = self.dram_out("s_rn", [2, 8, 128, 1], F32)
        self.arena = self.sb("arena", [128, ARENA_BYTES // 2], BF16)
        self.psum = self.st.enter_context(nc.psum_tensor("psum", [128, 4096], F32))
        self.bank_tok = [Tok("bank%d" % i) for i in range(8)]
        self.tok_uF, self.tok_uH = Tok("uF"), Tok("uH")
        self.ident = self.sb("ident", [128, 128], BF16)
        self.F1d = self.sb("F1d", [128, 128], BF16)
        self.F1k = self.sb("F1k", [128, 128], BF16)
        self.G = self.sb("G", [128, 256], BF16)
        self.FF = self.sb("FF", [128, 2, 128], BF16)
        self.gpre = self.sb("gpre", [128, D], F32)
        self.gpost = self.sb("gpost", [128, D], F32)
        self.bm = self.sb("bm", [128, 16], F32)
        self.ws = self.sb("ws", [128, 72], F32)
        self.sgn = self.sb("sgn", [128, 1], F32)
        self.ones = self.sb("ones", [128, 128], F32)
        self.negpi = self.sb("negpi", [128, 1], F32)
        self.epsb = self.sb("epsb", [128, 1], F32)
        self.onesb = self.sb("onesb", [1, 128], BF16)
        self.ctok = Tok("consts")
        cl = [("t_ident", self.ident), ("t_F1d", self.F1d), ("t_F1k", self.F1k), ("t_G", self.G),
              ("t_FF", self.FF), ("t_sgn", self.sgn)]
        for i, (n, tgt) in enumerate(cl):
            self.dma(tgt[:], self.T[n], [], [self.ctok], "const", last=False)
        self.dma(self.bm[:], self.W["bm_t"], [], [self.ctok], "const", last=False)
        self.dma(self.ws[:], self.W["ws_t"], [], [self.ctok], "const", last=False)
        self.dma(self.gpre[:], self.W["g_pre"].broadcast_to([128, D]), [], [self.ctok], "const", last=False)
        self.dma(self.gpost[:], self.W["g_post"].broadcast_to([128, D]), [], [self.ctok], "const", last=True)
        self.op("pool", lambda e: e.memset(self.ones[:], 1.0), writes=[self.ctok])
        self.op("pool", lambda e: e.memset(self.negpi[:], float(-np.pi)), writes=[self.ctok])
        self.op("pool", lambda e: e.memset(self.epsb[:], float(RMS_EPS)), writes=[self.ctok])
        self.op("pool", lambda e: e.memset(self.onesb[:], 1.0), writes=[self.ctok])
        self.P.barrier()

        if d.get("prologue_w", True):
            self.prologue_w()
            self.P.barrier()
        if d.get("prologue_k", True):
            self.prologue_k()
            self.P.barrier()
        for s in range(d.get("nseq", NSEQ)):
            if d.get("phase0", True):
                self.phase0(s)
                self.P.barrier()
            for cb in d.get("cbs", range(8)):
                if d.get("phaseF", True):
                    self.phaseF(s, cb)
                    self.P.barrier(skip=("u_stF", "u_stH"))
                if d.get("phaseH", True):
                    self.phaseH(s, cb)
                    self.P.barrier(skip=("u_stF", "u_stH"))
            self.P.barrier()
            if d.get("tail", True):
                self.tail(s)
                self.P.barrier()
        finals = [n for n in self.P.dma_streams]
        self.P.emit(final_streams=finals)
        self.st.close()
        return nc

    def prologue_w(self):
        win = self.W["w_in"].rearrange("(kc p) n -> p kc n", p=128)
        wsts = [self.av(0, [128, 8, 1024], F32), self.av(64 * 1024, [128, 8, 1024], F32)]
        wbss = [self.av(32 * 1024, [128, 8, 1024], BF16), self.av(96 * 1024, [128, 8, 1024], BF16)]
        t_wsts, t_wbss = [Tok(), Tok()], [Tok(), Tok()]
        col0 = {2: 1024, 3: 2048, 4: 3072, 5: 4096, 6: 5120}
        pi = 0
        for slot in range(2, 7):
            wst, wbs, t_wst, t_wbs = wsts[pi % 2], wbss[pi % 2], t_wsts[pi % 2], t_wbss[pi % 2]
            self.dma(wst, win[:, :, col0[slot]:col0[slot] + 1024], [], [t_wst], "pw_ld%d" % (pi % 2))
            self.copy("act" if pi % 2 else "dve", wbs, wst, [t_wst], [t_wbs])
            for cb in range(8):
                self.dma(self.s_wbf[slot, cb].rearrange("p (kc c) -> p kc c", kc=8),
                         wbs[:, :, 128 * cb:128 * cb + 128], [t_wbs], [Tok()], "pw_st%d" % (pi % 2), last=(cb == 7))
            pi += 1
        srcs = [win[:, :, 6144:7168], win[:, :, 7168:8192],
                self.W["wf"].rearrange("(kc p) n -> p kc n", p=128),
                self.W["wh"].rearrange("(kc p) n -> p kc n", p=128),
                self.W["wo"].rearrange("(kc p) n -> p kc n", p=128)]
        for i, src in enumerate(srcs):
            wst, wbs, t_wst, t_wbs = wsts[pi % 2], wbss[pi % 2], t_wsts[pi % 2], t_wbss[pi % 2]
            self.dma(wst, src, [], [t_wst], "pw_ld%d" % (pi % 2))
            self.copy("act" if pi % 2 else "dve", wbs, wst, [t_wst], [t_wbs])
            self.dma(self.s_wt[:, :, 1024 * i:1024 * i + 1024], wbs, [t_wbs], [Tok()], "pw_st%d" % (pi % 2))
            pi += 1
        wst, wbs, t_wst, t_wbs = wsts[0], wbss[0], t_wsts[0], t_wbss[0]
        cs = self.av(48 * 1024, [128, 2, 2, 256], BF16)
        t_cs = Tok()
        self.dma(cs, self.T["t_CS"], [], [t_cs], "pw_cs")
        wfT = self.av(52 * 1024, [128, 2, 1024], BF16)
        abst = self.av(56 * 1024, [128, 8, 2, 256], BF16)
        t_wfT, t_ab = Tok(), Tok()
        for g in range(4):
            self.dma(wst[:, :, 0:256], win[:, :, 256 * g:256 * g + 256], [], [t_wst], "pw_ld0")
            self.copy("dve", wbs[:, :, 0:256], wst[:, :, 0:256], [t_wst], [t_wbs])
            for half in range(2):
                for kq in range(2):
                    ps, pt = self.bank()
                    for k4 in range(4):
                        kc = 4 * kq + k4
                        self.op("pe", lambda e, ps=ps, kc=kc, half=half, k4=k4: e.matmul(
                            ps[:, 128 * k4:128 * k4 + 128], lhsT=wbs[:, kc, 128 * half:128 * half + 128],
                            rhs=self.ident[:], start=True, stop=True), reads=[t_wbs, self.ctok], writes=pt)
                    self.copy(self.ev(), wfT[:, half, 512 * kq:512 * kq + 512], ps, pt, [t_wfT])
            for kc in range(8):
                ps, pt = self.bank()
                for ab in range(2):
                    for half in range(2):
                        self.op("pe", lambda e, ps=ps, kc=kc, ab=ab, half=half: e.matmul(
                            ps[:, 256 * ab:256 * ab + 256], lhsT=wfT[:, half, 128 * kc:128 * kc + 128],
                            rhs=cs[:, half, ab, :], start=(half == 0), stop=(half == 1)),
                            reads=[t_wfT, t_cs], writes=pt)
                self.copy(self.ev(), abst[:, kc, :, :], ps.rearrange("p (a c) -> p a c", a=2), pt, [t_ab])
            for ab in range(2):
                for hc in range(2):
                    cb = 2 * g + hc
                    self.dma(self.s_wbf[ab, cb].rearrange("p (kc c) -> p kc c", kc=8),
                             abst[:, :, ab, 128 * hc:128 * hc + 128], [t_ab], [Tok()], "pw_st2",
                             last=(ab == 1 and hc == 1))

    def prologue_k(self):
        KB = 1024
        h2T = self.av(0, [64, 16384], BF16)
        w3b = self.av(32 * KB, [64, 4096], BF16)
        zfc = self.av(40 * KB, [33, 2048], F32)
        pre = self.av(48 * KB, [64, 2048], F32)
        h1 = self.av(56 * KB, [64, 2048], F32)
        w3s = self.av(64 * KB, [64, 4096], F32)
        rk = self.av(80 * KB, [64, 2048], F32)
        t_rk = Tok()
        sm = self.sb("pk_small", [64, 8 + 64 + 64], F32)
        t_sm, t_w3s, t_w3b, t_h2T = Tok(), Tok(), Tok(), Tok()
        self.dma(sm[:, 0:1], self.W["fb1"], [], [t_sm], "pk_c", last=False)
        self.dma(sm[:, 1:2], self.W["ff1"], [], [t_sm], "pk_c", last=False)
        self.dma(sm[:, 2:3], self.W["fb2"], [], [t_sm], "pk_c", last=False)
        self.dma(sm[:, 3:4], self.W["ff2"], [], [t_sm], "pk_c", last=False)
        self.dma(sm[0:33, 8:72], self.W["fw1"], [], [t_sm], "pk_c", last=False)
        self.dma(sm[:, 72:136], self.W["fw2"], [], [t_sm], "pk_c", last=False)
        self.dma(w3s, self.W["fw3"], [], [t_w3s], "pk_c", last=True)
        self.copy("dve", w3b, w3s, [t_w3s], [t_w3b])
        t_zf, t_pre, t_h1 = Tok(), Tok(), Tok()
        for ch in range(8):
            self.dma(zfc, self.T["t_zf"][:, 2048 * ch:2048 * ch + 2048], [], [t_zf], "pk_zf")
            for layer in range(2):
                for q in range(4):
                    ps, pt = self.bank()
                    if layer == 0:
                        self.op("pe", lambda e, ps=ps, q=q: e.matmul(
                            ps[0:64, :], lhsT=sm[0:33, 8:72], rhs=zfc[:, 512 * q:512 * q + 512],
                            start=True, stop=True), reads=[t_sm, t_zf], writes=pt)
                    else:
                        self.op("pe", lambda e, ps=ps, q=q: e.matmul(
                            ps[0:64, :], lhsT=sm[:, 72:136], rhs=h1[:, 512 * q:512 * q + 512],
                            start=True, stop=True), reads=[t_sm, t_h1], writes=pt)
                    bcol = 0 if layer == 0 else 2
                    self.op("dve", lambda e, ps=ps, q=q, bcol=bcol: e.tensor_scalar(
                        out=pre[:, 512 * q:512 * q + 512], in0=ps[0:64, :], scalar1=sm[:, bcol:bcol + 1],
                        scalar2=sm[:, bcol + 1:bcol + 2], op0=ALU.add, op1=ALU.mult),
                        reads=pt + [t_sm], writes=[t_pre])
                MAGIC = 12582912.0
                self.op("dve", lambda e: e.tensor_scalar(
                    out=rk[:, :], in0=pre[:, :], scalar1=float(1.0 / PI2), scalar2=MAGIC,
                    op0=ALU.mult, op1=ALU.add), reads=[t_pre], writes=[t_rk])
                self.op("dve", lambda e: e.tensor_scalar_add(out=rk[:, :], in0=rk[:, :], scalar1=-MAGIC),
                        reads=[t_rk], writes=[t_rk])
                self.op("dve", lambda e: e.scalar_tensor_tensor(
                    out=pre[:, :], in0=rk[:, :], scalar=float(-PI2), in1=pre[:, :], op0=ALU.mult, op1=ALU.add),
                    reads=[t_rk, t_pre], writes=[t_pre])
                if layer == 0:
                    self.op("act", lambda e: e.activation(out=h1[:, :], in_=pre[:, :], func=AF.Sin,
                                                          scale=1.0),
                            reads=[t_pre, self.ctok], writes=[t_h1])
                else:
                    self.op("act", lambda e, ch=ch: e.activation(
                        out=h2T[:, 2048 * ch:2048 * ch + 2048], in_=pre[:, :], func=AF.Sin,
                        scale=1.0), reads=[t_pre, self.ctok], writes=[t_h2T])
        if self.dbg.get("dump_h2T"):
            self.dump("h2T", h2T, [64, 16384], BF16, [t_h2T])
        ZT = self.av(40 * KB, [128, 128, 128], BF16)
        Es = [self.av(72 * KB, [128, 64, 128], F32), self.av(40 * KB, [128, 64, 128], F32)]
        t_Es = [Tok(), Tok()]
        hk = self.av(104 * KB, [128, 128, 128], BF16)
        KA = self.av(136 * KB, [128, 128, 2, 64], BF16)
        Ms = [self.av((168 + 6 * i) * KB, [128, 8, 3, 128], BF16) for i in range(2)]
        absd = self.sb("pk_absd", [128, 128], F32)
        rs = self.sb("pk_rs", [128, 128], F32)
        rn = self.sb("pk_rn", [128, 128], F32)
        dsk = self.sb("pk_dsk", [128, 128], F32)
        tneg = self.sb("pk_tneg", [128, 128], F32)
        t_dsk, t_tneg = Tok(), Tok()
        self.dma(tneg[:], self.T["t_tneg"], [], [t_tneg], "pk_c2")
        self.P.barrier()
        bp = _bperm()
        t_Ms = [Tok(), Tok()]
        mi = 0
        t_absd, t_E, t_hk, t_rs, t_rn, t_ZT, t_KA = Tok(), Tok(), MT("hk"), Tok(), Tok(), MT("ZTk"), MT("KA")
        t_KA2 = Tok()
        dSb = self.sb("pk_dSb", [1, 128], F32)
        t_hk2 = Tok()
        t_dS = Tok()
        iters = [(o, cb) for o in range(2) for cb in self.dbg.get("kcbs", range(8))]
        absds = [absd, self.sb("pk_absd2", [128, 128], F32)]
        dsks = [dsk, self.sb("pk_dsk2", [128, 128], F32)]
        t_absds = [Tok(), Tok()]
        t_dsks = [Tok(), Tok()]

        def small_loads(n):
            o, cb = iters[n]
            ab, tab = absds[n % 2], t_absds[n % 2]
            self.dma(dsks[n % 2][:], self.W["hd"][o:o + 1, 128 * cb:128 * cb + 128].broadcast_to([128, 128]), [],
                     [t_dsks[n % 2]], "pk_dk%d" % (n % 2))
            for dr in range(2):
                src = self.W["fdecay"][2 * dr + o:2 * dr + o + 1, 128 * cb:128 * cb + 128]
                self.dma(ab[64 * dr:64 * dr + 64, :], src.broadcast_to([64, 128]), [], [tab], "pk_ad%d" % (n % 2),
                         last=(dr == 1))
            self.op("act", lambda e: e.activation(out=ab[:], in_=ab[:], func=AF.Abs), reads=[tab], writes=[tab])

        def ecomp(n, chh):
            E, t_E = Es[chh], t_Es[chh]
            ab, tab = absds[n % 2], t_absds[n % 2]
            aft = [t_ZT] if chh == 1 else []
            self.op("pool", lambda e: e.tensor_tensor(
                out=E[:, :, :], in0=ab[:, 64 * chh:64 * chh + 64].unsqueeze(2).broadcast_to([128, 64, 128]),
                in1=tneg[:].unsqueeze(1).broadcast_to([128, 64, 128]), op=ALU.mult),
                reads=[tab, t_tneg], writes=[t_E], after=aft)
            self.op("act", lambda e: e.activation(out=E[:, :, :], in_=E[:, :, :], func=AF.Exp),
                    reads=[t_E], writes=[t_E])
            self.op("pool", lambda e: e.memset(E[64:65, :, 0:1], 0.0), reads=[t_E], writes=[t_E])

        def h3_hk(n, chh):
            o, cb = iters[n]
            E, t_E = Es[chh], t_Es[chh]
            for jq in range(16):
                ps, pt = self.bank()
                for j8 in range(8):
                    j = 8 * jq + j8
                    for dr in range(2):
                        col = dr * 2048 + o * 1024 + cb * 128 + 64 * chh
                        self.op("pe", lambda e, ps=ps, j8=j8, dr=dr, col=col, j=j: e.matmul(
                            ps[64 * dr:64 * dr + 64, 64 * j8:64 * j8 + 64],
                            lhsT=h2T[:, 128 * j + 64 * dr:128 * j + 64 * dr + 64],
                            rhs=w3b[:, col:col + 64], start=True, stop=True),
                            reads=[t_h2T, t_w3b], writes=pt)
                self.op("dve", lambda e, ps=ps, jq=jq: e.scalar_tensor_tensor(
                    out=hk[:, 64 * chh:64 * chh + 64, 8 * jq:8 * jq + 8],
                    in0=ps.rearrange("p (j c) -> p c j", j=8), scalar=self.sgn[:, 0:1],
                    in1=E[:, :, 8 * jq:8 * jq + 8], op0=ALU.mult, op1=ALU.mult),
                    reads=pt + [t_E, self.ctok], writes=[t_hk], after=[t_hk2])

        def norm(n):
            o, cb = iters[n]
            self.op("dve", lambda e: e.tensor_reduce(out=rs[:], in_=hk[:, :, :], axis=AX.X, op=ALU.add,
                                                     apply_absolute_value=True),
                    reads=[t_hk], writes=[t_rs])
            ps, pt = self.bank()
            self.op("pe", lambda e, ps=ps: e.matmul(ps[0:1, 0:128], lhsT=self.ones[:, 0:1], rhs=rs[:], start=True, stop=True),
                    reads=[t_rs, self.ctok], writes=pt)
            self.op("pe", lambda e, ps=ps: e.matmul(ps[:, 128:129], lhsT=rs[:], rhs=self.ones[:, 0:1], start=True, stop=True),
                    reads=[t_rs, self.ctok], writes=pt)
            dk, tdk = dsks[n % 2], t_dsks[n % 2]
            self.op("dve", lambda e, ps=ps: e.tensor_tensor(out=dSb[0:1, :], in0=ps[0:1, 0:128], in1=dk[0:1, :], op=ALU.mult),
                    reads=pt + [tdk], writes=[t_dS])
            self.op("dve", lambda e: e.tensor_tensor(out=hk[0:1, :, 0], in0=hk[0:1, :, 0], in1=dSb[0:1, :], op=ALU.add),
                    reads=[t_hk, t_dS], writes=[t_hk2])
            self.op("dve", lambda e, ps=ps: e.reciprocal(out=rn[:, 0:1], in_=ps[:, 128:129]), reads=pt, writes=[t_rn])
            self.dma(self.s_rn[o, cb], rn[:, 0:1], [t_rn], [Tok()], "rn_st")
            if self.dbg.get("dump_hk") and o == 0:
                self.dump("hk%d" % cb, hk, [128, 128, 128], BF16, [t_hk])

        def fft1(n):
            for c4 in range(32):
                ps, pt = self.bank()
                for ci in range(4):
                    c = 4 * c4 + ci
                    self.op("pe", lambda e, ps=ps, ci=ci, c=c: e.matmul(
                        ps[:, 128 * ci:128 * ci + 128], lhsT=hk[:, c, :], rhs=self.F1k[:],
                        start=True, stop=True), reads=[t_hk, t_hk2, self.ctok], writes=pt)
                self.copy(self.ev(), ZT[:, :, 4 * c4:4 * c4 + 4], ps.rearrange("p (c k) -> p k c", c=4), pt, [t_ZT],
                          after=[t_Es[1]])

        mi_box = [0]

        def fft2(n):
            o, cb = iters[n]
            for k1q in range(8):
                mi = mi_box[0]
                Mb = Ms[mi % 2]
                tM = t_Ms[mi % 2]
                self.dma(Mb, self.T["t_M"][:, 8 * k1q:8 * k1q + 8], [], [tM], "Ms%d" % (mi % 2))
                mi_box[0] += 1
                for kp in range(4):
                    ps, pt = self.bank()
                    for ki in range(2):
                        k8 = 2 * kp + ki
                        k1 = 8 * k1q + k8
                        base = 256 * ki
                        mm = [(0, k1, 0, True, False), (2, 64 + k1, 0, False, True),
                              (1, k1, 128, True, False), (0, 64 + k1, 128, False, True)]
                        for mi_, (mt, zrow, off, sta, sto) in enumerate(mm):
                            self.op("pe", lambda e, ps=ps, Mb=Mb, k8=k8, mt=mt, zrow=zrow, off=off, sta=sta,
                                    sto=sto, base=base: e.matmul(
                                ps[:, base + off:base + off + 128], lhsT=Mb[:, k8, mt, :], rhs=ZT[:, zrow, :],
                                start=sta, stop=sto), reads=[tM, t_ZT], writes=pt)
                    k1a = 8 * k1q + 2 * kp
                    self.copy(self.ev(), KA[:, :, :, k1a:k1a + 2], ps.rearrange("p (k h c) -> p c h k", k=2, h=2), pt,
                              [t_KA])
            self.dma(self.s_ka[o, cb], KA.rearrange("p c h k -> p (c h k)"), [t_KA], [Tok()], "ka_st")

        NI = len(iters)
        small_loads(0)
        ecomp(0, 0)
        for n in range(NI):
            if n + 1 < NI:
                small_loads(n + 1)
            ecomp(n, 1)
            h3_hk(n, 0)
            h3_hk(n, 1)
            norm(n)
            if n + 1 < NI:
                ecomp(n + 1, 0)
            fft1(n)
            fft2(n)

    def phase0(self, s):
        KB = 1024
        xts = [self.av(16 * KB * i, [128, 4, D], F32) for i in range(2)]
        xn = self.av(32 * KB, [128, 4, D], BF16)
        xnTs = [self.av((40 + 8 * i) * KB, [128, 8, 512], BF16) for i in range(2)]
        sqj = self.av(56 * KB, [128, D], BF16)
        st = self.sb("p0_st%d" % s, [128, 16 * 4 * 2], F32)
        t_x = [Tok(), Tok()]
        t_xn, t_sq = MT("xn"), Tok()
        t_xnT = [MT("xnT0"), MT("xnT1")]
        xsrc = self.xs[s].rearrange("(a b) d -> b a d", b=128)
        for q in range(16):
            xt = xts[q % 2]
            tx = t_x[q % 2]

            def xload(q2):
                for i in range(4):
                    for e_ in range(2):
                        b = 2 * (4 * q2 + i) + e_
                        self.dma(xts[q2 % 2][64 * e_:64 * e_ + 64, i, :], xsrc[b], [], [t_x[q2 % 2]],
                                 "p0_x%d" % (q2 % 2), last=(i == 3 and e_ == 1))
            if q == 0:
                xload(0)
            if q + 1 < 16:
                xload(q + 1)
            lvl = self.dbg.get("p0_lvl", 9)
            for i in range(4):
                c0 = 8 * q + 2 * i
                t_st = Tok()
                self.op("act", lambda e, xt=xt, i=i, c0=c0: e.activation(
                    out=sqj[:, :], in_=xt[:, i, :], func=AF.Square, accum_out=st[:, c0:c0 + 1]),
                    reads=[tx], writes=[t_sq, t_st])
                if lvl < 2:
                    continue
                self.op("act", lambda e, c0=c0: e.activation(
                    out=st[:, c0 + 1:c0 + 2], in_=st[:, c0:c0 + 1], func=AF.Sqrt, scale=1.0 / D, bias=self.epsb[:]),
                    reads=[t_st, self.ctok], writes=[t_st])
                self.op("dve", lambda e, c0=c0: e.reciprocal(out=st[:, c0 + 1:c0 + 2], in_=st[:, c0 + 1:c0 + 2]),
                        reads=[t_st], writes=[t_st])
                if lvl < 3:
                    continue
                self.op("dve", lambda e, xt=xt, i=i, c0=c0: e.scalar_tensor_tensor(
                    out=xn[:, i, :], in0=xt[:, i, :], scalar=st[:, c0 + 1:c0 + 2], in1=self.gpre[:],
                    op0=ALU.mult, op1=ALU.mult), reads=[tx, t_st, self.ctok], writes=[t_xn])
            xnT = xnTs[q % 2]
            txT = t_xnT[q % 2]
            if lvl < 4:
                continue
            for kq in range(2):
                pss = [self.bank() for _ in range(4)]
                for k4 in range(4):
                    kc = 4 * kq + k4
                    ps, pt = pss[k4]
                    for i in range(4):
                        self.op("pe", lambda e, ps=ps, i=i, kc=kc: e.matmul(
                            ps[:, 128 * i:128 * i + 128], lhsT=xn[:, i, 128 * kc:128 * kc + 128], rhs=self.ident[:],
                            start=True, stop=True), reads=[t_xn, self.ctok], writes=pt)
                    self.copy(self.ev(), xnT[:, kc, :], ps, pt, [txT])
            self.dma(self.s_xnT[q], xnT.rearrange("p k t -> p (k t)"), [txT], [Tok()], "p0_st%d" % (q % 2))

    def load_wblk(self, cb, slots, wblk, t_w):
        for i, sl in enumerate(slots):
            self.dma(wblk[:, :, i, :], self.s_wbf[sl, cb].rearrange("p (kc c) -> p kc c", kc=8), [], [t_w], "wblk",
                     last=(i == len(slots) - 1))

    def phaseF(self, s, cb):
        KB = 1024
        wblk = self.av(0, [128, 8, 3, 128], BF16)
        xnTs = [self.av((14 + 8 * i) * KB, [128, 8, 512], BF16) for i in range(2)]
        UAB = self.av(30 * KB, [128, 2, 128, 64], BF16)
        sfg = self.av(64 * KB, [128, L], BF16)
        ZT = self.av(80 * KB, [128, 128, 128], BF16)
        FMs = [self.av((112 + 4 * i) * KB, [128, 8, 2, 128], BF16) for i in range(2)]
        t_w, t_U, t_sfg, t_ZT, t_uf = Tok(), MT("U"), MT("sfg"), MT("ZT"), MT("uf")
        t_xT = [Tok(), Tok()]
        t_FM = [Tok(), Tok()]
        self.load_wblk(cb, [0, 1, 2], wblk, t_w)
        for q in range(16):
            xnT, tx = xnTs[q % 2], t_xT[q % 2]
            self.dma(xnT.rearrange("p k t -> p (k t)"), self.s_xnT[q], [], [tx], "xnT%d" % (q % 2))
            for ip in range(2):
                ps, pt = self.bank()
                for i2 in range(2):
                    i = 2 * ip + i2
                    for kc in range(8):
                        self.op("pe", lambda e, ps=ps, i2=i2, i=i, kc=kc, xnT=xnT: e.matmul(
                            ps[:, 256 * i2:256 * i2 + 256], lhsT=xnT[:, kc, 128 * i:128 * i + 128],
                            rhs=wblk[:, kc, 0:2, :], start=(kc == 0), stop=(kc == 7)),
                            reads=[tx, t_w], writes=pt)
                bq = 4 * q + 2 * ip
                self.copy(self.ev(), UAB[:, :, :, bq:bq + 2],
                          ps.rearrange("p (t s c) -> p s c t", t=2, s=2), pt, [t_U])
            ps, pt = self.bank()
            for kc in range(8):
                self.op("pe", lambda e, ps=ps, kc=kc, xnT=xnT: e.matmul(
                    ps[:, :], lhsT=wblk[:, kc, 2, :], rhs=xnT[:, kc, :], start=(kc == 0), stop=(kc == 7)),
                    reads=[tx, t_w], writes=pt)
            self.op("act", lambda e, ps=ps, q=q: e.activation(out=sfg[:, 512 * q:512 * q + 512], in_=ps, func=AF.Silu),
                    reads=pt, writes=[t_sfg], after=[self.tok_uF])
        for c4 in range(32):
            ps, pt = self.bank()
            for ci in range(4):
                c = 4 * c4 + ci
                for e_ in range(2):
                    for s2 in range(2):
                        self.op("pe", lambda e, ps=ps, ci=ci, c=c, e_=e_, s2=s2: e.matmul(
                            ps[64 * e_:64 * e_ + 64, 128 * ci:128 * ci + 128],
                            lhsT=UAB[64 * e_:64 * e_ + 64, s2, c, :], rhs=self.FF[64 * e_:64 * e_ + 64, s2, :],
                            start=(s2 == 0), stop=(s2 == 1)), reads=[t_U, self.ctok], writes=pt)
            self.copy(self.ev(), ZT[:, :, 4 * c4:4 * c4 + 4], ps.rearrange("p (c k) -> p k c", c=4), pt, [t_ZT],
                      after=[self.tok_uH])
        if self.dbg.get("dump_F") and cb == self.dbg["dump_F"][1] and s == self.dbg["dump_F"][0]:
            self.dump("F_UAB", UAB.rearrange("p s c b -> p (s c b)"), [128, 2 * 128 * 64], BF16, [t_U])
            self.dump("F_ZT", ZT.rearrange("p k c -> p (k c)"), [128, 128 * 128], BF16, [t_ZT])
            self.dump("F_sfg", sfg, [128, L], BF16, [t_sfg])
        self.pre = None
        if self.dbg.get("phaseH", True) and self.dbg.get("preload", True):
            wblkH = self.av(0, [128, 8, 4, 128], BF16)
            xnT0H = self.av(16 * KB, [128, 8, 512], BF16)
            t_wH, t_x0H = Tok(), Tok()
            aft = [t_w, t_xT[0], t_xT[1]]
            for i_, sl_ in enumerate([3, 4, 5, 6]):
                self.op("sp", lambda e, i_=i_, sl_=sl_: e.dma_start(
                    out=wblkH[:, :, i_, :], in_=self.s_wbf[sl_, cb].rearrange("p (kc c) -> p kc c", kc=8)),
                    writes=[t_wH], dma="wblk", last=(i_ == 3), after=aft)
            self.op("sp", lambda e: e.dma_start(out=xnT0H.rearrange("p k t -> p (k t)"), in_=self.s_xnT[0]),
                    writes=[t_x0H], dma="xnT0", after=aft)
            self.pre = (s, cb, t_wH, t_x0H)
        sfgv = sfg.rearrange("p (h k a) -> p k h a", h=2, k=64)
        for k1q in range(8):
            FMb, tF = FMs[k1q % 2], t_FM[k1q % 2]
            self.dma(FMb, self.T["t_FM"][:, 8 * k1q:8 * k1q + 8], [], [tF], "FMs%d" % (k1q % 2))
            for kp in range(2):
                ps, pt = self.bank()
                for ki in range(4):
                    k8 = 4 * kp + ki
                    k1 = 8 * k1q + k8
                    for ri in range(2):
                        self.op("pe", lambda e, ps=ps, ki=ki, k8=k8, k1=k1, ri=ri, FMb=FMb: e.matmul(
                            ps[:, 128 * ki:128 * ki + 128], lhsT=ZT[:, 64 * ri + k1, :], rhs=FMb[:, k8, ri, :],
                            start=(ri == 0), stop=(ri == 1)), reads=[t_ZT, tF], writes=pt)
                k1a = 8 * k1q + 4 * kp
                self.op("dve", lambda e, ps=ps, k1a=k1a: e.tensor_tensor(
                    out=sfgv[:, k1a:k1a + 4], in0=ps.rearrange("p (k h a) -> p k h a", k=4, h=2),
                    in1=sfgv[:, k1a:k1a + 4], op=ALU.mult), reads=pt + [t_sfg], writes=[t_uf])
        self.dma(self.s_u[0, cb], sfg, [t_uf], [self.tok_uF], "u_stF")

    def phaseH(self, s, cb):
        KB = 1024
        wblk = self.av(0, [128, 8, 4, 128], BF16)
        xnTs = [self.av((16 + 8 * i) * KB, [128, 8, 512], BF16) for i in range(2)]
        Uv = self.av(32 * KB, [128, 128, 64], BF16)
        Ux1 = self.av(48 * KB, [128, 128, 64], BF16)
        x2c = self.av(64 * KB, [128, L], BF16)
        shg = self.av(80 * KB, [128, L], BF16)
        raw = [self.av((96 + 16 * i) * KB, [128, L], BF16) for i in range(3)]
        ctmp = self.av(144 * KB, [128, L], BF16)
        ZT = self.av(96 * KB, [128, 128, 128], BF16)
        O0 = ZT.rearrange("p k c -> p (k c)").rearrange("p (c n) -> p c n", c=128)
        Yb = self.av(128 * KB, [128, 128, 2, 64], BF16)
        O1 = Yb.rearrange("p c r k -> p c (r k)")
        Ms = [self.av(o_ * KB, [128, 8, 3, 128], BF16) for o_ in (0, 6, 64, 70)]
        As = [self.av((12 + 4 * i) * KB, [128, 16, 2, 64], BF16) for i in range(2)]
        KAs = [self.av(o_ * KB, [128, 16, 2, 64], BF16) for o_ in (20, 24, 76, 176)]
        P1s = [self.av((160 + 4 * i) * KB, [128, 16, 2, 64], BF16) for i in range(2)]
        P2s = [self.av((168 + 4 * i) * KB, [128, 16, 2, 64], BF16) for i in range(2)]
        t_w = Tok()
        t_xT = [Tok(), Tok()]
        t_rawc = [[Tok() for _ in range(16)] for _ in range(3)]
        t_shgc = [Tok() for _ in range(16)]
        t_dc = [[Tok() for _ in range(16)] for _ in range(3)]
        t_g2c = [Tok() for _ in range(16)]
        t_Uv, t_Ux1 = MT("Uv"), MT("Ux1")
        t_uh, t_Uz = MT("uh"), MT("Uz")
        x1c = self.av(160 * KB, [128, L], BF16)
        dsts = [ctmp, x1c, x2c]
        Us = [(Uv, t_Uv), (Ux1, t_Ux1)]
        hl = self.dbg.get("h_lvl", 99)
        rnc = self.sb("h_rnc_%d_%d" % (s, cb), [128, 2], F32)
        wsx = self.sb("h_wsx_%d_%d" % (s, cb), [128, 6], F32)
        t_rnc, t_wsx = Tok(), Tok()
        for o_ in range(2):
            self.dma(rnc[:, o_:o_ + 1], self.s_rn[o_, cb], [], [t_rnc], "h_rn", last=(o_ == 1))
        for o_ in range(2):
            c0_ = ((1 + o_) * 8 + cb) * 3
            self.op("dve", lambda e, o_=o_, c0_=c0_: e.tensor_scalar_mul(
                out=wsx[:, 3 * o_:3 * o_ + 3], in0=self.ws[:, c0_:c0_ + 3], scalar1=rnc[:, o_:o_ + 1]),
                reads=[t_rnc, self.ctok], writes=[t_wsx])

        def wsc(ti, tap):
            if ti == 0:
                return self.ws[:, (ti * 8 + cb) * 3 + tap:(ti * 8 + cb) * 3 + tap + 1]
            return wsx[:, 3 * (ti - 1) + tap:3 * (ti - 1) + tap + 1]

        def conv_chunk(r):
            lo, hi = 512 * r, 512 * r + 512
            for ti in range(3):
                rw, dst = raw[ti], dsts[ti]
                rd = [t_rawc[ti][q] for q in (r - 1, r, r + 1) if 0 <= q < 16] + [t_wsx]
                self.op("act", lambda e, rw=rw, dst=dst, ti=ti: e.activation(
                    out=dst[:, lo:hi], in_=rw[:, lo:hi], func=AF.Copy, scale=wsc(ti, 1)),
                    reads=rd + [self.ctok], writes=[t_dc[ti][r]], after=([self.tok_uF] if ti == 2 else []))
                a0 = max(lo, 64)
                self.op("dve", lambda e, rw=rw, dst=dst, ti=ti, a0=a0: e.scalar_tensor_tensor(
                    out=dst[:, a0:hi], in0=rw[:, a0 - 64:hi - 64], scalar=wsc(ti, 0), in1=dst[:, a0:hi],
                    op0=ALU.mult, op1=ALU.add), reads=rd + [t_dc[ti][r], self.ctok], writes=[t_dc[ti][r]])
                b1 = min(hi, L - 64)
                self.op("dve", lambda e, rw=rw, dst=dst, ti=ti, b1=b1: e.scalar_tensor_tensor(
                    out=dst[:, lo:b1], in0=rw[:, lo + 64:b1 + 64], scalar=wsc(ti, 2), in1=dst[:, lo:b1],
                    op0=ALU.mult, op1=ALU.add), reads=rd + [t_dc[ti][r], self.ctok], writes=[t_dc[ti][r]])

        def post_chunk(r):
            self.op("dve", lambda e: e.tensor_tensor(out=shg[:, 512 * r:512 * r + 512], in0=shg[:, 512 * r:512 * r + 512],
                                                     in1=x2c[:, 512 * r:512 * r + 512], op=ALU.mult),
                    reads=[t_dc[2][r], t_shgc[r]], writes=[t_g2c[r]])
            if hl < 3:
                return
            for ti, (U, tU) in enumerate(Us):
                ps, pt = self.bank()
                for bi in range(4):
                    bq = 4 * r + bi
                    self.op("pe", lambda e, ps=ps, bi=bi, bq=bq, src=dsts[ti]: e.matmul(
                        ps[:, 128 * bi:128 * bi + 128], lhsT=src[:, 128 * bq:128 * bq + 128], rhs=self.ident[:],
                        start=True, stop=True), reads=[t_dc[ti][r], self.ctok], writes=pt)
                self.copy(self.ev(), U[:, :, 4 * r:4 * r + 4], ps.rearrange("p (b c) -> p c b", b=4), pt, [tU])

        pre = getattr(self, "pre", None)
        pre_ok = bool(pre) and pre[0] == s and pre[1] == cb
        if pre_ok:
            t_w = pre[2]
            t_xT[0] = pre[3]
        else:
            self.load_wblk(cb, [3, 4, 5, 6], wblk, t_w)
        self.pre = None
        for q in range(16):
            xnT, tx = xnTs[q % 2], t_xT[q % 2]
            if not (pre_ok and q == 0):
                self.dma(xnT.rearrange("p k t -> p (k t)"), self.s_xnT[q], [], [tx], "xnT%d" % (q % 2))
            for sl in range(4):
                ps, pt = self.bank()
                for kc in range(8):
                    self.op("pe", lambda e, ps=ps, kc=kc, xnT=xnT, sl=sl: e.matmul(
                        ps[:, :], lhsT=wblk[:, kc, sl, :], rhs=xnT[:, kc, :], start=(kc == 0), stop=(kc == 7)),
                        reads=[tx, t_w], writes=pt)
                if sl < 3:
                    self.copy(self.ev(), raw[sl][:, 512 * q:512 * q + 512], ps, pt, [t_rawc[sl][q]])
                else:
                    self.op("act", lambda e, ps=ps, q=q: e.activation(out=shg[:, 512 * q:512 * q + 512], in_=ps,
                                                                       func=AF.Silu), reads=pt, writes=[t_shgc[q]],
                            after=[self.tok_uH])
            if hl >= 2 and q >= 1:
                conv_chunk(q - 1)
                if 1 <= q - 2 <= 14:
                    post_chunk(q - 2)
        if hl < 2:
            return
        conv_chunk(15)
        post_chunk(14)
        for ti in range(3):
            rw, dst = raw[ti], dsts[ti]
            self.op("dve", lambda e, rw=rw, dst=dst, ti=ti: e.scalar_tensor_tensor(
                out=dst[:, 1:64], in0=rw[:, L - 64:L - 1], scalar=wsc(ti, 0), in1=dst[:, 1:64],
                op0=ALU.mult, op1=ALU.add), reads=[t_rawc[ti][15], t_dc[ti][0], self.ctok, t_wsx], writes=[t_dc[ti][0]])
            self.op("dve", lambda e, rw=rw, dst=dst, ti=ti: e.scalar_tensor_tensor(
                out=dst[:, L - 64:L - 1], in0=rw[:, 1:64], scalar=wsc(ti, 2), in1=dst[:, L - 64:L - 1],
                op0=ALU.mult, op1=ALU.add), reads=[t_rawc[ti][0], t_dc[ti][15], self.ctok, t_wsx], writes=[t_dc[ti][15]])
        post_chunk(0)
        post_chunk(15)
        t_g2 = None
        dmp = self.dbg.get("dump_H") and cb == self.dbg["dump_H"][1] and s == self.dbg["dump_H"][0]
        if dmp:
            self.dump("H_Uv", Uv.rearrange("p c b -> p (c b)"), [128, L], BF16, [t_Uv])
            self.dump("H_Ux1", Ux1.rearrange("p c b -> p (c b)"), [128, L], BF16, [t_Ux1])
            self.dump("H_gate2", shg, [128, L], BF16, t_g2c)
        if hl < 4:
            return
        self.P.barrier()
        t_M = [Tok(), Tok(), Tok(), Tok()]
        t_A = [Tok(), Tok()]
        t_KA = [Tok(), Tok(), Tok(), Tok()]
        t_P = [MT("P0"), MT("P1")]
        mi = ai = 0
        t_ZT = MT("ZT")
        t_sub = [MT("Yb%d" % i) for i in range(8)]
        t_O1 = [MT("O1_%d" % i) for i in range(8)]
        t_O0 = [MT("O0_%d" % i) for i in range(8)]
        for conv in range(2):
            Uin, t_Uin = Uv, (t_Uv if conv == 0 else t_Uz)
            for c4 in range(32):
                ps, pt = self.bank()
                for ci in range(4):
                    c = 4 * c4 + ci
                    for e_ in range(2):
                        self.op("pe", lambda e, ps=ps, ci=ci, c=c, e_=e_: e.matmul(
                            ps[64 * e_:64 * e_ + 64, 128 * ci:128 * ci + 128], lhsT=Uin[64 * e_:64 * e_ + 64, c, :],
                            rhs=self.F1d[64 * e_:64 * e_ + 64, :], start=True, stop=True),
                            reads=[t_Uin, self.ctok], writes=pt)
                self.copy(self.ev(), ZT[:, :, 4 * c4:4 * c4 + 4], ps.rearrange("p (c k) -> p k c", c=4), pt,
                          [t_ZT], after=t_O0)
            if hl < 5:
                return
            for k1q in range(8):
                Mb, tM = Ms[mi % 4], t_M[mi % 4]
                self.dma(Mb, self.T["t_M"][:, 8 * k1q:8 * k1q + 8], [], [tM], "HMs%d" % (mi % 4))
                mi += 1
                for kp in range(4):
                    ps, pt = self.bank()
                    for ki in range(2):
                        k8 = 2 * kp + ki
                        k1 = 8 * k1q + k8
                        base = 256 * ki
                        mm = [(0, k1, 0, True, False), (2, 64 + k1, 0, False, True),
                              (1, k1, 128, True, False), (0, 64 + k1, 128, False, True)]
                        for (mt, zrow, off, sta, sto) in mm:
                            self.op("pe", lambda e, ps=ps, Mb=Mb, k8=k8, mt=mt, zrow=zrow, off=off, sta=sta, sto=sto,
                                    base=base: e.matmul(
                                ps[:, base + off:base + off + 128], lhsT=Mb[:, k8, mt, :], rhs=ZT[:, zrow, :],
                                start=sta, stop=sto), reads=[tM, t_ZT], writes=pt)
                    k1a = 8 * k1q + 2 * kp
                    self.copy(self.ev(), Yb[:, :, :, k1a:k1a + 2], ps.rearrange("p (k r c) -> p c r k", k=2, r=2), pt,
                              t_sub, after=t_O1)
            if dmp and conv == 0:
                self.dump("H_Yb", Yb.rearrange("p c r k -> p (c r k)"), [128, 16384], BF16, t_sub)
            if hl < 6:
                return
            def products(sb_):
                slot = sb_ % 2
                KAb, tK = KAs[sb_ % 4], t_KA[sb_ % 4]
                P1, P2, tP = P1s[slot], P2s[slot], t_P[slot]
                self.dma(KAb.rearrange("p c h k -> p (c h k)"),
                         self.s_ka[conv, cb][:, 2048 * sb_:2048 * sb_ + 2048], [], [tK], "KAs%d" % (sb_ % 4))
                Ysub = Yb[:, 16 * sb_:16 * sb_ + 16]
                self.op("dve" if sb_ % 2 == 0 else "pool", lambda e, Ysub=Ysub, KAb=KAb, P1=P1: e.tensor_tensor(
                    out=P1[:, :, :, :], in0=Ysub[:, :, 0:1, :].broadcast_to([128, 16, 2, 64]), in1=KAb[:, :, :, :],
                    op=ALU.mult), reads=[t_sub[sb_], tK], writes=[tP])
                self.op("dve", lambda e, Ysub=Ysub, KAb=KAb, P2=P2: e.scalar_tensor_tensor(
                    out=P2[:, :, 0, :], in0=Ysub[:, :, 1, :], scalar=-1.0, in1=KAb[:, :, 1, :],
                    op0=ALU.mult, op1=ALU.mult), reads=[t_sub[sb_], tK], writes=[tP])
                self.op("dve", lambda e, Ysub=Ysub, KAb=KAb, P2=P2: e.tensor_tensor(
                    out=P2[:, :, 1, :], in0=Ysub[:, :, 1, :], in1=KAb[:, :, 0, :], op=ALU.mult),
                    reads=[t_sub[sb_], tK], writes=[tP])

            def stage1p(sb_):
                slot = sb_ % 2
                P1, P2, tP = P1s[slot], P2s[slot], t_P[slot]
                for c4 in range(8):
                    ps, pt = self.bank()
                    for ci in range(2):
                        cl = 2 * c4 + ci
                        self.op("pe", lambda e, ps=ps, ci=ci, cl=cl, P1=P1: e.matmul(
                            ps[:, 256 * ci:256 * ci + 256], lhsT=P1[:, cl, :, :], rhs=self.G[:], start=True, stop=False),
                            reads=[tP, self.ctok], writes=pt)
                        self.op("pe", lambda e, ps=ps, ci=ci, cl=cl, P2=P2: e.matmul(
                            ps[:, 256 * ci:256 * ci + 256], lhsT=P2[:, cl, :, :], rhs=self.G[:], start=False, stop=True),
                            reads=[tP, self.ctok], writes=pt)
                    c0 = 16 * sb_ + 2 * c4
                    psv = ps.rearrange("p (c h n) -> p c h n", c=2, h=2)
                    eng_o = self.ev()
                    self.copy(eng_o, O0[:, c0:c0 + 2, :], psv[:, :, 0, :], pt, [t_O0[sb_]], after=[t_ZT])
                    self.copy(eng_o, O1[:, c0:c0 + 2, :], psv[:, :, 1, :], pt, [t_O1[sb_]], after=[t_sub[sb_]])

            products(0)
            for sb_ in range(8):
                if sb_ + 1 < 8:
                    products(sb_ + 1)
                stage1p(sb_)
            if hl < 7:
                return
            Os = [O0, O1]
            tOs = [t_O0, t_O1]
            if dmp and conv == 0:
                self.dump("H_O0", O0.rearrange("p c n -> p (c n)"), [128, 16384], BF16, t_O0)
                self.dump("H_O1", O1.rearrange("p c n -> p (c n)"), [128, 16384], BF16, t_O1)
            if conv == 0:
                t_U2 = Tok()
                for bq16 in range(8):
                    Ab, tA = As[ai % 2], t_A[ai % 2]
                    self.dma(Ab, self.T["t_A"][:, 16 * bq16:16 * bq16 + 16], [], [tA], "As%d" % (ai % 2))
                    ai += 1
                    for b4 in range(2):
                        ps, pt = self.bank()
                        for bi in range(4):
                            bp_ = 8 * bq16 + 4 * b4 + bi
                            for e_ in range(2):
                                b = 2 * bp_ + e_
                                hb, b64 = b // 64, b % 64
                                for g in range(2):
                                    self.op("pe", lambda e, ps=ps, bi=bi, e_=e_, g=g, hb=hb, b64=b64, Ab=Ab,
                                            bl=b - 16 * bq16: e.matmul(
                                        ps[64 * e_:64 * e_ + 64, 128 * bi:128 * bi + 128], lhsT=Ab[:, bl, g, :],
                                        rhs=Os[hb][:, :, 64 * g + b64], start=(g == 0), stop=(g == 1)),
                                        reads=[tA] + tOs[hb], writes=pt)
                        bq = 8 * bq16 + 4 * b4
                        self.op("dve", lambda e, ps=ps, bq=bq: e.tensor_tensor(
                            out=Uv[:, :, bq:bq + 4], in0=ps.rearrange("p (b c) -> p c b", b=4),
                            in1=Ux1[:, :, bq:bq + 4], op=ALU.mult), reads=pt + [t_Ux1], writes=[t_Uz], after=[t_Uv])
                if dmp:
                    self.dump("H_U2", Uv.rearrange("p c b -> p (c b)"), [128, L], BF16, [t_Uz])
            else:
                if hl < 9:
                    return
                for bq16 in range(8):
                    Ab, tA = As[ai % 2], t_A[ai % 2]
                    self.dma(Ab, self.T["t_A"][:, 16 * bq16:16 * bq16 + 16], [], [tA], "As%d" % (ai % 2))
                    ai += 1
                    for b8 in range(2):
                        ps, pt = self.bank()
                        for bi in range(8):
                            bl = 8 * b8 + bi
                            b = 16 * bq16 + bl
                            hb, b64 = b // 64, b % 64
                            for g in range(2):
                                self.op("pe", lambda e, ps=ps, bi=bi, g=g, hb=hb, b64=b64, Ab=Ab, bl=bl: e.matmul(
                                    ps[:, 64 * bi:64 * bi + 64], lhsT=Os[hb][:, :, 64 * g + b64], rhs=Ab[:, bl, g, :],
                                    start=(g == 0), stop=(g == 1)), reads=[tA] + tOs[hb], writes=pt)
                        i0 = 64 * (16 * bq16 + 8 * b8)
                        self.op("dve", lambda e, ps=ps, i0=i0: e.tensor_tensor(
                            out=shg[:, i0:i0 + 512], in0=ps, in1=shg[:, i0:i0 + 512], op=ALU.mult),
                            reads=pt + [t_g2c[i0 // 512]], writes=[t_uh])
        self.dma(self.s_u[1, cb], shg, [t_uh], [self.tok_uH], "u_stH")

    def tail(self, s):
        KB = 1024
        wt = self.av(0, [128, 8, 5120], BF16)
        xnTs = [self.av((80 + 8 * i) * KB, [128, 8, 512], BF16) for i in range(2)]
        ufs = [self.av((96 + 8 * i) * KB, [128, 8, 512], BF16) for i in range(2)]
        uhs = [self.av((112 + 8 * i) * KB, [128, 8, 512], BF16) for i in range(2)]
        xts = [self.av((128 + 16 * i) * KB, [128, 4, D], F32) for i in range(2)]
        mrg = self.av(160 * KB, [128, 8, 512], BF16)
        gf = self.av(168 * KB, [128, 512], BF16)
        gh = self.av(169 * KB, [128, 512], BF16)
        t1 = self.av(170 * KB, [128, 512], F32)
        t2 = self.av(172 * KB, [128, 512], F32)
        sqj = self.av(174 * KB, [128, 512], BF16)
        st = self.sb("tl_st%d" % s, [128, 16 * 4 * 4], F32)
        t_wt = Tok()
        self.dma(wt.rearrange("p k n -> p (k n)"), self.s_wt.rearrange("p k n -> p (k n)"), [], [t_wt], "wt")
        t_xT, t_uf, t_uh, t_x = [Tok(), Tok()], [Tok(), Tok()], [Tok(), Tok()], [Tok(), Tok()]
        t_mrg, t_gf, t_gh, t_t1, t_t2, t_sq = MT("mrg"), Tok(), Tok(), Tok(), Tok(), Tok()
        tmpo = self.av(176 * KB, [128, 512], F32)
        t_tmpo = Tok()
        xsrc = self.xs[s].rearrange("(a b) d -> b a d", b=128)
        ydst = self.y[s].rearrange("(a b) d -> b a d", b=128)
        for q in range(16):
            sl = q % 2
            xnT, uf, uh, xt = xnTs[sl], ufs[sl], uhs[sl], xts[sl]

            def loads(q2):
                s2 = q2 % 2
                self.dma(xnTs[s2].rearrange("p k t -> p (k t)"), self.s_xnT[q2], [], [t_xT[s2]], "xnT%d" % s2)
                self.dma(ufs[s2], self.s_u[0].rearrange("c p t -> p c t")[:, :, 512 * q2:512 * q2 + 512], [],
                         [t_uf[s2]], "tuf%d" % s2)
                self.dma(uhs[s2], self.s_u[1].rearrange("c p t -> p c t")[:, :, 512 * q2:512 * q2 + 512], [],
                         [t_uh[s2]], "tuh%d" % s2)
                for i in range(4):
                    for e_ in range(2):
                        b = 2 * (4 * q2 + i) + e_
                        self.dma(xts[s2][64 * e_:64 * e_ + 64, i, :], xsrc[b], [], [t_x[s2]], "tx%d" % s2,
                                 last=(i == 3 and e_ == 1))
            if q == 0:
                loads(0)
            if q + 1 < 16:
                loads(q + 1)
            for db in range(8):
                banks = [self.bank() for _ in range(4)]
                srcs = [(uf, t_uf[sl], 2048), (uh, t_uh[sl], 3072), (xnT, t_xT[sl], 0), (xnT, t_xT[sl], 1024)]
                for bi, (src, tsrc, wcol) in enumerate(srcs):
                    ps, pt = banks[bi]
                    for kc in range(8):
                        self.op("pe", lambda e, ps=ps, kc=kc, src=src, wcol=wcol, db=db: e.matmul(
                            ps[:, :], lhsT=wt[:, kc, wcol + 128 * db:wcol + 128 * db + 128], rhs=src[:, kc, :],
                            start=(kc == 0), stop=(kc == 7)), reads=[t_wt, tsrc], writes=pt)
                self.op("act", lambda e, ps=banks[2][0], db=db: e.activation(
                    out=gf[:, :], in_=ps, func=AF.Sigmoid, bias=self.bm[:, db:db + 1], scale=1.0),
                    reads=banks[2][1] + [self.ctok], writes=[t_gf])
                self.op("act", lambda e, ps=banks[3][0], db=db: e.activation(
                    out=gh[:, :], in_=ps, func=AF.Sigmoid, bias=self.bm[:, 8 + db:9 + db], scale=1.0),
                    reads=banks[3][1] + [self.ctok], writes=[t_gh])
                self.op("dve", lambda e, ps=banks[0][0]: e.tensor_tensor(out=t1[:, :], in0=ps, in1=gf[:, :], op=ALU.mult),
                        reads=banks[0][1] + [t_gf], writes=[t_t1])
                self.op("dve", lambda e, ps=banks[1][0]: e.tensor_tensor(out=t2[:, :], in0=ps, in1=gh[:, :], op=ALU.mult),
                        reads=banks[1][1] + [t_gh], writes=[t_t2])
                self.op("pool", lambda e, db=db: e.tensor_tensor(out=mrg[:, db, :], in0=t1[:, :], in1=t2[:, :], op=ALU.add),
                        reads=[t_t1, t_t2], writes=[t_mrg])
            for i in range(4):
                c0 = 16 * q + 4 * i
                pss = [self.bank() for _ in range(2)]
                t_st = Tok()
                for hf in range(2):
                    ps, pt = pss[hf]
                    for db in range(8):
                        self.op("pe", lambda e, ps=ps, db=db, i=i, hf=hf: e.matmul(
                            ps[:, :], lhsT=mrg[:, db, 128 * i:128 * i + 128],
                            rhs=wt[:, db, 4096 + 512 * hf:4096 + 512 * hf + 512], start=(db == 0), stop=(db == 7)),
                            reads=[t_mrg, t_wt], writes=pt)
                    self.op("act", lambda e, ps=ps, c0=c0, hf=hf: e.activation(
                        out=sqj[:, :], in_=ps, func=AF.Square, accum_out=st[:, c0 + hf:c0 + hf + 1]),
                        reads=pt, writes=[t_sq, t_st])
                self.op("dve", lambda e, c0=c0: e.tensor_tensor(
                    out=st[:, c0 + 2:c0 + 3], in0=st[:, c0:c0 + 1], in1=st[:, c0 + 1:c0 + 2], op=ALU.add),
                    reads=[t_st], writes=[t_st])
                self.op("act", lambda e, c0=c0: e.activation(
                    out=st[:, c0 + 3:c0 + 4], in_=st[:, c0 + 2:c0 + 3], func=AF.Sqrt, scale=1.0 / D, bias=self.epsb[:]),
                    reads=[t_st, self.ctok], writes=[t_st])
                self.op("dve", lambda e, c0=c0: e.reciprocal(out=st[:, c0 + 3:c0 + 4], in_=st[:, c0 + 3:c0 + 4]),
                        reads=[t_st], writes=[t_st])
                for hf in range(2):
                    ps, pt = pss[hf]
                    self.op("dve", lambda e, ps=ps, c0=c0, hf=hf, tmpo=tmpo: e.scalar_tensor_tensor(
                        out=tmpo[:, :], in0=ps, scalar=st[:, c0 + 3:c0 + 4],
                        in1=self.gpost[:, 512 * hf:512 * hf + 512], op0=ALU.mult, op1=ALU.mult),
                        reads=pt + [t_st, self.ctok], writes=[t_tmpo])
                    self.op("pool", lambda e, hf=hf, i=i, xt=xt, tmpo=tmpo: e.tensor_tensor(
                        out=xt[:, i, 512 * hf:512 * hf + 512], in0=xt[:, i, 512 * hf:512 * hf + 512],
                        in1=tmpo[:, :], op=ALU.add), reads=[t_tmpo, t_x[sl]], writes=[t_x[sl]])
            for i in range(4):
                for e_ in range(2):
                    b = 2 * (4 * q + i) + e_
                    self.dma(ydst[b], xt[64 * e_:64 * e_ + 64, i, :], [t_x[sl]], [Tok()], "ty%d" % sl,
                             last=(i == 3 and e_ == 1))


_CACHE = {}


def _get_nc(dbg=None):
    key = repr(sorted((dbg or {}).items()))
    if key not in _CACHE:
        k = K(dbg)
        nc = k.build()
        _CACHE[key] = (nc, k)
    return _CACHE[key]


def _weights_layout(inp):
    f32 = lambda a: np.ascontiguousarray(np.asarray(a, np.float32))
    w = {}
    w["w_in"] = f32(inp["w_in"][0])
    w["g_pre"] = f32(inp["g_pre"][0]).reshape(1, D)
    w["g_post"] = f32(inp["g_post"][0]).reshape(1, D)
    ws = f32(inp["w_short"][0])
    w["ws_t"] = np.ascontiguousarray(ws.reshape(3, 3, 8, 128).transpose(3, 1, 2, 0).reshape(128, 72))
    w["fw1"] = f32(inp["filt_w1"][0])
    w["fb1"] = f32(inp["filt_b1"][0]).reshape(64, 1)
    w["ff1"] = f32(inp["filt_freq1"][0]).reshape(64, 1)
    w["fw2"] = f32(inp["filt_w2"][0])
    w["fb2"] = f32(inp["filt_b2"][0]).reshape(64, 1)
    w["ff2"] = f32(inp["filt_freq2"][0]).reshape(64, 1)
    w["fw3"] = f32(inp["filt_w3"][0])
    w["fdecay"] = f32(inp["filt_decay"][0]).reshape(4, D)
    w["hd"] = f32(inp["hyena_d"][0])
    w["wf"] = f32(inp["w_fourier_out"][0])
    w["wh"] = f32(inp["w_hyena_out"][0])
    w["wo"] = f32(inp["w_out"][0])
    w["bm_t"] = np.ascontiguousarray(f32(inp["b_merge"][0]).reshape(16, 128).T)
    return w


def kernel(**inputs):
    nc, k = _get_nc()
    xp = np.asarray(inputs["x_prompt"], np.float32)
    xsm = np.asarray(inputs["x_sample"], np.float32)
    w = _weights_layout(inputs)
    tabs = _tables()
    in_maps = []
    for c in range(NCORES):
        second = xp[c] if c < 2 else xsm[c]
        m = {"xs": np.ascontiguousarray(np.stack([xsm[c], second], 0))}
        m.update(w)
        m.update(tabs)
        in_maps.append(m)
    res = run_bass_kernel_spmd(nc, in_maps, core_ids=list(range(NCORES)))
    y_sample = np.stack([res.results[c]["y"][0] for c in range(NCORES)], 0)
    y_prompt = np.stack([res.results[c]["y"][1] for c in range(2)], 0)
    return (np.ascontiguousarray(y_prompt, np.float32), np.ascontiguousarray(y_sample, np.float32))
```

```python
import contextlib
import numpy as np
import ml_dtypes
import concourse.bass as bass
import concourse.mybir as mybir
from concourse.bass_utils import run_bass_kernel_spmd

F32 = mybir.dt.float32
BF16 = mybir.dt.bfloat16
AF = mybir.ActivationFunctionType
ALU = mybir.AluOpType
AX = mybir.AxisListType

L = 8192
N2 = 16384
D = 1024
PI2 = 2.0 * np.pi
NCORES = 8
NSEQ = 2
RMS_EPS = 1e-6


class Tok:
    __slots__ = ("name", "w", "r", "multi", "wl", "prev_r")

    def __init__(self, name="", multi=False):
        self.name = name
        self.w = None
        self.r = []
        self.multi = multi
        self.wl = []
        self.prev_r = []


def MT(name=""):
    return Tok(name, multi=True)


class Op:
    __slots__ = ("eng", "fn", "deps", "marked", "val", "sem", "is_dma", "last")

    def __init__(self, eng, fn, is_dma=False):
        self.eng = eng
        self.fn = fn
        self.deps = []
        self.marked = False
        self.val = None
        self.sem = None
        self.is_dma = is_dma
        self.last = True


class Prog:
    ENGS = ("pe", "act", "dve", "pool", "sp")

    def __init__(self, nc):
        self.nc = nc
        self.ops = {e: [] for e in self.ENGS}
        self.dma_streams = {}
        self.last_op = {e: None for e in self.ENGS}
        self.pending_bar = {e: None for e in self.ENGS}
        self.open_group = {}

    def barrier(self, skip=()):
        deps = [o for o in self.last_op.values() if o is not None]
        for name, lst in self.dma_streams.items():
            if lst and name not in skip:
                deps.append(lst[-1])
        for e in self.ENGS:
            old = self.pending_bar[e] or []
            self.pending_bar[e] = old + deps

    def op(self, eng, fn, reads=(), writes=(), dma=None, last=True, after=()):
        o = Op(eng, fn, is_dma=dma is not None)
        raw = set()
        deps = []
        for t in after:
            if t.multi:
                deps.extend(t.wl)
                deps.extend(t.prev_r)
            elif t.w is not None:
                deps.append(t.w)
            deps.extend(t.r)
        for t in reads:
            if t.multi:
                deps.extend(t.wl)
            elif t.w is not None:
                deps.append(t.w)
        for t in writes:
            if t.multi:
                assert not any(t is q for q in reads), "op reads+writes a multi token: " + t.name
                if t.r:
                    t.prev_r = t.r
                    t.r = []
                    t.wl = []
                deps.extend(t.prev_r)
            else:
                if t.w is not None:
                    deps.append(t.w)
                deps.extend(t.r)
        if self.pending_bar[eng] is not None:
            for d in self.pending_bar[eng]:
                deps.append(d)
                raw.add(id(d))
            self.pending_bar[eng] = None
        seen = set()
        opengrp = self.open_group.get(dma, ()) if dma is not None else ()
        for d in deps:
            if d is o or id(d) in seen:
                continue
            seen.add(id(d))
            if any(d is g for g in opengrp):
                continue
            if not (d.is_dma or o.is_dma) and d.eng == eng:
                if eng == "pe":
                    continue
            o.deps.append(d)
        for t in reads:
            t.r.append(o)
        for t in writes:
            if t.multi:
                t.wl.append(o)
            else:
                t.w = o
                t.r = []
        self.ops[eng].append(o)
        if dma is not None:
            self.dma_streams.setdefault(dma, []).append(o)
            o.sem = dma
            o.last = last
            if last:
                self.open_group[dma] = []
            else:
                self.open_group.setdefault(dma, []).append(o)
        else:
            self.last_op[eng] = o
        return o

    def emit(self, final_streams=()):
        nc = self.nc
        for e in self.ENGS:
            for o in self.ops[e]:
                for d in o.deps:
                    d.marked = True
        with contextlib.ExitStack() as st:
            esem = {e: st.enter_context(nc.semaphore("s_" + e)) for e in self.ENGS}
            dsem = {n: st.enter_context(nc.semaphore("d_" + n)) for n in self.dma_streams}
            for e in self.ENGS:
                c = 0
                for o in self.ops[e]:
                    if o.is_dma:
                        continue
                    if o.marked:
                        c += 1
                        o.val = c
                        o.sem = esem[e]
            for name, lst in self.dma_streams.items():
                c = 0
                pend = []
                for o in lst:
                    c += 16
                    pend.append(o)
                    o.sem = dsem[name]
                    o.marked = True
                    if o.last:
                        for q in pend:
                            q.val = c
                        pend = []
                assert not pend, name
            engobj = {"pe": "tensor", "act": "scalar", "dve": "vector", "pool": "gpsimd", "sp": "sync"}
            block = st.enter_context(nc.Block())
            self.n_waits = {e: 0 for e in self.ENGS}

            def make(e):
                def body(eng):
                    known = {}
                    for o in self.ops[e]:
                        need = {}
                        for d in o.deps:
                            k = id(d.sem)
                            if known.get(k, 0) >= d.val:
                                continue
                            if k not in need or need[k][1] < d.val:
                                need[k] = (d.sem, d.val)
                        for k, (sem, val) in need.items():
                            eng.wait_ge(sem, val)
                            known[k] = val
                            self.n_waits[e] += 1
                        ins = o.fn(eng)
                        if o.is_dma:
                            ins.then_inc(o.sem, 16)
                        elif o.marked:
                            ins.then_inc(o.sem, 1)
                    if e == "sp":
                        for name in final_streams:
                            lst = self.dma_streams.get(name)
                            if lst:
                                eng.wait_ge(dsem[name], lst[-1].val)
                return body

            for e in self.ENGS:
                if not self.ops[e] and e != "sp":
                    continue
                getattr(block, engobj[e])(make(e))


def _bperm():
    p = np.arange(128)
    return 2 * (p % 64) + p // 64


def _tables():
    bf = lambda x: np.ascontiguousarray(np.asarray(x, np.float32).astype(ml_dtypes.bfloat16))
    t = {}
    t["t_ident"] = bf(np.eye(128))
    def F1(na):
        a = np.arange(na)[:, None].astype(np.float64)
        k1 = np.arange(64)[None, :].astype(np.float64)
        th = PI2 * a * (k1 + 0.5) / 128.0
        return np.concatenate([np.cos(th), -np.sin(th)], axis=1)
    f1 = F1(64)
    t["t_F1d"] = bf(np.concatenate([f1, f1], 0))
    t["t_F1k"] = bf(F1(128))
    b = _bperm()[:, None].astype(np.float64)
    k2 = np.arange(128)[None, :].astype(np.float64)
    M = np.zeros((128, 64, 3, 128))
    for k1 in range(64):
        th = PI2 * (b * (k1 + 0.5) / N2 + b * k2 / 128.0)
        M[:, k1, 0] = np.cos(th)
        M[:, k1, 1] = -np.sin(th)
        M[:, k1, 2] = np.sin(th)
    t["t_M"] = bf(M)
    k2c = np.arange(128)[:, None].astype(np.float64)
    bb = np.arange(128)[None, :].astype(np.float64)
    th = PI2 * bb * k2c / 128.0
    Gr, Gi = np.cos(th), np.sin(th)
    G = np.zeros((128, 2, 2, 64))
    for hb in range(2):
        G[:, hb, 0] = Gr[:, 64 * hb:64 * hb + 64]
        G[:, hb, 1] = Gi[:, 64 * hb:64 * hb + 64]
    t["t_G"] = bf(G.reshape(128, 256))
    k1 = np.arange(64)[:, None].astype(np.float64)
    a = np.arange(64)[None, :].astype(np.float64)
    A = np.zeros((128, 128, 2, 64))
    for bi in range(128):
        ph = PI2 * (k1 + 0.5) * (128.0 * a + bi) / N2
        Br, Bi = np.cos(ph), np.sin(ph)
        A[:64, bi, 0] = Br
        A[64:, bi, 0] = -Bi
        A[:64, bi, 1] = -Bi
        A[64:, bi, 1] = -Br
    t["t_A"] = bf(A * (2.0 / N2))
    aa = np.arange(64)[:, None].astype(np.float64)
    kk = np.arange(64)[None, :].astype(np.float64)
    th = PI2 * aa * kk / 64.0
    Fr, Fi = np.cos(th), -np.sin(th)
    ff = np.stack([np.concatenate([Fr, Fi], 1), np.concatenate([Fi, -Fr], 1)], 1)
    t["t_FF"] = bf(np.concatenate([ff, ff], 0))
    jj = np.arange(128)
    k2o = (2 * (jj % 64) + jj // 64)[None, :].astype(np.float64)
    FM = np.zeros((128, 64, 2, 128))
    for k1i in range(64):
        th = PI2 * (b * k1i / L + b * k2o / 128.0)
        FM[:, k1i, 0] = np.cos(th)
        FM[:, k1i, 1] = np.sin(th)
    t["t_FM"] = bf(FM / np.sqrt(L))
    c = np.arange(256)[:, None].astype(np.float64)
    m = np.arange(256)[None, :].astype(np.float64)
    th = PI2 * c * m / 256.0
    CS = np.stack([np.cos(th), np.sin(th)], 1) / 16.0
    t["t_CS"] = bf(CS.reshape(2, 128, 2, 256).transpose(1, 0, 2, 3))
    bj = _bperm()[:, None]
    an = np.arange(128)[None, :]
    n = (128 * an + bj).reshape(-1)
    pos = np.where(n < L, n, 2 * L - 1 - n).astype(np.float32)
    pos = np.clip(pos, 0, L - 1)
    tt = pos / np.float32(L - 1)
    bands = np.linspace(1e-4, 15, 16, dtype=np.float32)
    ang = (np.float32(PI2) * pos / np.float32(L))[:, None] * bands[None]
    z = np.concatenate([tt[:, None], np.cos(ang), -np.sin(ang)], axis=-1)
    t["t_zf"] = np.ascontiguousarray(z.T.astype(np.float32))
    t["t_tneg"] = np.ascontiguousarray((-tt.reshape(128, 128).T).astype(np.float32))
    sg = np.ones((128, 1), np.float32)
    sg[64:] = -1.0
    t["t_sgn"] = sg
    return t


TABLE_SHAPES = {
    "t_ident": ([128, 128], BF16), "t_F1d": ([128, 128], BF16), "t_F1k": ([128, 128], BF16),
    "t_M": ([128, 64, 3, 128], BF16), "t_G": ([128, 256], BF16), "t_A": ([128, 128, 2, 64], BF16),
    "t_FF": ([128, 2, 128], BF16), "t_FM": ([128, 64, 2, 128], BF16), "t_CS": ([128, 2, 2, 256], BF16),
    "t_zf": ([33, 16384], F32), "t_tneg": ([128, 128], F32), "t_sgn": ([128, 1], F32),
}

WEIGHT_SHAPES = {
    "w_in": [D, 8192], "g_pre": [1, D], "g_post": [1, D], "ws_t": [128, 72],
    "fw1": [33, 64], "fb1": [64, 1], "ff1": [64, 1], "fw2": [64, 64], "fb2": [64, 1], "ff2": [64, 1],
    "fw3": [64, 4096], "fdecay": [4, D], "hd": [2, D], "wf": [D, D], "wh": [D, D], "wo": [D, D],
    "bm_t": [128, 16],
}

ARENA_BYTES = 180 * 1024


class K:
    def __init__(self, dbg=None):
        self.dbg = dbg or {}
        self.nc = bass.Bass("TRN2", target_bir_lowering=False)
        self.st = contextlib.ExitStack()
        self.P = Prog(self.nc)
        self.dbg_outs = []
        self.rr = 0
        self.bank_i = 0

    def dram_in(self, name, shape, dt=F32):
        return self.nc.dram_tensor(name, list(shape), dt, kind="ExternalInput").ap()

    def dram_out(self, name, shape, dt=F32):
        return self.nc.dram_tensor(name, list(shape), dt, kind="ExternalOutput").ap()

    def sb(self, name, shape, dt):
        return self.st.enter_context(self.nc.sbuf_tensor(name, list(shape), dt))

    def av(self, off_bytes, shape, dt=BF16):
        n = int(np.prod(shape[1:]))
        el = 2 if dt == BF16 else 4
        assert off_bytes % 4 == 0
        assert off_bytes + n * el <= ARENA_BYTES, (off_bytes, shape)
        a = self.arena[0:shape[0], off_bytes // 2: off_bytes // 2 + n * el // 2]
        if dt != BF16:
            a = a.bitcast(dt)
        if len(shape) == 3:
            a = a.rearrange("p (a b) -> p a b", a=shape[1])
        elif len(shape) == 4:
            a = a.rearrange("p (a b c) -> p a b c", a=shape[1], b=shape[2])
        elif len(shape) == 5:
            a = a.rearrange("p (a b c d) -> p a b c d", a=shape[1], b=shape[2], c=shape[3])
        return a

    def bank(self, n=1):
        i = self.bank_i
        if i % n:
            i += n - i % n
        if i + n > 8:
            i = 0
        self.bank_i = (i + n) % 8
        return self.psum[:, 512 * i: 512 * (i + n)], self.bank_tok[i:i + n]

    def ev(self):
        self.rr += 1
        return "act" if self.rr % 2 else "dve"

    def op(self, *a, **k):
        return self.P.op(*a, **k)

    def copy(self, eng, out, in_, reads, writes, after=()):
        if eng == "act":
            return self.op("act", lambda e: e.activation(out=out, in_=in_, func=AF.Copy), reads=reads, writes=writes,
                           after=after)
        if eng == "dve":
            return self.op("dve", lambda e: e.tensor_copy(out=out, in_=in_), reads=reads, writes=writes, after=after)
        return self.op("pool", lambda e: e.tensor_copy(out=out, in_=in_), reads=reads, writes=writes, after=after)

    def dma(self, out, in_, reads, writes, stream, last=True, eng="sp"):
        return self.op(eng, lambda e: e.dma_start(out=out, in_=in_), reads=reads, writes=writes, dma=stream, last=last)

    def dump(self, name, ap, shape, dt, reads):
        o = self.dram_out("dbg_" + name, shape, dt)
        t = Tok()
        self.dma(o, ap, reads, [t], "dbg_" + name)
        self.dbg_outs.append("dbg_" + name)

    def build(self):
        nc = self.nc
        d = self.dbg
        self.xs = self.dram_in("xs", [NSEQ, L, D])
        self.W = {k: self.dram_in(k, s) for k, s in WEIGHT_SHAPES.items()}
        self.T = {k: self.dram_in(k, s, dt) for k, (s, dt) in TABLE_SHAPES.items()}
        self.y = self.dram_out("y", [NSEQ, L, D])
        self.s_xnT = self.dram_out("s_xnT", [16, 128, 4096], BF16)
        self.s_u = self.dram_out("s_u", [2, 8, 128, L], BF16)
        self.s_ka = self.dram_out("s_ka", [2, 8, 128, 16384], BF16)
        self.s_wbf = self.dram_out("s_wbf", [7, 8, 128, 1024], BF16)
        self.s_wt = self.dram_out("s_wt", [128, 8, 5120], BF16)
        self.s_rn = self.dram_out("s_rn", [2, 8, 128, 1], F32)
        self.arena = self.sb("arena", [128, ARENA_BYTES // 2], BF16)
        self.psum = self.st.enter_context(nc.psum_tensor("psum", [128, 4096], F32))
        self.bank_tok = [Tok("bank%d" % i) for i in range(8)]
        self.tok_uF, self.tok_uH = Tok("uF"), Tok("uH")
        self.ident = self.sb("ident", [128, 128], BF16)
        self.F1d = self.sb("F1d", [128, 128], BF16)
        self.F1k = self.sb("F1k", [128, 128], BF16)
        self.G = self.sb("G", [128, 256], BF16)
        self.FF = self.sb("FF", [128, 2, 128], BF16)
        self.gpre = self.sb("gpre", [128, D], F32)
        self.gpost = self.sb("gpost", [128, D], F32)
        self.bm = self.sb("bm", [128, 16], F32)
        self.ws = self.sb("ws", [128, 72], F32)
        self.sgn = self.sb("sgn", [128, 1], F32)
        self.ones = self.sb("ones", [128, 128], F32)
        self.negpi = self.sb("negpi", [128, 1], F32)
        self.epsb = self.sb("epsb", [128, 1], F32)
        self.onesb = self.sb("onesb", [1, 128], BF16)
        self.ctok = Tok("consts")
        cl = [("t_ident", self.ident), ("t_F1d", self.F1d), ("t_F1k", self.F1k), ("t_G", self.G),
              ("t_FF", self.FF), ("t_sgn", self.sgn)]
        for i, (n, tgt) in enumerate(cl):
            self.dma(tgt[:], self.T[n], [], [self.ctok], "const", last=False)
        self.dma(self.bm[:], self.W["bm_t"], [], [self.ctok], "const", last=False)
        self.dma(self.ws[:], self.W["ws_t"], [], [self.ctok], "const", last=False)
        self.dma(self.gpre[:], self.W["g_pre"].broadcast_to([128, D]), [], [self.ctok], "const", last=False)
        self.dma(self.gpost[:], self.W["g_post"].broadcast_to([128, D]), [], [self.ctok], "const", last=True)
        self.op("pool", lambda e: e.memset(self.ones[:], 1.0), writes=[self.ctok])
        self.op("pool", lambda e: e.memset(self.negpi[:], float(-np.pi)), writes=[self.ctok])
        self.op("pool", lambda e: e.memset(self.epsb[:], float(RMS_EPS)), writes=[self.ctok])
        self.op("pool", lambda e: e.memset(self.onesb[:], 1.0), writes=[self.ctok])
        self.P.barrier()

        if d.get("prologue_w", True):
            self.prologue_w()
            self.P.barrier()
        if d.get("prologue_k", True):
            self.prologue_k()
            self.P.barrier()
        for s in range(d.get("nseq", NSEQ)):
            if d.get("phase0", True):
                self.phase0(s)
                self.P.barrier()
            for cb in d.get("cbs", range(8)):
                if d.get("phaseF", True):
                    self.phaseF(s, cb)
                    self.P.barrier(skip=("u_stF", "u_stH"))
                if d.get("phaseH", True):
                    self.phaseH(s, cb)
                    self.P.barrier(skip=("u_stF", "u_stH"))
            self.P.barrier()
            if d.get("tail", True):
                self.tail(s)
                self.P.barrier()
        finals = [n for n in self.P.dma_streams]
        self.P.emit(final_streams=finals)
        self.st.close()
        return nc

    def prologue_w(self):
        win = self.W["w_in"].rearrange("(kc p) n -> p kc n", p=128)
        wsts = [self.av(0, [128, 8, 1024], F32), self.av(64 * 1024, [128, 8, 1024], F32)]
        wbss = [self.av(32 * 1024, [128, 8, 1024], BF16), self.av(96 * 1024, [128, 8, 1024], BF16)]
        t_wsts, t_wbss = [Tok(), Tok()], [Tok(), Tok()]
        col0 = {2: 1024, 3: 2048, 4: 3072, 5: 4096, 6: 5120}
        pi = 0
        for slot in range(2, 7):
            wst, wbs, t_wst, t_wbs = wsts[pi % 2], wbss[pi % 2], t_wsts[pi % 2], t_wbss[pi % 2]
            self.dma(wst, win[:, :, col0[slot]:col0[slot] + 1024], [], [t_wst], "pw_ld%d" % (pi % 2))
            self.copy("act" if pi % 2 else "dve", wbs, wst, [t_wst], [t_wbs])
            for cb in range(8):
                self.dma(self.s_wbf[slot, cb].rearrange("p (kc c) -> p kc c", kc=8),
                         wbs[:, :, 128 * cb:128 * cb + 128], [t_wbs], [Tok()], "pw_st%d" % (pi % 2), last=(cb == 7))
            pi += 1
        srcs = [win[:, :, 6144:7168], win[:, :, 7168:8192],
                self.W["wf"].rearrange("(kc p) n -> p kc n", p=128),
                self.W["wh"].rearrange("(kc p) n -> p kc n", p=128),
                self.W["wo"].rearrange("(kc p) n -> p kc n", p=128)]
        for i, src in enumerate(srcs):
            wst, wbs, t_wst, t_wbs = wsts[pi % 2], wbss[pi % 2], t_wsts[pi % 2], t_wbss[pi % 2]
            self.dma(wst, src, [], [t_wst], "pw_ld%d" % (pi % 2))
            self.copy("act" if pi % 2 else "dve", wbs, wst, [t_wst], [t_wbs])
            self.dma(self.s_wt[:, :, 1024 * i:1024 * i + 1024], wbs, [t_wbs], [Tok()], "pw_st%d" % (pi % 2))
            pi += 1
        wst, wbs, t_wst, t_wbs = wsts[0], wbss[0], t_wsts[0], t_wbss[0]
        cs = self.av(48 * 1024, [128, 2, 2, 256], BF16)
        t_cs = Tok()
        self.dma(cs, self.T["t_CS"], [], [t_cs], "pw_cs")
        wfT = self.av(52 * 1024, [128, 2, 1024], BF16)
        abst = self.av(56 * 1024, [128, 8, 2, 256], BF16)
        t_wfT, t_ab = Tok(), Tok()
        for g in range(4):
            self.dma(wst[:, :, 0:256], win[:, :, 256 * g:256 * g + 256], [], [t_wst], "pw_ld0")
            self.copy("dve", wbs[:, :, 0:256], wst[:, :, 0:256], [t_wst], [t_wbs])
            for half in range(2):
                for kq in range(2):
                    ps, pt = self.bank()
                    for k4 in range(4):
                        kc = 4 * kq + k4
                        self.op("pe", lambda e, ps=ps, kc=kc, half=half, k4=k4: e.matmul(
                            ps[:, 128 * k4:128 * k4 + 128], lhsT=wbs[:, kc, 128 * half:128 * half + 128],
                            rhs=self.ident[:], start=True, stop=True), reads=[t_wbs, self.ctok], writes=pt)
                    self.copy(self.ev(), wfT[:, half, 512 * kq:512 * kq + 512], ps, pt, [t_wfT])
            for kc in range(8):
                ps, pt = self.bank()
                for ab in range(2):
                    for half in range(2):
                        self.op("pe", lambda e, ps=ps, kc=kc, ab=ab, half=half: e.matmul(
                            ps[:, 256 * ab:256 * ab + 256], lhsT=wfT[:, half, 128 * kc:128 * kc + 128],
                            rhs=cs[:, half, ab, :], start=(half == 0), stop=(half == 1)),
                            reads=[t_wfT, t_cs], writes=pt)
                self.copy(self.ev(), abst[:, kc, :, :], ps.rearrange("p (a c) -> p a c", a=2), pt, [t_ab])
            for ab in range(2):
                for hc in range(2):
                    cb = 2 * g + hc
                    self.dma(self.s_wbf[ab, cb].rearrange("p (kc c) -> p kc c", kc=8),
                             abst[:, :, ab, 128 * hc:128 * hc + 128], [t_ab], [Tok()], "pw_st2",
                             last=(ab == 1 and hc == 1))

    def prologue_k(self):
        KB = 1024
        h2T = self.av(0, [64, 16384], BF16)
        w3b = self.av(32 * KB, [64, 4096], BF16)
        zfc = self.av(40 * KB, [33, 2048], F32)
        pre = self.av(48 * KB, [64, 2048], F32)
        h1 = self.av(56 * KB, [64, 2048], F32)
        w3s = self.av(64 * KB, [64, 4096], F32)
        rk = self.av(80 * KB, [64, 2048], F32)
        t_rk = Tok()
        sm = self.sb("pk_small", [64, 8 + 64 + 64], F32)
        t_sm, t_w3s, t_w3b, t_h2T = Tok(), Tok(), Tok(), Tok()
        self.dma(sm[:, 0:1], self.W["fb1"], [], [t_sm], "pk_c", last=False)
        self.dma(sm[:, 1:2], self.W["ff1"], [], [t_sm], "pk_c", last=False)
        self.dma(sm[:, 2:3], self.W["fb2"], [], [t_sm], "pk_c", last=False)
        self.dma(sm[:, 3:4], self.W["ff2"], [], [t_sm], "pk_c", last=False)
        self.dma(sm[0:33, 8:72], self.W["fw1"], [], [t_sm], "pk_c", last=False)
        self.dma(sm[:, 72:136], self.W["fw2"], [], [t_sm], "pk_c", last=False)
        self.dma(w3s, self.W["fw3"], [], [t_w3s], "pk_c", last=True)
        self.copy("dve", w3b, w3s, [t_w3s], [t_w3b])
        t_zf, t_pre, t_h1 = Tok(), Tok(), Tok()
        for ch in range(8):
            self.dma(zfc, self.T["t_zf"][:, 2048 * ch:2048 * ch + 2048], [], [t_zf], "pk_zf")
            for layer in range(2):
                for q in range(4):
                    ps, pt = self.bank()
                    if layer == 0:
                        self.op("pe", lambda e, ps=ps, q=q: e.matmul(
                            ps[0:64, :], lhsT=sm[0:33, 8:72], rhs=zfc[:, 512 * q:512 * q + 512],
                            start=True, stop=True), reads=[t_sm, t_zf], writes=pt)
                    else:
                        self.op("pe", lambda e, ps=ps, q=q: e.matmul(
                            ps[0:64, :], lhsT=sm[:, 72:136], rhs=h1[:, 512 * q:512 * q + 512],
                            start=True, stop=True), reads=[t_sm, t_h1], writes=pt)
                    bcol = 0 if layer == 0 else 2
                    self.op("dve", lambda e, ps=ps, q=q, bcol=bcol: e.tensor_scalar(
                        out=pre[:, 512 * q:512 * q + 512], in0=ps[0:64, :], scalar1=sm[:, bcol:bcol + 1],
                        scalar2=sm[:, bcol + 1:bcol + 2], op0=ALU.add, op1=ALU.mult),
                        reads=pt + [t_sm], writes=[t_pre])
                MAGIC = 12582912.0
                self.op("dve", lambda e: e.tensor_scalar(
                    out=rk[:, :], in0=pre[:, :], scalar1=float(1.0 / PI2), scalar2=MAGIC,
                    op0=ALU.mult, op1=ALU.add), reads=[t_pre], writes=[t_rk])
                self.op("dve", lambda e: e.tensor_scalar_add(out=rk[:, :], in0=rk[:, :], scalar1=-MAGIC),
                        reads=[t_rk], writes=[t_rk])
                self.op("dve", lambda e: e.scalar_tensor_tensor(
                    out=pre[:, :], in0=rk[:, :], scalar=float(-PI2), in1=pre[:, :], op0=ALU.mult, op1=ALU.add),
                    reads=[t_rk, t_pre], writes=[t_pre])
                if layer == 0:
                    self.op("act", lambda e: e.activation(out=h1[:, :], in_=pre[:, :], func=AF.Sin,
                                                          scale=1.0),
                            reads=[t_pre, self.ctok], writes=[t_h1])
                else:
                    self.op("act", lambda e, ch=ch: e.activation(
                        out=h2T[:, 2048 * ch:2048 * ch + 2048], in_=pre[:, :], func=AF.Sin,
                        scale=1.0), reads=[t_pre, self.ctok], writes=[t_h2T])
        if self.dbg.get("dump_h2T"):
            self.dump("h2T", h2T, [64, 16384], BF16, [t_h2T])
        ZT = self.av(40 * KB, [128, 128, 128], BF16)
        Es = [self.av(72 * KB, [128, 64, 128], F32), self.av(40 * KB, [128, 64, 128], F32)]
        t_Es = [Tok(), Tok()]
        hk = self.av(104 * KB, [128, 128, 128], BF16)
        KA = self.av(136 * KB, [128, 128, 2, 64], BF16)
        Ms = [self.av((168 + 6 * i) * KB, [128, 8, 3, 128], BF16) for i in range(2)]
        absd = self.sb("pk_absd", [128, 128], F32)
        rs = self.sb("pk_rs", [128, 128], F32)
        rn = self.sb("pk_rn", [128, 128], F32)
        dsk = self.sb("pk_dsk", [128, 128], F32)
        tneg = self.sb("pk_tneg", [128, 128], F32)
        t_dsk, t_tneg = Tok(), Tok()
        self.dma(tneg[:], self.T["t_tneg"], [], [t_tneg], "pk_c2")
        self.P.barrier()
        bp = _bperm()
        t_Ms = [Tok(), Tok()]
        mi = 0
        t_absd, t_E, t_hk, t_rs, t_rn, t_ZT, t_KA = Tok(), Tok(), MT("hk"), Tok(), Tok(), MT("ZTk"), MT("KA")
        t_KA2 = Tok()
        dSb = self.sb("pk_dSb", [1, 128], F32)
        t_hk2 = Tok()
        t_dS = Tok()
        iters = [(o, cb) for o in range(2) for cb in self.dbg.get("kcbs", range(8))]
        absds = [absd, self.sb("pk_absd2", [128, 128], F32)]
        dsks = [dsk, self.sb("pk_dsk2", [128, 128], F32)]
        t_absds = [Tok(), Tok()]
        t_dsks = [Tok(), Tok()]

        def small_loads(n):
            o, cb = iters[n]
            ab, tab = absds[n % 2], t_absds[n % 2]
            self.dma(dsks[n % 2][:], self.W["hd"][o:o + 1, 128 * cb:128 * cb + 128].broadcast_to([128, 128]), [],
                     [t_dsks[n % 2]], "pk_dk%d" % (n % 2))
            for dr in range(2):
                src = self.W["fdecay"][2 * dr + o:2 * dr + o + 1, 128 * cb:128 * cb + 128]
                self.dma(ab[64 * dr:64 * dr + 64, :], src.broadcast_to([64, 128]), [], [tab], "pk_ad%d" % (n % 2),
                         last=(dr == 1))
            self.op("act", lambda e: e.activation(out=ab[:], in_=ab[:], func=AF.Abs), reads=[tab], writes=[tab])

        def ecomp(n, chh):
            E, t_E = Es[chh], t_Es[chh]
            ab, tab = absds[n % 2], t_absds[n % 2]
            aft = [t_ZT] if chh == 1 else []
            self.op("pool", lambda e: e.tensor_tensor(
                out=E[:, :, :], in0=ab[:, 64 * chh:64 * chh + 64].unsqueeze(2).broadcast_to([128, 64, 128]),
                in1=tneg[:].unsqueeze(1).broadcast_to([128, 64, 128]), op=ALU.mult),
                reads=[tab, t_tneg], writes=[t_E], after=aft)
            self.op("act", lambda e: e.activation(out=E[:, :, :], in_=E[:, :, :], func=AF.Exp),
                    reads=[t_E], writes=[t_E])
            self.op("pool", lambda e: e.memset(E[64:65, :, 0:1], 0.0), reads=[t_E], writes=[t_E])

        def h3_hk(n, chh):
            o, cb = iters[n]
            E, t_E = Es[chh], t_Es[chh]
            for jq in range(16):
                ps, pt = self.bank()
                for j8 in range(8):
                    j = 8 * jq + j8
                    for dr in range(2):
                        col = dr * 2048 + o * 1024 + cb * 128 + 64 * chh
                        self.op("pe", lambda e, ps=ps, j8=j8, dr=dr, col=col, j=j: e.matmul(
                            ps[64 * dr:64 * dr + 64, 64 * j8:64 * j8 + 64],
                            lhsT=h2T[:, 128 * j + 64 * dr:128 * j + 64 * dr + 64],
                            rhs=w3b[:, col:col + 64], start=True, stop=True),
                            reads=[t_h2T, t_w3b], writes=pt)
                self.op("dve", lambda e, ps=ps, jq=jq: e.scalar_tensor_tensor(
                    out=hk[:, 64 * chh:64 * chh + 64, 8 * jq:8 * jq + 8],
                    in0=ps.rearrange("p (j c) -> p c j", j=8), scalar=self.sgn[:, 0:1],
                    in1=E[:, :, 8 * jq:8 * jq + 8], op0=ALU.mult, op1=ALU.mult),
                    reads=pt + [t_E, self.ctok], writes=[t_hk], after=[t_hk2])

        def norm(n):
            o, cb = iters[n]
            self.op("dve", lambda e: e.tensor_reduce(out=rs[:], in_=hk[:, :, :], axis=AX.X, op=ALU.add,
                                                     apply_absolute_value=True),
                    reads=[t_hk], writes=[t_rs])
            ps, pt = self.bank()
            self.op("pe", lambda e, ps=ps: e.matmul(ps[0:1, 0:128], lhsT=self.ones[:, 0:1], rhs=rs[:], start=True, stop=True),
                    reads=[t_rs, self.ctok], writes=pt)
            self.op("pe", lambda e, ps=ps: e.matmul(ps[:, 128:129], lhsT=rs[:], rhs=self.ones[:, 0:1], start=True, stop=True),
                    reads=[t_rs, self.ctok], writes=pt)
            dk, tdk = dsks[n % 2], t_dsks[n % 2]
            self.op("dve", lambda e, ps=ps: e.tensor_tensor(out=dSb[0:1, :], in0=ps[0:1, 0:128], in1=dk[0:1, :], op=ALU.mult),
                    reads=pt + [tdk], writes=[t_dS])
            self.op("dve", lambda e: e.tensor_tensor(out=hk[0:1, :, 0], in0=hk[0:1, :, 0], in1=dSb[0:1, :], op=ALU.add),
                    reads=[t_hk, t_dS], writes=[t_hk2])
            self.op("dve", lambda e, ps=ps: e.reciprocal(out=rn[:, 0:1], in_=ps[:, 128:129]), reads=pt, writes=[t_rn])
            self.dma(self.s_rn[o, cb], rn[:, 0:1], [t_rn], [Tok()], "rn_st")
            if self.dbg.get("dump_hk") and o == 0:
                self.dump("hk%d" % cb, hk, [128, 128, 128], BF16, [t_hk])

        def fft1(n):
            for c4 in range(32):
                ps, pt = self.bank()
                for ci in range(4):
                    c = 4 * c4 + ci
                    self.op("pe", lambda e, ps=ps, ci=ci, c=c: e.matmul(
                        ps[:, 128 * ci:128 * ci + 128], lhsT=hk[:, c, :], rhs=self.F1k[:],
                        start=True, stop=True), reads=[t_hk, t_hk2, self.ctok], writes=pt)
                self.copy(self.ev(), ZT[:, :, 4 * c4:4 * c4 + 4], ps.rearrange("p (c k) -> p k c", c=4), pt, [t_ZT],
                          after=[t_Es[1]])

        mi_box = [0]

        def fft2(n):
            o, cb = iters[n]
            for k1q in range(8):
                mi = mi_box[0]
                Mb = Ms[mi % 2]
                tM = t_Ms[mi % 2]
                self.dma(Mb, self.T["t_M"][:, 8 * k1q:8 * k1q + 8], [], [tM], "Ms%d" % (mi % 2))
                mi_box[0] += 1
                for kp in range(4):
                    ps, pt = self.bank()
                    for ki in range(2):
                        k8 = 2 * kp + ki
                        k1 = 8 * k1q + k8
                        base = 256 * ki
                        mm = [(0, k1, 0, True, False), (2, 64 + k1, 0, False, True),
                              (1, k1, 128, True, False), (0, 64 + k1, 128, False, True)]
                        for mi_, (mt, zrow, off, sta, sto) in enumerate(mm):
                            self.op("pe", lambda e, ps=ps, Mb=Mb, k8=k8, mt=mt, zrow=zrow, off=off, sta=sta,
                                    sto=sto, base=base: e.matmul(
                                ps[:, base + off:base + off + 128], lhsT=Mb[:, k8, mt, :], rhs=ZT[:, zrow, :],
                                start=sta, stop=sto), reads=[tM, t_ZT], writes=pt)
                    k1a = 8 * k1q + 2 * kp
                    self.copy(self.ev(), KA[:, :, :, k1a:k1a + 2], ps.rearrange("p (k h c) -> p c h k", k=2, h=2), pt,
                              [t_KA])
            self.dma(self.s_ka[o, cb], KA.rearrange("p c h k -> p (c h k)"), [t_KA], [Tok()], "ka_st")

        NI = len(iters)
        small_loads(0)
        ecomp(0, 0)
        for n in range(NI):
            if n + 1 < NI:
                small_loads(n + 1)
            ecomp(n, 1)
            h3_hk(n, 0)
            h3_hk(n, 1)
            norm(n)
            if n + 1 < NI:
                ecomp(n + 1, 0)
            fft1(n)
            fft2(n)

    def phase0(self, s):
        KB = 1024
        xts = [self.av(16 * KB * i, [128, 4, D], F32) for i in range(2)]
        xn = self.av(32 * KB, [128, 4, D], BF16)
        xnTs = [self.av((40 + 8 * i) * KB, [128, 8, 512], BF16) for i in range(2)]
        sqj = self.av(56 * KB, [128, D], BF16)
        st = self.sb("p0_st%d" % s, [128, 16 * 4 * 2], F32)
        t_x = [Tok(), Tok()]
        t_xn, t_sq = MT("xn"), Tok()
        t_xnT = [MT("xnT0"), MT("xnT1")]
        xsrc = self.xs[s].rearrange("(a b) d -> b a d", b=128)
        for q in range(16):
            xt = xts[q % 2]
            tx = t_x[q % 2]

            def xload(q2):
                for i in range(4):
                    for e_ in range(2):
                        b = 2 * (4 * q2 + i) + e_
                        self.dma(xts[q2 % 2][64 * e_:64 * e_ + 64, i, :], xsrc[b], [], [t_x[q2 % 2]],
                                 "p0_x%d" % (q2 % 2), last=(i == 3 and e_ == 1))
            if q == 0:
                xload(0)
            if q + 1 < 16:
                xload(q + 1)
            lvl = self.dbg.get("p0_lvl", 9)
            for i in range(4):
                c0 = 8 * q + 2 * i
                t_st = Tok()
                self.op("act", lambda e, xt=xt, i=i, c0=c0: e.activation(
                    out=sqj[:, :], in_=xt[:, i, :], func=AF.Square, accum_out=st[:, c0:c0 + 1]),
                    reads=[tx], writes=[t_sq, t_st])
                if lvl < 2:
                    continue
                self.op("act", lambda e, c0=c0: e.activation(
                    out=st[:, c0 + 1:c0 + 2], in_=st[:, c0:c0 + 1], func=AF.Sqrt, scale=1.0 / D, bias=self.epsb[:]),
                    reads=[t_st, self.ctok], writes=[t_st])
                self.op("dve", lambda e, c0=c0: e.reciprocal(out=st[:, c0 + 1:c0 + 2], in_=st[:, c0 + 1:c0 + 2]),
                        reads=[t_st], writes=[t_st])
                if lvl < 3:
                    continue
                self.op("dve", lambda e, xt=xt, i=i, c0=c0: e.scalar_tensor_tensor(
                    out=xn[:, i, :], in0=xt[:, i, :], scalar=st[:, c0 + 1:c0 + 2], in1=self.gpre[:],
                    op0=ALU.mult, op1=ALU.mult), reads=[tx, t_st, self.ctok], writes=[t_xn])
            xnT = xnTs[q % 2]
            txT = t_xnT[q % 2]
            if lvl < 4:
                continue
            for kq in range(2):
                pss = [self.bank() for _ in range(4)]
                for k4 in range(4):
                    kc = 4 * kq + k4
                    ps, pt = pss[k4]
                    for i in range(4):
                        self.op("pe", lambda e, ps=ps, i=i, kc=kc: e.matmul(
                            ps[:, 128 * i:128 * i + 128], lhsT=xn[:, i, 128 * kc:128 * kc + 128], rhs=self.ident[:],
                            start=True, stop=True), reads=[t_xn, self.ctok], writes=pt)
                    self.copy(self.ev(), xnT[:, kc, :], ps, pt, [txT])
            self.dma(self.s_xnT[q], xnT.rearrange("p k t -> p (k t)"), [txT], [Tok()], "p0_st%d" % (q % 2))

    def load_wblk(self, cb, slots, wblk, t_w):
        for i, sl in enumerate(slots):
            self.dma(wblk[:, :, i, :], self.s_wbf[sl, cb].rearrange("p (kc c) -> p kc c", kc=8), [], [t_w], "wblk",
                     last=(i == len(slots) - 1))

    def phaseF(self, s, cb):
        KB = 1024
        wblk = self.av(0, [128, 8, 3, 128], BF16)
        xnTs = [self.av((14 + 8 * i) * KB, [128, 8, 512], BF16) for i in range(2)]
        UAB = self.av(30 * KB, [128, 2, 128, 64], BF16)
        sfg = self.av(64 * KB, [128, L], BF16)
        ZT = self.av(80 * KB, [128, 128, 128], BF16)
        FMs = [self.av((120 + 4 * i) * KB, [128, 8, 2, 128], BF16) for i in range(8)]
        t_w, t_U, t_sfg, t_ZT, t_uf = Tok(), MT("U"), MT("sfg"), MT("ZT"), MT("uf")
        t_xT = [Tok(), Tok()]
        t_FM = [Tok() for _ in range(8)]
        self.load_wblk(cb, [0, 1, 2], wblk, t_w)
        for q in range(16):
            xnT, tx = xnTs[q % 2], t_xT[q % 2]
            self.dma(xnT.rearrange("p k t -> p (k t)"), self.s_xnT[q], [], [tx], "xnT%d" % (q % 2))
            for ip in range(2):
                ps, pt = self.bank()
                for i2 in range(2):
                    i = 2 * ip + i2
                    for kc in range(8):
                        self.op("pe", lambda e, ps=ps, i2=i2, i=i, kc=kc, xnT=xnT: e.matmul(
                            ps[:, 256 * i2:256 * i2 + 256], lhsT=xnT[:, kc, 128 * i:128 * i + 128],
                            rhs=wblk[:, kc, 0:2, :], start=(kc == 0), stop=(kc == 7)),
                            reads=[tx, t_w], writes=pt)
                bq = 4 * q + 2 * ip
                self.copy(self.ev(), UAB[:, :, :, bq:bq + 2],
                          ps.rearrange("p (t s c) -> p s c t", t=2, s=2), pt, [t_U])
            ps, pt = self.bank()
            for kc in range(8):
                self.op("pe", lambda e, ps=ps, kc=kc, xnT=xnT: e.matmul(
                    ps[:, :], lhsT=wblk[:, kc, 2, :], rhs=xnT[:, kc, :], start=(kc == 0), stop=(kc == 7)),
                    reads=[tx, t_w], writes=pt)
            self.op("act", lambda e, ps=ps, q=q: e.activation(out=sfg[:, 512 * q:512 * q + 512], in_=ps, func=AF.Silu),
                    reads=pt, writes=[t_sfg], after=[self.tok_uF])
        for k1q in range(8):
            self.dma(FMs[k1q], self.T["t_FM"][:, 8 * k1q:8 * k1q + 8], [], [t_FM[k1q]], "FMs%d" % k1q)
        for c4 in range(32):
            ps, pt = self.bank()
            for ci in range(4):
                c = 4 * c4 + ci
                for e_ in range(2):
                    for s2 in range(2):
                        self.op("pe", lambda e, ps=ps, ci=ci, c=c, e_=e_, s2=s2: e.matmul(
                            ps[64 * e_:64 * e_ + 64, 128 * ci:128 * ci + 128],
                            lhsT=UAB[64 * e_:64 * e_ + 64, s2, c, :], rhs=self.FF[64 * e_:64 * e_ + 64, s2, :],
                            start=(s2 == 0), stop=(s2 == 1)), reads=[t_U, self.ctok], writes=pt)
            self.copy(self.ev(), ZT[:, :, 4 * c4:4 * c4 + 4], ps.rearrange("p (c k) -> p k c", c=4), pt, [t_ZT],
                      after=[self.tok_uH])
        if self.dbg.get("dump_F") and cb == self.dbg["dump_F"][1] and s == self.dbg["dump_F"][0]:
            self.dump("F_UAB", UAB.rearrange("p s c b -> p (s c b)"), [128, 2 * 128 * 64], BF16, [t_U])
            self.dump("F_ZT", ZT.rearrange("p k c -> p (k c)"), [128, 128 * 128], BF16, [t_ZT])
            self.dump("F_sfg", sfg, [128, L], BF16, [t_sfg])
        self.pre = None
        if self.dbg.get("phaseH", True) and self.dbg.get("preload", True):
            wblkH = self.av(0, [128, 8, 4, 128], BF16)
            xnT0H = self.av(16 * KB, [128, 8, 512], BF16)
            t_wH, t_x0H = Tok(), Tok()
            aft = [t_w, t_xT[0], t_xT[1]]
            for i_, sl_ in enumerate([3, 4, 5, 6]):
                self.op("sp", lambda e, i_=i_, sl_=sl_: e.dma_start(
                    out=wblkH[:, :, i_, :], in_=self.s_wbf[sl_, cb].rearrange("p (kc c) -> p kc c", kc=8)),
                    writes=[t_wH], dma="wblk", last=(i_ == 3), after=aft)
            self.op("sp", lambda e: e.dma_start(out=xnT0H.rearrange("p k t -> p (k t)"), in_=self.s_xnT[0]),
                    writes=[t_x0H], dma="xnT0", after=aft)
            self.pre = (s, cb, t_wH, t_x0H)
        sfgv = sfg.rearrange("p (h k a) -> p k h a", h=2, k=64)
        for k1q in range(8):
            FMb, tF = FMs[k1q], t_FM[k1q]
            for kp in range(2):
                ps, pt = self.bank()
                for ki in range(4):
                    k8 = 4 * kp + ki
                    k1 = 8 * k1q + k8
                    for ri in range(2):
                        self.op("pe", lambda e, ps=ps, ki=ki, k8=k8, k1=k1, ri=ri, FMb=FMb: e.matmul(
                            ps[:, 128 * ki:128 * ki + 128], lhsT=ZT[:, 64 * ri + k1, :], rhs=FMb[:, k8, ri, :],
                            start=(ri == 0), stop=(ri == 1)), reads=[t_ZT, tF], writes=pt)
                k1a = 8 * k1q + 4 * kp
                self.op("dve", lambda e, ps=ps, k1a=k1a: e.tensor_tensor(
                    out=sfgv[:, k1a:k1a + 4], in0=ps.rearrange("p (k h a) -> p k h a", k=4, h=2),
                    in1=sfgv[:, k1a:k1a + 4], op=ALU.mult), reads=pt + [t_sfg], writes=[t_uf])
        self.dma(self.s_u[0, cb], sfg, [t_uf], [self.tok_uF], "u_stF")

    def phaseH(self, s, cb):
        KB = 1024
        wblk = self.av(0, [128, 8, 4, 128], BF16)
        xnTs = [self.av((16 + 8 * i) * KB, [128, 8, 512], BF16) for i in range(2)]
        Uv = self.av(32 * KB, [128, 128, 64], BF16)
        Ux1 = self.av(48 * KB, [128, 128, 64], BF16)
        x2c = self.av(64 * KB, [128, L], BF16)
        shg = self.av(80 * KB, [128, L], BF16)
        raw = [self.av((96 + 16 * i) * KB, [128, L], BF16) for i in range(3)]
        ctmp = self.av(144 * KB, [128, L], BF16)
        ZT = self.av(96 * KB, [128, 128, 128], BF16)
        O0 = ZT.rearrange("p k c -> p (k c)").rearrange("p (c n) -> p c n", c=128)
        Yb = self.av(128 * KB, [128, 128, 2, 64], BF16)
        O1 = Yb.rearrange("p c r k -> p c (r k)")
        Ms = [self.av(o_ * KB, [128, 8, 3, 128], BF16) for o_ in (0, 6, 64, 70)]
        As = [self.av((12 + 4 * i) * KB, [128, 16, 2, 64], BF16) for i in range(2)]
        KAs = [self.av(o_ * KB, [128, 16, 2, 64], BF16) for o_ in (20, 24, 76, 176)]
        P1s = [self.av((160 + 4 * i) * KB, [128, 16, 2, 64], BF16) for i in range(2)]
        P2s = [self.av((168 + 4 * i) * KB, [128, 16, 2, 64], BF16) for i in range(2)]
        t_w = Tok()
        t_xT = [Tok(), Tok()]
        t_rawc = [[Tok() for _ in range(16)] for _ in range(3)]
        t_shgc = [Tok() for _ in range(16)]
        t_dc = [[Tok() for _ in range(16)] for _ in range(3)]
        t_g2c = [Tok() for _ in range(16)]
        t_Uv, t_Ux1 = MT("Uv"), MT("Ux1")
        t_uh, t_Uz = MT("uh"), MT("Uz")
        x1c = self.av(160 * KB, [128, L], BF16)
        dsts = [ctmp, x1c, x2c]
        Us = [(Uv, t_Uv), (Ux1, t_Ux1)]
        hl = self.dbg.get("h_lvl", 99)
        rnc = self.sb("h_rnc_%d_%d" % (s, cb), [128, 2], F32)
        wsx = self.sb("h_wsx_%d_%d" % (s, cb), [128, 6], F32)
        t_rnc, t_wsx = Tok(), Tok()
        for o_ in range(2):
            self.dma(rnc[:, o_:o_ + 1], self.s_rn[o_, cb], [], [t_rnc], "h_rn", last=(o_ == 1))
        for o_ in range(2):
            c0_ = ((1 + o_) * 8 + cb) * 3
            self.op("dve", lambda e, o_=o_, c0_=c0_: e.tensor_scalar_mul(
                out=wsx[:, 3 * o_:3 * o_ + 3], in0=self.ws[:, c0_:c0_ + 3], scalar1=rnc[:, o_:o_ + 1]),
                reads=[t_rnc, self.ctok], writes=[t_wsx])

        def wsc(ti, tap):
            if ti == 0:
                return self.ws[:, (ti * 8 + cb) * 3 + tap:(ti * 8 + cb) * 3 + tap + 1]
            return wsx[:, 3 * (ti - 1) + tap:3 * (ti - 1) + tap + 1]

        def conv_chunk(r):
            lo, hi = 512 * r, 512 * r + 512
            for ti in range(3):
                rw, dst = raw[ti], dsts[ti]
                rd = [t_rawc[ti][q] for q in (r - 1, r, r + 1) if 0 <= q < 16] + [t_wsx]
                self.op("act", lambda e, rw=rw, dst=dst, ti=ti: e.activation(
                    out=dst[:, lo:hi], in_=rw[:, lo:hi], func=AF.Copy, scale=wsc(ti, 1)),
                    reads=rd + [self.ctok], writes=[t_dc[ti][r]], after=([self.tok_uF] if ti == 2 else []))
                a0 = max(lo, 64)
                self.op("dve", lambda e, rw=rw, dst=dst, ti=ti, a0=a0: e.scalar_tensor_tensor(
                    out=dst[:, a0:hi], in0=rw[:, a0 - 64:hi - 64], scalar=wsc(ti, 0), in1=dst[:, a0:hi],
                    op0=ALU.mult, op1=ALU.add), reads=rd + [t_dc[ti][r], self.ctok], writes=[t_dc[ti][r]])
                b1 = min(hi, L - 64)
                self.op("dve", lambda e, rw=rw, dst=dst, ti=ti, b1=b1: e.scalar_tensor_tensor(
                    out=dst[:, lo:b1], in0=rw[:, lo + 64:b1 + 64], scalar=wsc(ti, 2), in1=dst[:, lo:b1],
                    op0=ALU.mult, op1=ALU.add), reads=rd + [t_dc[ti][r], self.ctok], writes=[t_dc[ti][r]])

        def post_chunk(r):
            self.op("dve", lambda e: e.tensor_tensor(out=shg[:, 512 * r:512 * r + 512], in0=shg[:, 512 * r:512 * r + 512],
                                                     in1=x2c[:, 512 * r:512 * r + 512], op=ALU.mult),
                    reads=[t_dc[2][r], t_shgc[r]], writes=[t_g2c[r]])
            if hl < 3:
                return
            for ti, (U, tU) in enumerate(Us):
                ps, pt = self.bank()
                for bi in range(4):
                    bq = 4 * r + bi
                    self.op("pe", lambda e, ps=ps, bi=bi, bq=bq, src=dsts[ti]: e.matmul(
                        ps[:, 128 * bi:128 * bi + 128], lhsT=src[:, 128 * bq:128 * bq + 128], rhs=self.ident[:],
                        start=True, stop=True), reads=[t_dc[ti][r], self.ctok], writes=pt)
                self.copy(self.ev(), U[:, :, 4 * r:4 * r + 4], ps.rearrange("p (b c) -> p c b", b=4), pt, [tU])

        pre = getattr(self, "pre", None)
        pre_ok = bool(pre) and pre[0] == s and pre[1] == cb
        if pre_ok:
            t_w = pre[2]
            t_xT[0] = pre[3]
        else:
            self.load_wblk(cb, [3, 4, 5, 6], wblk, t_w)
        self.pre = None
        for q in range(16):
            xnT, tx = xnTs[q % 2], t_xT[q % 2]
            if not (pre_ok and q == 0):
                self.dma(xnT.rearrange("p k t -> p (k t)"), self.s_xnT[q], [], [tx], "xnT%d" % (q % 2))
            for sl in range(4):
                ps, pt = self.bank()
                for kc in range(8):
                    self.op("pe", lambda e, ps=ps, kc=kc, xnT=xnT, sl=sl: e.matmul(
                        ps[:, :], lhsT=wblk[:, kc, sl, :], rhs=xnT[:, kc, :], start=(kc == 0), stop=(kc == 7)),
                        reads=[tx, t_w], writes=pt)
                if sl < 3:
                    self.copy(self.ev(), raw[sl][:, 512 * q:512 * q + 512], ps, pt, [t_rawc[sl][q]])
                else:
                    self.op("act", lambda e, ps=ps, q=q: e.activation(out=shg[:, 512 * q:512 * q + 512], in_=ps,
                                                                       func=AF.Silu), reads=pt, writes=[t_shgc[q]],
                            after=[self.tok_uH])
            if hl >= 2 and q >= 1:
                conv_chunk(q - 1)
                if 1 <= q - 2 <= 14:
                    post_chunk(q - 2)
        if hl < 2:
            return
        conv_chunk(15)
        post_chunk(14)
        for ti in range(3):
            rw, dst = raw[ti], dsts[ti]
            self.op("dve", lambda e, rw=rw, dst=dst, ti=ti: e.scalar_tensor_tensor(
                out=dst[:, 1:64], in0=rw[:, L - 64:L - 1], scalar=wsc(ti, 0), in1=dst[:, 1:64],
                op0=ALU.mult, op1=ALU.add), reads=[t_rawc[ti][15], t_dc[ti][0], self.ctok, t_wsx], writes=[t_dc[ti][0]])
            self.op("dve", lambda e, rw=rw, dst=dst, ti=ti: e.scalar_tensor_tensor(
                out=dst[:, L - 64:L - 1], in0=rw[:, 1:64], scalar=wsc(ti, 2), in1=dst[:, L - 64:L - 1],
                op0=ALU.mult, op1=ALU.add), reads=[t_rawc[ti][0], t_dc[ti][15], self.ctok, t_wsx], writes=[t_dc[ti][15]])
        post_chunk(0)
        post_chunk(15)
        t_g2 = None
        dmp = self.dbg.get("dump_H") and cb == self.dbg["dump_H"][1] and s == self.dbg["dump_H"][0]
        if dmp:
            self.dump("H_Uv", Uv.rearrange("p c b -> p (c b)"), [128, L], BF16, [t_Uv])
            self.dump("H_Ux1", Ux1.rearrange("p c b -> p (c b)"), [128, L], BF16, [t_Ux1])
            self.dump("H_gate2", shg, [128, L], BF16, t_g2c)
        if hl < 4:
            return
        self.P.barrier()
        t_M = [Tok(), Tok(), Tok(), Tok()]
        t_A = [Tok(), Tok()]
        t_KA = [Tok(), Tok(), Tok(), Tok()]
        t_P = [MT("P0"), MT("P1")]
        mi = ai = 0
        t_ZT = MT("ZT")
        t_sub = [MT("Yb%d" % i) for i in range(8)]
        t_O1 = [MT("O1_%d" % i) for i in range(8)]
        t_O0 = [MT("O0_%d" % i) for i in range(8)]
        for conv in range(2):
            Uin, t_Uin = Uv, (t_Uv if conv == 0 else t_Uz)
            for c4 in range(32):
                ps, pt = self.bank()
                for ci in range(4):
                    c = 4 * c4 + ci
                    for e_ in range(2):
                        self.op("pe", lambda e, ps=ps, ci=ci, c=c, e_=e_: e.matmul(
                            ps[64 * e_:64 * e_ + 64, 128 * ci:128 * ci + 128], lhsT=Uin[64 * e_:64 * e_ + 64, c, :],
                            rhs=self.F1d[64 * e_:64 * e_ + 64, :], start=True, stop=True),
                            reads=[t_Uin, self.ctok], writes=pt)
                self.copy(self.ev(), ZT[:, :, 4 * c4:4 * c4 + 4], ps.rearrange("p (c k) -> p k c", c=4), pt,
                          [t_ZT], after=t_O0)
            if hl < 5:
                return
            for k1q in range(8):
                Mb, tM = Ms[mi % 4], t_M[mi % 4]
                self.dma(Mb, self.T["t_M"][:, 8 * k1q:8 * k1q + 8], [], [tM], "HMs%d" % (mi % 4))
                mi += 1
                for kp in range(4):
                    ps, pt = self.bank()
                    for ki in range(2):
                        k8 = 2 * kp + ki
                        k1 = 8 * k1q + k8
                        base = 256 * ki
                        mm = [(0, k1, 0, True, False), (2, 64 + k1, 0, False, True),
                              (1, k1, 128, True, False), (0, 64 + k1, 128, False, True)]
                        for (mt, zrow, off, sta, sto) in mm:
                            self.op("pe", lambda e, ps=ps, Mb=Mb, k8=k8, mt=mt, zrow=zrow, off=off, sta=sta, sto=sto,
                                    base=base: e.matmul(
                                ps[:, base + off:base + off + 128], lhsT=Mb[:, k8, mt, :], rhs=ZT[:, zrow, :],
                                start=sta, stop=sto), reads=[tM, t_ZT], writes=pt)
                    k1a = 8 * k1q + 2 * kp
                    self.copy(self.ev(), Yb[:, :, :, k1a:k1a + 2], ps.rearrange("p (k r c) -> p c r k", k=2, r=2), pt,
                              t_sub, after=t_O1)
            if dmp and conv == 0:
                self.dump("H_Yb", Yb.rearrange("p c r k -> p (c r k)"), [128, 16384], BF16, t_sub)
            if hl < 6:
                return
            def products(sb_):
                slot = sb_ % 2
                KAb, tK = KAs[sb_ % 4], t_KA[sb_ % 4]
                P1, P2, tP = P1s[slot], P2s[slot], t_P[slot]
                self.dma(KAb.rearrange("p c h k -> p (c h k)"),
                         self.s_ka[conv, cb][:, 2048 * sb_:2048 * sb_ + 2048], [], [tK], "KAs%d" % (sb_ % 4))
                Ysub = Yb[:, 16 * sb_:16 * sb_ + 16]
                self.op("dve" if sb_ % 2 == 0 else "pool", lambda e, Ysub=Ysub, KAb=KAb, P1=P1: e.tensor_tensor(
                    out=P1[:, :, :, :], in0=Ysub[:, :, 0:1, :].broadcast_to([128, 16, 2, 64]), in1=KAb[:, :, :, :],
                    op=ALU.mult), reads=[t_sub[sb_], tK], writes=[tP])
                self.op("dve", lambda e, Ysub=Ysub, KAb=KAb, P2=P2: e.scalar_tensor_tensor(
                    out=P2[:, :, 0, :], in0=Ysub[:, :, 1, :], scalar=-1.0, in1=KAb[:, :, 1, :],
                    op0=ALU.mult, op1=ALU.mult), reads=[t_sub[sb_], tK], writes=[tP])
                self.op("dve", lambda e, Ysub=Ysub, KAb=KAb, P2=P2: e.tensor_tensor(
                    out=P2[:, :, 1, :], in0=Ysub[:, :, 1, :], in1=KAb[:, :, 0, :], op=ALU.mult),
                    reads=[t_sub[sb_], tK], writes=[tP])

            def stage1p(sb_):
                slot = sb_ % 2
                P1, P2, tP = P1s[slot], P2s[slot], t_P[slot]
                for c4 in range(8):
                    ps, pt = self.bank()
                    for ci in range(2):
                        cl = 2 * c4 + ci
                        self.op("pe", lambda e, ps=ps, ci=ci, cl=cl, P1=P1: e.matmul(
                            ps[:, 256 * ci:256 * ci + 256], lhsT=P1[:, cl, :, :], rhs=self.G[:], start=True, stop=False),
                            reads=[tP, self.ctok], writes=pt)
                        self.op("pe", lambda e, ps=ps, ci=ci, cl=cl, P2=P2: e.matmul(
                            ps[:, 256 * ci:256 * ci + 256], lhsT=P2[:, cl, :, :], rhs=self.G[:], start=False, stop=True),
                            reads=[tP, self.ctok], writes=pt)
                    c0 = 16 * sb_ + 2 * c4
                    psv = ps.rearrange("p (c h n) -> p c h n", c=2, h=2)
                    eng_o = self.ev()
                    self.copy(eng_o, O0[:, c0:c0 + 2, :], psv[:, :, 0, :], pt, [t_O0[sb_]], after=[t_ZT])
                    self.copy(eng_o, O1[:, c0:c0 + 2, :], psv[:, :, 1, :], pt, [t_O1[sb_]], after=[t_sub[sb_]])

            products(0)
            for sb_ in range(8):
                if sb_ + 1 < 8:
                    products(sb_ + 1)
                stage1p(sb_)
            if hl < 7:
                return
            Os = [O0, O1]
            tOs = [t_O0, t_O1]
            if dmp and conv == 0:
                self.dump("H_O0", O0.rearrange("p c n -> p (c n)"), [128, 16384], BF16, t_O0)
                self.dump("H_O1", O1.rearrange("p c n -> p (c n)"), [128, 16384], BF16, t_O1)
            if conv == 0:
                t_U2 = Tok()
                for bq16 in range(8):
                    Ab, tA = As[ai % 2], t_A[ai % 2]
                    self.dma(Ab, self.T["t_A"][:, 16 * bq16:16 * bq16 + 16], [], [tA], "As%d" % (ai % 2))
                    ai += 1
                    for b4 in range(2):
                        ps, pt = self.bank()
                        for bi in range(4):
                            bp_ = 8 * bq16 + 4 * b4 + bi
                            for e_ in range(2):
                                b = 2 * bp_ + e_
                                hb, b64 = b // 64, b % 64
                                for g in range(2):
                                    self.op("pe", lambda e, ps=ps, bi=bi, e_=e_, g=g, hb=hb, b64=b64, Ab=Ab,
                                            bl=b - 16 * bq16: e.matmul(
                                        ps[64 * e_:64 * e_ + 64, 128 * bi:128 * bi + 128], lhsT=Ab[:, bl, g, :],
                                        rhs=Os[hb][:, :, 64 * g + b64], start=(g == 0), stop=(g == 1)),
                                        reads=[tA] + tOs[hb], writes=pt)
                        bq = 8 * bq16 + 4 * b4
                        self.op("dve", lambda e, ps=ps, bq=bq: e.tensor_tensor(
                            out=Uv[:, :, bq:bq + 4], in0=ps.rearrange("p (b c) -> p c b", b=4),
                            in1=Ux1[:, :, bq:bq + 4], op=ALU.mult), reads=pt + [t_Ux1], writes=[t_Uz], after=[t_Uv])
                if dmp:
                    self.dump("H_U2", Uv.rearrange("p c b -> p (c b)"), [128, L], BF16, [t_Uz])
            else:
                if hl < 9:
                    return
                for bq16 in range(8):
                    Ab, tA = As[ai % 2], t_A[ai % 2]
                    self.dma(Ab, self.T["t_A"][:, 16 * bq16:16 * bq16 + 16], [], [tA], "As%d" % (ai % 2))
                    ai += 1
                    for b8 in range(2):
                        ps, pt = self.bank()
                        for bi in range(8):
                            bl = 8 * b8 + bi
                            b = 16 * bq16 + bl
                            hb, b64 = b // 64, b % 64
                            for g in range(2):
                                self.op("pe", lambda e, ps=ps, bi=bi, g=g, hb=hb, b64=b64, Ab=Ab, bl=bl: e.matmul(
                                    ps[:, 64 * bi:64 * bi + 64], lhsT=Os[hb][:, :, 64 * g + b64], rhs=Ab[:, bl, g, :],
                                    start=(g == 0), stop=(g == 1)), reads=[tA] + tOs[hb], writes=pt)
                        i0 = 64 * (16 * bq16 + 8 * b8)
                        self.op("dve", lambda e, ps=ps, i0=i0: e.tensor_tensor(
                            out=shg[:, i0:i0 + 512], in0=ps, in1=shg[:, i0:i0 + 512], op=ALU.mult),
                            reads=pt + [t_g2c[i0 // 512]], writes=[t_uh])
        self.dma(self.s_u[1, cb], shg, [t_uh], [self.tok_uH], "u_stH")

    def tail(self, s):
        KB = 1024
        wt = self.av(0, [128, 8, 5120], BF16)
        xnTs = [self.av((80 + 8 * i) * KB, [128, 8, 512], BF16) for i in range(2)]
        ufs = [self.av((96 + 8 * i) * KB, [128, 8, 512], BF16) for i in range(2)]
        uhs = [self.av((112 + 8 * i) * KB, [128, 8, 512], BF16) for i in range(2)]
        xts = [self.av((128 + 16 * i) * KB, [128, 4, D], F32) for i in range(2)]
        mrg = self.av(160 * KB, [128, 8, 512], BF16)
        gf = self.av(168 * KB, [128, 512], BF16)
        gh = self.av(169 * KB, [128, 512], BF16)
        t1 = self.av(170 * KB, [128, 512], F32)
        t2 = self.av(172 * KB, [128, 512], F32)
        sqj = self.av(174 * KB, [128, 512], BF16)
        st = self.sb("tl_st%d" % s, [128, 16 * 4 * 4], F32)
        t_wt = Tok()
        self.dma(wt.rearrange("p k n -> p (k n)"), self.s_wt.rearrange("p k n -> p (k n)"), [], [t_wt], "wt")
        t_xT, t_uf, t_uh, t_x = [Tok(), Tok()], [Tok(), Tok()], [Tok(), Tok()], [Tok(), Tok()]
        t_mrg, t_gf, t_gh, t_t1, t_t2, t_sq = MT("mrg"), Tok(), Tok(), Tok(), Tok(), Tok()
        tmpo = self.av(176 * KB, [128, 512], F32)
        t_tmpo = Tok()
        xsrc = self.xs[s].rearrange("(a b) d -> b a d", b=128)
        ydst = self.y[s].rearrange("(a b) d -> b a d", b=128)
        for q in range(16):
            sl = q % 2
            xnT, uf, uh, xt = xnTs[sl], ufs[sl], uhs[sl], xts[sl]

            def loads(q2):
                s2 = q2 % 2
                self.dma(xnTs[s2].rearrange("p k t -> p (k t)"), self.s_xnT[q2], [], [t_xT[s2]], "xnT%d" % s2)
                self.dma(ufs[s2], self.s_u[0].rearrange("c p t -> p c t")[:, :, 512 * q2:512 * q2 + 512], [],
                         [t_uf[s2]], "tuf%d" % s2)
                self.dma(uhs[s2], self.s_u[1].rearrange("c p t -> p c t")[:, :, 512 * q2:512 * q2 + 512], [],
                         [t_uh[s2]], "tuh%d" % s2)
                for i in range(4):
                    for e_ in range(2):
                        b = 2 * (4 * q2 + i) + e_
                        self.dma(xts[s2][64 * e_:64 * e_ + 64, i, :], xsrc[b], [], [t_x[s2]], "tx%d" % s2,
                                 last=(i == 3 and e_ == 1))
            if q == 0:
                loads(0)
            if q + 1 < 16:
                loads(q + 1)
            for db in range(8):
                banks = [self.bank() for _ in range(4)]
                srcs = [(uf, t_uf[sl], 2048), (uh, t_uh[sl], 3072), (xnT, t_xT[sl], 0), (xnT, t_xT[sl], 1024)]
                for bi, (src, tsrc, wcol) in enumerate(srcs):
                    ps, pt = banks[bi]
                    for kc in range(8):
                        self.op("pe", lambda e, ps=ps, kc=kc, src=src, wcol=wcol, db=db: e.matmul(
                            ps[:, :], lhsT=wt[:, kc, wcol + 128 * db:wcol + 128 * db + 128], rhs=src[:, kc, :],
                            start=(kc == 0), stop=(kc == 7)), reads=[t_wt, tsrc], writes=pt)
                self.op("act", lambda e, ps=banks[2][0], db=db: e.activation(
                    out=gf[:, :], in_=ps, func=AF.Sigmoid, bias=self.bm[:, db:db + 1], scale=1.0),
                    reads=banks[2][1] + [self.ctok], writes=[t_gf])
                self.op("act", lambda e, ps=banks[3][0], db=db: e.activation(
                    out=gh[:, :], in_=ps, func=AF.Sigmoid, bias=self.bm[:, 8 + db:9 + db], scale=1.0),
                    reads=banks[3][1] + [self.ctok], writes=[t_gh])
                self.op("dve", lambda e, ps=banks[0][0]: e.tensor_tensor(out=t1[:, :], in0=ps, in1=gf[:, :], op=ALU.mult),
                        reads=banks[0][1] + [t_gf], writes=[t_t1])
                self.op("dve", lambda e, ps=banks[1][0]: e.tensor_tensor(out=t2[:, :], in0=ps, in1=gh[:, :], op=ALU.mult),
                        reads=banks[1][1] + [t_gh], writes=[t_t2])
                self.op("pool", lambda e, db=db: e.tensor_tensor(out=mrg[:, db, :], in0=t1[:, :], in1=t2[:, :], op=ALU.add),
                        reads=[t_t1, t_t2], writes=[t_mrg])
            for i in range(4):
                c0 = 16 * q + 4 * i
                pss = [self.bank() for _ in range(2)]
                t_st = Tok()
                for hf in range(2):
                    ps, pt = pss[hf]
                    for db in range(8):
                        self.op("pe", lambda e, ps=ps, db=db, i=i, hf=hf: e.matmul(
                            ps[:, :], lhsT=mrg[:, db, 128 * i:128 * i + 128],
                            rhs=wt[:, db, 4096 + 512 * hf:4096 + 512 * hf + 512], start=(db == 0), stop=(db == 7)),
                            reads=[t_mrg, t_wt], writes=pt)
                    self.op("act", lambda e, ps=ps, c0=c0, hf=hf: e.activation(
                        out=sqj[:, :], in_=ps, func=AF.Square, accum_out=st[:, c0 + hf:c0 + hf + 1]),
                        reads=pt, writes=[t_sq, t_st])
                self.op("dve", lambda e, c0=c0: e.tensor_tensor(
                    out=st[:, c0 + 2:c0 + 3], in0=st[:, c0:c0 + 1], in1=st[:, c0 + 1:c0 + 2], op=ALU.add),
                    reads=[t_st], writes=[t_st])
                self.op("act", lambda e, c0=c0: e.activation(
                    out=st[:, c0 + 3:c0 + 4], in_=st[:, c0 + 2:c0 + 3], func=AF.Sqrt, scale=1.0 / D, bias=self.epsb[:]),
                    reads=[t_st, self.ctok], writes=[t_st])
                self.op("dve", lambda e, c0=c0: e.reciprocal(out=st[:, c0 + 3:c0 + 4], in_=st[:, c0 + 3:c0 + 4]),
                        reads=[t_st], writes=[t_st])
                for hf in range(2):
                    ps, pt = pss[hf]
                    self.op("dve", lambda e, ps=ps, c0=c0, hf=hf, tmpo=tmpo: e.scalar_tensor_tensor(
                        out=tmpo[:, :], in0=ps, scalar=st[:, c0 + 3:c0 + 4],
                        in1=self.gpost[:, 512 * hf:512 * hf + 512], op0=ALU.mult, op1=ALU.mult),
                        reads=pt + [t_st, self.ctok], writes=[t_tmpo])
                    self.op("pool", lambda e, hf=hf, i=i, xt=xt, tmpo=tmpo: e.tensor_tensor(
                        out=xt[:, i, 512 * hf:512 * hf + 512], in0=xt[:, i, 512 * hf:512 * hf + 512],
                        in1=tmpo[:, :], op=ALU.add), reads=[t_tmpo, t_x[sl]], writes=[t_x[sl]])
            for i in range(4):
                for e_ in range(2):
                    b = 2 * (4 * q + i) + e_
                    self.dma(ydst[b], xt[64 * e_:64 * e_ + 64, i, :], [t_x[sl]], [Tok()], "ty%d" % sl,
                             last=(i == 3 and e_ == 1))


_CACHE = {}


def _get_nc(dbg=None):
    key = repr(sorted((dbg or {}).items()))
    if key not in _CACHE:
        k = K(dbg)
        nc = k.build()
        _CACHE[key] = (nc, k)
    return _CACHE[key]


def _weights_layout(inp):
    f32 = lambda a: np.ascontiguousarray(np.asarray(a, np.float32))
    w = {}
    w["w_in"] = f32(inp["w_in"][0])
    w["g_pre"] = f32(inp["g_pre"][0]).reshape(1, D)
    w["g_post"] = f32(inp["g_post"][0]).reshape(1, D)
    ws = f32(inp["w_short"][0])
    w["ws_t"] = np.ascontiguousarray(ws.reshape(3, 3, 8, 128).transpose(3, 1, 2, 0).reshape(128, 72))
    w["fw1"] = f32(inp["filt_w1"][0])
    w["fb1"] = f32(inp["filt_b1"][0]).reshape(64, 1)
    w["ff1"] = f32(inp["filt_freq1"][0]).reshape(64, 1)
    w["fw2"] = f32(inp["filt_w2"][0])
    w["fb2"] = f32(inp["filt_b2"][0]).reshape(64, 1)
    w["ff2"] = f32(inp["filt_freq2"][0]).reshape(64, 1)
    w["fw3"] = f32(inp["filt_w3"][0])
    w["fdecay"] = f32(inp["filt_decay"][0]).reshape(4, D)
    w["hd"] = f32(inp["hyena_d"][0])
    w["wf"] = f32(inp["w_fourier_out"][0])
    w["wh"] = f32(inp["w_hyena_out"][0])
    w["wo"] = f32(inp["w_out"][0])
    w["bm_t"] = np.ascontiguousarray(f32(inp["b_merge"][0]).reshape(16, 128).T)
    return w


def kernel(**inputs):
    nc, k = _get_nc()
    xp = np.asarray(inputs["x_prompt"], np.float32)
    xsm = np.asarray(inputs["x_sample"], np.float32)
    w = _weights_layout(inputs)
    tabs = _tables()
    in_maps = []
    for c in range(NCORES):
        second = xp[c] if c < 2 else xsm[c]
        m = {"xs": np.ascontiguousarray(np.stack([xsm[c], second], 0))}
        m.update(w)
        m.update(tabs)
        in_maps.append(m)
    res = run_bass_kernel_spmd(nc, in_maps, core_ids=list(range(NCORES)))
    y_sample = np.stack([res.results[c]["y"][0] for c in range(NCORES)], 0)
    y_prompt = np.stack([res.results[c]["y"][1] for c in range(2)], 0)
    return (np.ascontiguousarray(y_prompt, np.float32), np.ascontiguousarray(y_sample, np.float32))
```

```python
import contextlib
import numpy as np
import ml_dtypes
import concourse.bass as bass
import concourse.mybir as mybir
from concourse.bass_utils import run_bass_kernel_spmd

F32 = mybir.dt.float32
BF16 = mybir.dt.bfloat16
AF = mybir.ActivationFunctionType
ALU = mybir.AluOpType
AX = mybir.AxisListType

L = 8192
N2 = 16384
D = 1024
PI2 = 2.0 * np.pi
NCORES = 8
NSEQ = 2
RMS_EPS = 1e-6


class Tok:
    __slots__ = ("name", "w", "r", "multi", "wl", "prev_r")

    def __init__(self, name="", multi=False):
        self.name = name
        self.w = None
        self.r = []
        self.multi = multi
        self.wl = []
        self.prev_r = []


def MT(name=""):
    return Tok(name, multi=True)


class Op:
    __slots__ = ("eng", "fn", "deps", "marked", "val", "sem", "is_dma", "last")

    def __init__(self, eng, fn, is_dma=False):
        self.eng = eng
        self.fn = fn
        self.deps = []
        self.marked = False
        self.val = None
        self.sem = None
        self.is_dma = is_dma
        self.last = True


class Prog:
    ENGS = ("pe", "act", "dve", "pool", "sp")

    def __init__(self, nc):
        self.nc = nc
        self.ops = {e: [] for e in self.ENGS}
        self.dma_streams = {}
        self.last_op = {e: None for e in self.ENGS}
        self.pending_bar = {e: None for e in self.ENGS}
        self.open_group = {}

    def barrier(self, skip=()):
        deps = [o for o in self.last_op.values() if o is not None]
        for name, lst in self.dma_streams.items():
            if lst and name not in skip:
                deps.append(lst[-1])
        for e in self.ENGS:
            old = self.pending_bar[e] or []
            self.pending_bar[e] = old + deps

    def op(self, eng, fn, reads=(), writes=(), dma=None, last=True, after=()):
        o = Op(eng, fn, is_dma=dma is not None)
        raw = set()
        deps = []
        for t in after:
            if t.multi:
                deps.extend(t.wl)
                deps.extend(t.prev_r)
            elif t.w is not None:
                deps.append(t.w)
            deps.extend(t.r)
        for t in reads:
            if t.multi:
                deps.extend(t.wl)
            elif t.w is not None:
                deps.append(t.w)
        for t in writes:
            if t.multi:
                assert not any(t is q for q in reads), "op reads+writes a multi token: " + t.name
                if t.r:
                    t.prev_r = t.r
                    t.r = []
                    t.wl = []
                deps.extend(t.prev_r)
            else:
                if t.w is not None:
                    deps.append(t.w)
                deps.extend(t.r)
        if self.pending_bar[eng] is not None:
            for d in self.pending_bar[eng]:
                deps.append(d)
                raw.add(id(d))
            self.pending_bar[eng] = None
        seen = set()
        opengrp = self.open_group.get(dma, ()) if dma is not None else ()
        for d in deps:
            if d is o or id(d) in seen:
                continue
            seen.add(id(d))
            if any(d is g for g in opengrp):
                continue
            if not (d.is_dma or o.is_dma) and d.eng == eng:
                if eng == "pe":
                    continue
            o.deps.append(d)
        for t in reads:
            t.r.append(o)
        for t in writes:
            if t.multi:
                t.wl.append(o)
            else:
                t.w = o
                t.r = []
        self.ops[eng].append(o)
        if dma is not None:
            self.dma_streams.setdefault(dma, []).append(o)
            o.sem = dma
            o.last = last
            if last:
                self.open_group[dma] = []
            else:
                self.open_group.setdefault(dma, []).append(o)
        else:
            self.last_op[eng] = o
        return o

    def emit(self, final_streams=()):
        nc = self.nc
        for e in self.ENGS:
            for o in self.ops[e]:
                for d in o.deps:
                    d.marked = True
        with contextlib.ExitStack() as st:
            esem = {e: st.enter_context(nc.semaphore("s_" + e)) for e in self.ENGS}
            dsem = {n: st.enter_context(nc.semaphore("d_" + n)) for n in self.dma_streams}
            for e in self.ENGS:
                c = 0
                for o in self.ops[e]:
                    if o.is_dma:
                        continue
                    if o.marked:
                        c += 1
                        o.val = c
                        o.sem = esem[e]
            for name, lst in self.dma_streams.items():
                c = 0
                pend = []
                for o in lst:
                    c += 16
                    pend.append(o)
                    o.sem = dsem[name]
                    o.marked = True
                    if o.last:
                        for q in pend:
                            q.val = c
                        pend = []
                assert not pend, name
            engobj = {"pe": "tensor", "act": "scalar", "dve": "vector", "pool": "gpsimd", "sp": "sync"}
            block = st.enter_context(nc.Block())
            self.n_waits = {e: 0 for e in self.ENGS}

            def make(e):
                def body(eng):
                    known = {}
                    for o in self.ops[e]:
                        need = {}
                        for d in o.deps:
                            k = id(d.sem)
                            if known.get(k, 0) >= d.val:
                                continue
                            if k not in need or need[k][1] < d.val:
                                need[k] = (d.sem, d.val)
                        for k, (sem, val) in need.items():
                            eng.wait_ge(sem, val)
                            known[k] = val
                            self.n_waits[e] += 1
                        ins = o.fn(eng)
                        if o.is_dma:
                            ins.then_inc(o.sem, 16)
                        elif o.marked:
                            ins.then_inc(o.sem, 1)
                    if e == "sp":
                        for name in final_streams:
                            lst = self.dma_streams.get(name)
                            if lst:
                                eng.wait_ge(dsem[name], lst[-1].val)
                return body

            for e in self.ENGS:
                if not self.ops[e] and e != "sp":
                    continue
                getattr(block, engobj[e])(make(e))


def _bperm():
    p = np.arange(128)
    return 2 * (p % 64) + p // 64


def _tables():
    bf = lambda x: np.ascontiguousarray(np.asarray(x, np.float32).astype(ml_dtypes.bfloat16))
    t = {}
    t["t_ident"] = bf(np.eye(128))
    def F1(na):
        a = np.arange(na)[:, None].astype(np.float64)
        k1 = np.arange(64)[None, :].astype(np.float64)
        th = PI2 * a * (k1 + 0.5) / 128.0
        return np.concatenate([np.cos(th), -np.sin(th)], axis=1)
    f1 = F1(64)
    t["t_F1d"] = bf(np.concatenate([f1, f1], 0))
    t["t_F1k"] = bf(F1(128))
    b = _bperm()[:, None].astype(np.float64)
    k2 = np.arange(128)[None, :].astype(np.float64)
    M = np.zeros((128, 64, 3, 128))
    for k1 in range(64):
        th = PI2 * (b * (k1 + 0.5) / N2 + b * k2 / 128.0)
        M[:, k1, 0] = np.cos(th)
        M[:, k1, 1] = -np.sin(th)
        M[:, k1, 2] = np.sin(th)
    t["t_M"] = bf(M)
    k2c = np.arange(128)[:, None].astype(np.float64)
    bb = np.arange(128)[None, :].astype(np.float64)
    th = PI2 * bb * k2c / 128.0
    Gr, Gi = np.cos(th), np.sin(th)
    G = np.zeros((128, 2, 2, 64))
    for hb in range(2):
        G[:, hb, 0] = Gr[:, 64 * hb:64 * hb + 64]
        G[:, hb, 1] = Gi[:, 64 * hb:64 * hb + 64]
    t["t_G"] = bf(G.reshape(128, 256))
    k1 = np.arange(64)[:, None].astype(np.float64)
    a = np.arange(64)[None, :].astype(np.float64)
    A = np.zeros((128, 128, 2, 64))
    for bi in range(128):
        ph = PI2 * (k1 + 0.5) * (128.0 * a + bi) / N2
        Br, Bi = np.cos(ph), np.sin(ph)
        A[:64, bi, 0] = Br
        A[64:, bi, 0] = -Bi
        A[:64, bi, 1] = -Bi
        A[64:, bi, 1] = -Br
    t["t_A"] = bf(A * (2.0 / N2))
    aa = np.arange(64)[:, None].astype(np.float64)
    kk = np.arange(64)[None, :].astype(np.float64)
    th = PI2 * aa * kk / 64.0
    Fr, Fi = np.cos(th), -np.sin(th)
    ff = np.stack([np.concatenate([Fr, Fi], 1), np.concatenate([Fi, -Fr], 1)], 1)
    t["t_FF"] = bf(np.concatenate([ff, ff], 0))
    jj = np.arange(128)
    k2o = (2 * (jj % 64) + jj // 64)[None, :].astype(np.float64)
    FM = np.zeros((128, 64, 2, 128))
    for k1i in range(64):
        th = PI2 * (b * k1i / L + b * k2o / 128.0)
        FM[:, k1i, 0] = np.cos(th)
        FM[:, k1i, 1] = np.sin(th)
    t["t_FM"] = bf(FM / np.sqrt(L))
    c = np.arange(256)[:, None].astype(np.float64)
    m = np.arange(256)[None, :].astype(np.float64)
    th = PI2 * c * m / 256.0
    CS = np.stack([np.cos(th), np.sin(th)], 1) / 16.0
    t["t_CS"] = bf(CS.reshape(2, 128, 2, 256).transpose(1, 0, 2, 3))
    bj = _bperm()[:, None]
    an = np.arange(128)[None, :]
    n = (128 * an + bj).reshape(-1)
    pos = np.where(n < L, n, 2 * L - 1 - n).astype(np.float32)
    pos = np.clip(pos, 0, L - 1)
    tt = pos / np.float32(L - 1)
    bands = np.linspace(1e-4, 15, 16, dtype=np.float32)
    ang = (np.float32(PI2) * pos / np.float32(L))[:, None] * bands[None]
    z = np.concatenate([tt[:, None], np.cos(ang), -np.sin(ang)], axis=-1)
    t["t_zf"] = np.ascontiguousarray(z.T.astype(np.float32))
    t["t_tneg"] = np.ascontiguousarray((-tt.reshape(128, 128).T).astype(np.float32))
    sg = np.ones((128, 1), np.float32)
    sg[64:] = -1.0
    t["t_sgn"] = sg
    return t


TABLE_SHAPES = {
    "t_ident": ([128, 128], BF16), "t_F1d": ([128, 128], BF16), "t_F1k": ([128, 128], BF16),
    "t_M": ([128, 64, 3, 128], BF16), "t_G": ([128, 256], BF16), "t_A": ([128, 128, 2, 64], BF16),
    "t_FF": ([128, 2, 128], BF16), "t_FM": ([128, 64, 2, 128], BF16), "t_CS": ([128, 2, 2, 256], BF16),
    "t_zf": ([33, 16384], F32), "t_tneg": ([128, 128], F32), "t_sgn": ([128, 1], F32),
}

WEIGHT_SHAPES = {
    "w_in": [D, 8192], "g_pre": [1, D], "g_post": [1, D], "ws_t": [128, 72],
    "fw1": [33, 64], "fb1": [64, 1], "ff1": [64, 1], "fw2": [64, 64], "fb2": [64, 1], "ff2": [64, 1],
    "fw3": [64, 4096], "fdecay": [4, D], "hd": [2, D], "wf": [D, D], "wh": [D, D], "wo": [D, D],
    "bm_t": [128, 16],
}

ARENA_BYTES = 180 * 1024


class K:
    def __init__(self, dbg=None):
        self.dbg = dbg or {}
        self.nc = bass.Bass("TRN2", target_bir_lowering=False)
        self.st = contextlib.ExitStack()
        self.P = Prog(self.nc)
        self.dbg_outs = []
        self.rr = 0
        self.bank_i = 0

    def dram_in(self, name, shape, dt=F32):
        return self.nc.dram_tensor(name, list(shape), dt, kind="ExternalInput").ap()

    def dram_out(self, name, shape, dt=F32):
        return self.nc.dram_tensor(name, list(shape), dt, kind="ExternalOutput").ap()

    def sb(self, name, shape, dt):
        return self.st.enter_context(self.nc.sbuf_tensor(name, list(shape), dt))

    def av(self, off_bytes, shape, dt=BF16):
        n = int(np.prod(shape[1:]))
        el = 2 if dt == BF16 else 4
        assert off_bytes % 4 == 0
        assert off_bytes + n * el <= ARENA_BYTES, (off_bytes, shape)
        a = self.arena[0:shape[0], off_bytes // 2: off_bytes // 2 + n * el // 2]
        if dt != BF16:
            a = a.bitcast(dt)
        if len(shape) == 3:
            a = a.rearrange("p (a b) -> p a b", a=shape[1])
        elif len(shape) == 4:
            a = a.rearrange("p (a b c) -> p a b c", a=shape[1], b=shape[2])
        elif len(shape) == 5:
            a = a.rearrange("p (a b c d) -> p a b c d", a=shape[1], b=shape[2], c=shape[3])
        return a

    def bank(self, n=1):
        i = self.bank_i
        if i % n:
            i += n - i % n
        if i + n > 8:
            i = 0
        self.bank_i = (i + n) % 8
        return self.psum[:, 512 * i: 512 * (i + n)], self.bank_tok[i:i + n]

    def ev(self):
        self.rr += 1
        return "act" if self.rr % 2 else "dve"

    def op(self, *a, **k):
        return self.P.op(*a, **k)

    def copy(self, eng, out, in_, reads, writes, after=()):
        if eng == "act":
            return self.op("act", lambda e: e.activation(out=out, in_=in_, func=AF.Copy), reads=reads, writes=writes,
                           after=after)
        if eng == "dve":
            return self.op("dve", lambda e: e.tensor_copy(out=out, in_=in_), reads=reads, writes=writes, after=after)
        return self.op("pool", lambda e: e.tensor_copy(out=out, in_=in_), reads=reads, writes=writes, after=after)

    def dma(self, out, in_, reads, writes, stream, last=True, eng="sp"):
        return self.op(eng, lambda e: e.dma_start(out=out, in_=in_), reads=reads, writes=writes, dma=stream, last=last)

    def dump(self, name, ap, shape, dt, reads):
        o = self.dram_out("dbg_" + name, shape, dt)
        t = Tok()
        self.dma(o, ap, reads, [t], "dbg_" + name)
        self.dbg_outs.append("dbg_" + name)

    def build(self):
        nc = self.nc
        d = self.dbg
        self.xs = self.dram_in("xs", [NSEQ, L, D])
        self.W = {k: self.dram_in(k, s) for k, s in WEIGHT_SHAPES.items()}
        self.T = {k: self.dram_in(k, s, dt) for k, (s, dt) in TABLE_SHAPES.items()}
        self.y = self.dram_out("y", [NSEQ, L, D])
        self.s_xnT = self.dram_out("s_xnT", [16, 128, 4096], BF16)
        self.s_u = self.dram_out("s_u", [2, 8, 128, L], BF16)
        self.s_ka = self.dram_out("s_ka", [2, 8, 128, 16384], BF16)
        self.s_wbf = self.dram_out("s_wbf", [7, 8, 128, 1024], BF16)
        self.s_wt = self.dram_out("s_wt", [128, 8, 5120], BF16)
        self.s_rn = self.dram_out("s_rn", [2, 8, 128, 1], F32)
        self.arena = self.sb("arena", [128, ARENA_BYTES // 2], BF16)
        self.psum = self.st.enter_context(nc.psum_tensor("psum", [128, 4096], F32))
        self.bank_tok = [Tok("bank%d" % i) for i in range(8)]
        self.tok_uF, self.tok_uH = Tok("uF"), Tok("uH")
        self.ident = self.sb("ident", [128, 128], BF16)
        self.F1d = self.sb("F1d", [128, 128], BF16)
        self.F1k = self.sb("F1k", [128, 128], BF16)
        self.G = self.sb("G", [128, 256], BF16)
        self.FF = self.sb("FF", [128, 2, 128], BF16)
        self.gpre = self.sb("gpre", [128, D], F32)
        self.gpost = self.sb("gpost", [128, D], F32)
        self.bm = self.sb("bm", [128, 16], F32)
        self.ws = self.sb("ws", [128, 72], F32)
        self.sgn = self.sb("sgn", [128, 1], F32)
        self.ones = self.sb("ones", [128, 128], F32)
        self.negpi = self.sb("negpi", [128, 1], F32)
        self.epsb = self.sb("epsb", [128, 1], F32)
        self.onesb = self.sb("onesb", [1, 128], BF16)
        self.ctok = Tok("consts")
        cl = [("t_ident", self.ident), ("t_F1d", self.F1d), ("t_F1k", self.F1k), ("t_G", self.G),
              ("t_FF", self.FF), ("t_sgn", self.sgn)]
        for i, (n, tgt) in enumerate(cl):
            self.dma(tgt[:], self.T[n], [], [self.ctok], "const", last=False)
        self.dma(self.bm[:], self.W["bm_t"], [], [self.ctok], "const", last=False)
        self.dma(self.ws[:], self.W["ws_t"], [], [self.ctok], "const", last=False)
        self.dma(self.gpre[:], self.W["g_pre"].broadcast_to([128, D]), [], [self.ctok], "const", last=False)
        self.dma(self.gpost[:], self.W["g_post"].broadcast_to([128, D]), [], [self.ctok], "const", last=True)
        self.op("pool", lambda e: e.memset(self.ones[:], 1.0), writes=[self.ctok])
        self.op("pool", lambda e: e.memset(self.negpi[:], float(-np.pi)), writes=[self.ctok])
        self.op("pool", lambda e: e.memset(self.epsb[:], float(RMS_EPS)), writes=[self.ctok])
        self.op("pool", lambda e: e.memset(self.onesb[:], 1.0), writes=[self.ctok])
        self.P.barrier()

        if d.get("prologue_w", True):
            self.prologue_w()
            self.P.barrier()
        if d.get("prologue_k", True):
            self.prologue_k()
            self.P.barrier()
        for s in range(d.get("nseq", NSEQ)):
            if d.get("phase0", True):
                self.phase0(s)
                self.P.barrier()
            for cb in d.get("cbs", range(8)):
                if d.get("phaseF", True):
                    self.phaseF(s, cb)
                    self.P.barrier(skip=("u_stF", "u_stH"))
                if d.get("phaseH", True):
                    self.phaseH(s, cb)
                    self.P.barrier(skip=("u_stF", "u_stH"))
            self.P.barrier()
            if d.get("tail", True):
                self.tail(s)
                self.P.barrier()
        finals = [n for n in self.P.dma_streams]
        self.P.emit(final_streams=finals)
        self.st.close()
        return nc

    def prologue_w(self):
        win = self.W["w_in"].rearrange("(kc p) n -> p kc n", p=128)
        wsts = [self.av(0, [128, 8, 1024], F32), self.av(64 * 1024, [128, 8, 1024], F32)]
        wbss = [self.av(32 * 1024, [128, 8, 1024], BF16), self.av(96 * 1024, [128, 8, 1024], BF16)]
        t_wsts, t_wbss = [Tok(), Tok()], [Tok(), Tok()]
        col0 = {2: 1024, 3: 2048, 4: 3072, 5: 4096, 6: 5120}
        pi = 0
        for slot in range(2, 7):
            wst, wbs, t_wst, t_wbs = wsts[pi % 2], wbss[pi % 2], t_wsts[pi % 2], t_wbss[pi % 2]
            self.dma(wst, win[:, :, col0[slot]:col0[slot] + 1024], [], [t_wst], "pw_ld%d" % (pi % 2))
            self.copy("act" if pi % 2 else "dve", wbs, wst, [t_wst], [t_wbs])
            for cb in range(8):
                self.dma(self.s_wbf[slot, cb].rearrange("p (kc c) -> p kc c", kc=8),
                         wbs[:, :, 128 * cb:128 * cb + 128], [t_wbs], [Tok()], "pw_st%d" % (pi % 2), last=(cb == 7))
            pi += 1
        srcs = [win[:, :, 6144:7168], win[:, :, 7168:8192],
                self.W["wf"].rearrange("(kc p) n -> p kc n", p=128),
                self.W["wh"].rearrange("(kc p) n -> p kc n", p=128),
                self.W["wo"].rearrange("(kc p) n -> p kc n", p=128)]
        for i, src in enumerate(srcs):
            wst, wbs, t_wst, t_wbs = wsts[pi % 2], wbss[pi % 2], t_wsts[pi % 2], t_wbss[pi % 2]
            self.dma(wst, src, [], [t_wst], "pw_ld%d" % (pi % 2))
            self.copy("act" if pi % 2 else "dve", wbs, wst, [t_wst], [t_wbs])
            self.dma(self.s_wt[:, :, 1024 * i:1024 * i + 1024], wbs, [t_wbs], [Tok()], "pw_st%d" % (pi % 2))
            pi += 1
        wst, wbs, t_wst, t_wbs = wsts[0], wbss[0], t_wsts[0], t_wbss[0]
        cs = self.av(48 * 1024, [128, 2, 2, 256], BF16)
        t_cs = Tok()
        self.dma(cs, self.T["t_CS"], [], [t_cs], "pw_cs")
        wfT = self.av(52 * 1024, [128, 2, 1024], BF16)
        abst = self.av(56 * 1024, [128, 8, 2, 256], BF16)
        t_wfT, t_ab = Tok(), Tok()
        for g in range(4):
            self.dma(wst[:, :, 0:256], win[:, :, 256 * g:256 * g + 256], [], [t_wst], "pw_ld0")
            self.copy("dve", wbs[:, :, 0:256], wst[:, :, 0:256], [t_wst], [t_wbs])
            for half in range(2):
                for kq in range(2):
                    ps, pt = self.bank()
                    for k4 in range(4):
                        kc = 4 * kq + k4
                        self.op("pe", lambda e, ps=ps, kc=kc, half=half, k4=k4: e.matmul(
                            ps[:, 128 * k4:128 * k4 + 128], lhsT=wbs[:, kc, 128 * half:128 * half + 128],
                            rhs=self.ident[:], start=True, stop=True), reads=[t_wbs, self.ctok], writes=pt)
                    self.copy(self.ev(), wfT[:, half, 512 * kq:512 * kq + 512], ps, pt, [t_wfT])
            for kc in range(8):
                ps, pt = self.bank()
                for ab in range(2):
                    for half in range(2):
                        self.op("pe", lambda e, ps=ps, kc=kc, ab=ab, half=half: e.matmul(
                            ps[:, 256 * ab:256 * ab + 256], lhsT=wfT[:, half, 128 * kc:128 * kc + 128],
                            rhs=cs[:, half, ab, :], start=(half == 0), stop=(half == 1)),
                            reads=[t_wfT, t_cs], writes=pt)
                self.copy(self.ev(), abst[:, kc, :, :], ps.rearrange("p (a c) -> p a c", a=2), pt, [t_ab])
            for ab in range(2):
                for hc in range(2):
                    cb = 2 * g + hc
                    self.dma(self.s_wbf[ab, cb].rearrange("p (kc c) -> p kc c", kc=8),
                             abst[:, :, ab, 128 * hc:128 * hc + 128], [t_ab], [Tok()], "pw_st2",
                             last=(ab == 1 and hc == 1))

    def prologue_k(self):
        KB = 1024
        h2T = self.av(0, [64, 16384], BF16)
        w3b = self.av(32 * KB, [64, 4096], BF16)
        zfc = self.av(40 * KB, [33, 2048], F32)
        pre = self.av(48 * KB, [64, 2048], F32)
        h1 = self.av(56 * KB, [64, 2048], F32)
        w3s = self.av(64 * KB, [64, 4096], F32)
        rk = self.av(80 * KB, [64, 2048], F32)
        t_rk = Tok()
        sm = self.sb("pk_small", [64, 8 + 64 + 64], F32)
        t_sm, t_w3s, t_w3b, t_h2T = Tok(), Tok(), Tok(), Tok()
        self.dma(sm[:, 0:1], self.W["fb1"], [], [t_sm], "pk_c", last=False)
        self.dma(sm[:, 1:2], self.W["ff1"], [], [t_sm], "pk_c", last=False)
        self.dma(sm[:, 2:3], self.W["fb2"], [], [t_sm], "pk_c", last=False)
        self.dma(sm[:, 3:4], self.W["ff2"], [], [t_sm], "pk_c", last=False)
        self.dma(sm[0:33, 8:72], self.W["fw1"], [], [t_sm], "pk_c", last=False)
        self.dma(sm[:, 72:136], self.W["fw2"], [], [t_sm], "pk_c", last=False)
        self.dma(w3s, self.W["fw3"], [], [t_w3s], "pk_c", last=True)
        self.copy("dve", w3b, w3s, [t_w3s], [t_w3b])
        t_zf, t_pre, t_h1 = Tok(), Tok(), Tok()
        for ch in range(8):
            self.dma(zfc, self.T["t_zf"][:, 2048 * ch:2048 * ch + 2048], [], [t_zf], "pk_zf")
            for layer in range(2):
                for q in range(4):
                    ps, pt = self.bank()
                    if layer == 0:
                        self.op("pe", lambda e, ps=ps, q=q: e.matmul(
                            ps[0:64, :], lhsT=sm[0:33, 8:72], rhs=zfc[:, 512 * q:512 * q + 512],
                            start=True, stop=True), reads=[t_sm, t_zf], writes=pt)
                    else:
                        self.op("pe", lambda e, ps=ps, q=q: e.matmul(
                            ps[0:64, :], lhsT=sm[:, 72:136], rhs=h1[:, 512 * q:512 * q + 512],
                            start=True, stop=True), reads=[t_sm, t_h1], writes=pt)
                    bcol = 0 if layer == 0 else 2
                    self.op("dve", lambda e, ps=ps, q=q, bcol=bcol: e.tensor_scalar(
                        out=pre[:, 512 * q:512 * q + 512], in0=ps[0:64, :], scalar1=sm[:, bcol:bcol + 1],
                        scalar2=sm[:, bcol + 1:bcol + 2], op0=ALU.add, op1=ALU.mult),
                        reads=pt + [t_sm], writes=[t_pre])
                MAGIC = 12582912.0
                self.op("dve", lambda e: e.tensor_scalar(
                    out=rk[:, :], in0=pre[:, :], scalar1=float(1.0 / PI2), scalar2=MAGIC,
                    op0=ALU.mult, op1=ALU.add), reads=[t_pre], writes=[t_rk])
                self.op("dve", lambda e: e.tensor_scalar_add(out=rk[:, :], in0=rk[:, :], scalar1=-MAGIC),
                        reads=[t_rk], writes=[t_rk])
                self.op("dve", lambda e: e.scalar_tensor_tensor(
                    out=pre[:, :], in0=rk[:, :], scalar=float(-PI2), in1=pre[:, :], op0=ALU.mult, op1=ALU.add),
                    reads=[t_rk, t_pre], writes=[t_pre])
                if layer == 0:
                    self.op("act", lambda e: e.activation(out=h1[:, :], in_=pre[:, :], func=AF.Sin,
                                                          scale=1.0),
                            reads=[t_pre, self.ctok], writes=[t_h1])
                else:
                    self.op("act", lambda e, ch=ch: e.activation(
                        out=h2T[:, 2048 * ch:2048 * ch + 2048], in_=pre[:, :], func=AF.Sin,
                        scale=1.0), reads=[t_pre, self.ctok], writes=[t_h2T])
        if self.dbg.get("dump_h2T"):
            self.dump("h2T", h2T, [64, 16384], BF16, [t_h2T])
        ZT = self.av(40 * KB, [128, 128, 128], BF16)
        Es = [self.av(72 * KB, [128, 64, 128], F32), self.av(40 * KB, [128, 64, 128], F32)]
        t_Es = [Tok(), Tok()]
        hk = self.av(104 * KB, [128, 128, 128], BF16)
        KA = self.av(136 * KB, [128, 128, 2, 64], BF16)
        Ms = [self.av((168 + 6 * i) * KB, [128, 8, 3, 128], BF16) for i in range(2)]
        absd = self.sb("pk_absd", [128, 128], F32)
        rs = self.sb("pk_rs", [128, 128], F32)
        rn = self.sb("pk_rn", [128, 128], F32)
        dsk = self.sb("pk_dsk", [128, 128], F32)
        tneg = self.sb("pk_tneg", [128, 128], F32)
        t_dsk, t_tneg = Tok(), Tok()
        self.dma(tneg[:], self.T["t_tneg"], [], [t_tneg], "pk_c2")
        self.P.barrier()
        bp = _bperm()
        t_Ms = [Tok(), Tok()]
        mi = 0
        t_absd, t_E, t_hk, t_rs, t_rn, t_ZT, t_KA = Tok(), Tok(), MT("hk"), Tok(), Tok(), MT("ZTk"), MT("KA")
        t_KA2 = Tok()
        dSb = self.sb("pk_dSb", [1, 128], F32)
        t_hk2 = Tok()
        t_dS = Tok()
        iters = [(o, cb) for o in range(2) for cb in self.dbg.get("kcbs", range(8))]
        absds = [absd, self.sb("pk_absd2", [128, 128], F32)]
        dsks = [dsk, self.sb("pk_dsk2", [128, 128], F32)]
        t_absds = [Tok(), Tok()]
        t_dsks = [Tok(), Tok()]

        def small_loads(n):
            o, cb = iters[n]
            ab, tab = absds[n % 2], t_absds[n % 2]
            self.dma(dsks[n % 2][:], self.W["hd"][o:o + 1, 128 * cb:128 * cb + 128].broadcast_to([128, 128]), [],
                     [t_dsks[n % 2]], "pk_dk%d" % (n % 2))
            for dr in range(2):
                src = self.W["fdecay"][2 * dr + o:2 * dr + o + 1, 128 * cb:128 * cb + 128]
                self.dma(ab[64 * dr:64 * dr + 64, :], src.broadcast_to([64, 128]), [], [tab], "pk_ad%d" % (n % 2),
                         last=(dr == 1))
            self.op("act", lambda e: e.activation(out=ab[:], in_=ab[:], func=AF.Abs), reads=[tab], writes=[tab])

        def ecomp(n, chh):
            E, t_E = Es[chh], t_Es[chh]
            ab, tab = absds[n % 2], t_absds[n % 2]
            aft = [t_ZT] if chh == 1 else []
            self.op("pool", lambda e: e.tensor_tensor(
                out=E[:, :, :], in0=ab[:, 64 * chh:64 * chh + 64].unsqueeze(2).broadcast_to([128, 64, 128]),
                in1=tneg[:].unsqueeze(1).broadcast_to([128, 64, 128]), op=ALU.mult),
                reads=[tab, t_tneg], writes=[t_E], after=aft)
            self.op("act", lambda e: e.activation(out=E[:, :, :], in_=E[:, :, :], func=AF.Exp),
                    reads=[t_E], writes=[t_E])
            self.op("pool", lambda e: e.memset(E[64:65, :, 0:1], 0.0), reads=[t_E], writes=[t_E])

        def h3_hk(n, chh):
            o, cb = iters[n]
            E, t_E = Es[chh], t_Es[chh]
            for jq in range(16):
                ps, pt = self.bank()
                for j8 in range(8):
                    j = 8 * jq + j8
                    for dr in range(2):
                        col = dr * 2048 + o * 1024 + cb * 128 + 64 * chh
                        self.op("pe", lambda e, ps=ps, j8=j8, dr=dr, col=col, j=j: e.matmul(
                            ps[64 * dr:64 * dr + 64, 64 * j8:64 * j8 + 64],
                            lhsT=h2T[:, 128 * j + 64 * dr:128 * j + 64 * dr + 64],
                            rhs=w3b[:, col:col + 64], start=True, stop=True),
                            reads=[t_h2T, t_w3b], writes=pt)
                self.op("dve", lambda e, ps=ps, jq=jq: e.scalar_tensor_tensor(
                    out=hk[:, 64 * chh:64 * chh + 64, 8 * jq:8 * jq + 8],
                    in0=ps.rearrange("p (j c) -> p c j", j=8), scalar=self.sgn[:, 0:1],
                    in1=E[:, :, 8 * jq:8 * jq + 8], op0=ALU.mult, op1=ALU.mult),
                    reads=pt + [t_E, self.ctok], writes=[t_hk], after=[t_hk2])

        def norm(n):
            o, cb = iters[n]
            self.op("dve", lambda e: e.tensor_reduce(out=rs[:], in_=hk[:, :, :], axis=AX.X, op=ALU.add,
                                                     apply_absolute_value=True),
                    reads=[t_hk], writes=[t_rs])
            ps, pt = self.bank()
            self.op("pe", lambda e, ps=ps: e.matmul(ps[0:1, 0:128], lhsT=self.ones[:, 0:1], rhs=rs[:], start=True, stop=True),
                    reads=[t_rs, self.ctok], writes=pt)
            self.op("pe", lambda e, ps=ps: e.matmul(ps[:, 128:129], lhsT=rs[:], rhs=self.ones[:, 0:1], start=True, stop=True),
                    reads=[t_rs, self.ctok], writes=pt)
            dk, tdk = dsks[n % 2], t_dsks[n % 2]
            self.op("dve", lambda e, ps=ps: e.tensor_tensor(out=dSb[0:1, :], in0=ps[0:1, 0:128], in1=dk[0:1, :], op=ALU.mult),
                    reads=pt + [tdk], writes=[t_dS])
            self.op("dve", lambda e: e.tensor_tensor(out=hk[0:1, :, 0], in0=hk[0:1, :, 0], in1=dSb[0:1, :], op=ALU.add),
                    reads=[t_hk, t_dS], writes=[t_hk2])
            self.op("dve", lambda e, ps=ps: e.reciprocal(out=rn[:, 0:1], in_=ps[:, 128:129]), reads=pt, writes=[t_rn])
            self.dma(self.s_rn[o, cb], rn[:, 0:1], [t_rn], [Tok()], "rn_st")
            if self.dbg.get("dump_hk") and o == 0:
                self.dump("hk%d" % cb, hk, [128, 128, 128], BF16, [t_hk])

        def fft1(n):
            for c4 in range(32):
                ps, pt = self.bank()
                for ci in range(4):
                    c = 4 * c4 + ci
                    self.op("pe", lambda e, ps=ps, ci=ci, c=c: e.matmul(
                        ps[:, 128 * ci:128 * ci + 128], lhsT=hk[:, c, :], rhs=self.F1k[:],
                        start=True, stop=True), reads=[t_hk, t_hk2, self.ctok], writes=pt)
                self.copy(self.ev(), ZT[:, :, 4 * c4:4 * c4 + 4], ps.rearrange("p (c k) -> p k c", c=4), pt, [t_ZT],
                          after=[t_Es[1]])

        mi_box = [0]

        def fft2(n):
            o, cb = iters[n]
            for k1q in range(8):
                mi = mi_box[0]
                Mb = Ms[mi % 2]
                tM = t_Ms[mi % 2]
                self.dma(Mb, self.T["t_M"][:, 8 * k1q:8 * k1q + 8], [], [tM], "Ms%d" % (mi % 2))
                mi_box[0] += 1
                for kp in range(4):
                    ps, pt = self.bank()
                    for ki in range(2):
                        k8 = 2 * kp + ki
                        k1 = 8 * k1q + k8
                        base = 256 * ki
                        mm = [(0, k1, 0, True, False), (2, 64 + k1, 0, False, True),
                              (1, k1, 128, True, False), (0, 64 + k1, 128, False, True)]
                        for mi_, (mt, zrow, off, sta, sto) in enumerate(mm):
                            self.op("pe", lambda e, ps=ps, Mb=Mb, k8=k8, mt=mt, zrow=zrow, off=off, sta=sta,
                                    sto=sto, base=base: e.matmul(
                                ps[:, base + off:base + off + 128], lhsT=Mb[:, k8, mt, :], rhs=ZT[:, zrow, :],
                                start=sta, stop=sto), reads=[tM, t_ZT], writes=pt)
                    k1a = 8 * k1q + 2 * kp
                    self.copy(self.ev(), KA[:, :, :, k1a:k1a + 2], ps.rearrange("p (k h c) -> p c h k", k=2, h=2), pt,
                              [t_KA])
            self.dma(self.s_ka[o, cb], KA.rearrange("p c h k -> p (c h k)"), [t_KA], [Tok()], "ka_st")

        NI = len(iters)
        small_loads(0)
        ecomp(0, 0)
        for n in range(NI):
            if n + 1 < NI:
                small_loads(n + 1)
            ecomp(n, 1)
            h3_hk(n, 0)
            h3_hk(n, 1)
            norm(n)
            if n + 1 < NI:
                ecomp(n + 1, 0)
            fft1(n)
            fft2(n)

    def phase0(self, s):
        KB = 1024
        xts = [self.av(16 * KB * i, [128, 4, D], F32) for i in range(2)]
        xn = self.av(32 * KB, [128, 4, D], BF16)
        xnTs = [self.av((40 + 8 * i) * KB, [128, 8, 512], BF16) for i in range(2)]
        sqj = self.av(56 * KB, [128, D], BF16)
        st = self.sb("p0_st%d" % s, [128, 16 * 4 * 2], F32)
        t_x = [Tok(), Tok()]
        t_xn, t_sq = MT("xn"), Tok()
        t_xnT = [MT("xnT0"), MT("xnT1")]
        xsrc = self.xs[s].rearrange("(a b) d -> b a d", b=128)
        for q in range(16):
            xt = xts[q % 2]
            tx = t_x[q % 2]

            def xload(q2):
                for i in range(4):
                    for e_ in range(2):
                        b = 2 * (4 * q2 + i) + e_
                        self.dma(xts[q2 % 2][64 * e_:64 * e_ + 64, i, :], xsrc[b], [], [t_x[q2 % 2]],
                                 "p0_x%d" % (q2 % 2), last=(i == 3 and e_ == 1))
            if q == 0:
                xload(0)
            if q + 1 < 16:
                xload(q + 1)
            lvl = self.dbg.get("p0_lvl", 9)
            for i in range(4):
                c0 = 8 * q + 2 * i
                t_st = Tok()
                self.op("act", lambda e, xt=xt, i=i, c0=c0: e.activation(
                    out=sqj[:, :], in_=xt[:, i, :], func=AF.Square, accum_out=st[:, c0:c0 + 1]),
                    reads=[tx], writes=[t_sq, t_st])
                if lvl < 2:
                    continue
                self.op("act", lambda e, c0=c0: e.activation(
                    out=st[:, c0 + 1:c0 + 2], in_=st[:, c0:c0 + 1], func=AF.Sqrt, scale=1.0 / D, bias=self.epsb[:]),
                    reads=[t_st, self.ctok], writes=[t_st])
                self.op("dve", lambda e, c0=c0: e.reciprocal(out=st[:, c0 + 1:c0 + 2], in_=st[:, c0 + 1:c0 + 2]),
                        reads=[t_st], writes=[t_st])
                if lvl < 3:
                    continue
                self.op("dve", lambda e, xt=xt, i=i, c0=c0: e.scalar_tensor_tensor(
                    out=xn[:, i, :], in0=xt[:, i, :], scalar=st[:, c0 + 1:c0 + 2], in1=self.gpre[:],
                    op0=ALU.mult, op1=ALU.mult), reads=[tx, t_st, self.ctok], writes=[t_xn])
            xnT = xnTs[q % 2]
            txT = t_xnT[q % 2]
            if lvl < 4:
                continue
            for kq in range(2):
                pss = [self.bank() for _ in range(4)]
                for k4 in range(4):
                    kc = 4 * kq + k4
                    ps, pt = pss[k4]
                    for i in range(4):
                        self.op("pe", lambda e, ps=ps, i=i, kc=kc: e.matmul(
                            ps[:, 128 * i:128 * i + 128], lhsT=xn[:, i, 128 * kc:128 * kc + 128], rhs=self.ident[:],
                            start=True, stop=True), reads=[t_xn, self.ctok], writes=pt)
                    self.copy(self.ev(), xnT[:, kc, :], ps, pt, [txT])
            self.dma(self.s_xnT[q], xnT.rearrange("p k t -> p (k t)"), [txT], [Tok()], "p0_st%d" % (q % 2))

    def load_wblk(self, cb, slots, wblk, t_w):
        for i, sl in enumerate(slots):
            self.dma(wblk[:, :, i, :], self.s_wbf[sl, cb].rearrange("p (kc c) -> p kc c", kc=8), [], [t_w], "wblk",
                     last=(i == len(slots) - 1))

    def phaseF(self, s, cb):
        KB = 1024
        wblk = self.av(0, [128, 8, 3, 128], BF16)
        xnTs = [self.av((14 + 8 * i) * KB, [128, 8, 512], BF16) for i in range(2)]
        UAB = self.av(30 * KB, [128, 2, 128, 64], BF16)
        sfg = self.av(64 * KB, [128, L], BF16)
        ZT = self.av(80 * KB, [128, 128, 128], BF16)
        FMs = [self.av((112 + 4 * i) * KB, [128, 8, 2, 128], BF16) for i in range(2)]
        t_w, t_U, t_sfg, t_ZT, t_uf = Tok(), MT("U"), MT("sfg"), MT("ZT"), MT("uf")
        t_xT = [Tok(), Tok()]
        t_FM = [Tok(), Tok()]
        self.load_wblk(cb, [0, 1, 2], wblk, t_w)
        for q in range(16):
            xnT, tx = xnTs[q % 2], t_xT[q % 2]
            self.dma(xnT.rearrange("p k t -> p (k t)"), self.s_xnT[q], [], [tx], "xnT%d" % (q % 2))
            for ip in range(2):
                ps, pt = self.bank()
                for i2 in range(2):
                    i = 2 * ip + i2
                    for kc in range(8):
                        self.op("pe", lambda e, ps=ps, i2=i2, i=i, kc=kc, xnT=xnT: e.matmul(
                            ps[:, 256 * i2:256 * i2 + 256], lhsT=xnT[:, kc, 128 * i:128 * i + 128],
                            rhs=wblk[:, kc, 0:2, :], start=(kc == 0), stop=(kc == 7)),
                            reads=[tx, t_w], writes=pt)
                bq = 4 * q + 2 * ip
                self.copy(self.ev(), UAB[:, :, :, bq:bq + 2],
                          ps.rearrange("p (t s c) -> p s c t", t=2, s=2), pt, [t_U])
            ps, pt = self.bank()
            for kc in range(8):
                self.op("pe", lambda e, ps=ps, kc=kc, xnT=xnT: e.matmul(
                    ps[:, :], lhsT=wblk[:, kc, 2, :], rhs=xnT[:, kc, :], start=(kc == 0), stop=(kc == 7)),
                    reads=[tx, t_w], writes=pt)
            self.op("act", lambda e, ps=ps, q=q: e.activation(out=sfg[:, 512 * q:512 * q + 512], in_=ps, func=AF.Silu),
                    reads=pt, writes=[t_sfg], after=[self.tok_uF])
        for c4 in range(32):
            ps, pt = self.bank()
            for ci in range(4):
                c = 4 * c4 + ci
                for e_ in range(2):
                    for s2 in range(2):
                        self.op("pe", lambda e, ps=ps, ci=ci, c=c, e_=e_, s2=s2: e.matmul(
                            ps[64 * e_:64 * e_ + 64, 128 * ci:128 * ci + 128],
                            lhsT=UAB[64 * e_:64 * e_ + 64, s2, c, :], rhs=self.FF[64 * e_:64 * e_ + 64, s2, :],
                            start=(s2 == 0), stop=(s2 == 1)), reads=[t_U, self.ctok], writes=pt)
            self.copy(self.ev(), ZT[:, :, 4 * c4:4 * c4 + 4], ps.rearrange("p (c k) -> p k c", c=4), pt, [t_ZT],
                      after=[self.tok_uH])
        if self.dbg.get("dump_F") and cb == self.dbg["dump_F"][1] and s == self.dbg["dump_F"][0]:
            self.dump("F_UAB", UAB.rearrange("p s c b -> p (s c b)"), [128, 2 * 128 * 64], BF16, [t_U])
            self.dump("F_ZT", ZT.rearrange("p k c -> p (k c)"), [128, 128 * 128], BF16, [t_ZT])
            self.dump("F_sfg", sfg, [128, L], BF16, [t_sfg])
        self.pre = None
        if self.dbg.get("phaseH", True) and self.dbg.get("preload", True):
            wblkH = self.av(0, [128, 8, 4, 128], BF16)
            xnT0H = self.av(16 * KB, [128, 8, 512], BF16)
            t_wH, t_x0H = Tok(), Tok()
            aft = [t_w, t_xT[0], t_xT[1]]
            for i_, sl_ in enumerate([3, 4, 5, 6]):
                self.op("sp", lambda e, i_=i_, sl_=sl_: e.dma_start(
                    out=wblkH[:, :, i_, :], in_=self.s_wbf[sl_, cb].rearrange("p (kc c) -> p kc c", kc=8)),
                    writes=[t_wH], dma="wblk", last=(i_ == 3), after=aft)
            self.op("sp", lambda e: e.dma_start(out=xnT0H.rearrange("p k t -> p (k t)"), in_=self.s_xnT[0]),
                    writes=[t_x0H], dma="xnT0", after=aft)
            self.pre = (s, cb, t_wH, t_x0H)
        sfgv = sfg.rearrange("p (h k a) -> p k h a", h=2, k=64)
        for k1q in range(8):
            FMb, tF = FMs[k1q % 2], t_FM[k1q % 2]
            self.dma(FMb, self.T["t_FM"][:, 8 * k1q:8 * k1q + 8], [], [tF], "FMs%d" % (k1q % 2))
            for kp in range(2):
                ps, pt = self.bank()
                for ki in range(4):
                    k8 = 4 * kp + ki
                    k1 = 8 * k1q + k8
                    for ri in range(2):
                        self.op("pe", lambda e, ps=ps, ki=ki, k8=k8, k1=k1, ri=ri, FMb=FMb: e.matmul(
                            ps[:, 128 * ki:128 * ki + 128], lhsT=ZT[:, 64 * ri + k1, :], rhs=FMb[:, k8, ri, :],
                            start=(ri == 0), stop=(ri == 1)), reads=[t_ZT, tF], writes=pt)
                k1a = 8 * k1q + 4 * kp
                self.op("dve", lambda e, ps=ps, k1a=k1a: e.tensor_tensor(
                    out=sfgv[:, k1a:k1a + 4], in0=ps.rearrange("p (k h a) -> p k h a", k=4, h=2),
                    in1=sfgv[:, k1a:k1a + 4], op=ALU.mult), reads=pt + [t_sfg], writes=[t_uf])
        self.dma(self.s_u[0, cb], sfg, [t_uf], [self.tok_uF], "u_stF")

    def phaseH(self, s, cb):
        KB = 1024
        wblk = self.av(0, [128, 8, 4, 128], BF16)
        xnTs = [self.av((16 + 8 * i) * KB, [128, 8, 512], BF16) for i in range(2)]
        Uv = self.av(32 * KB, [128, 128, 64], BF16)
        Ux1 = self.av(48 * KB, [128, 128, 64], BF16)
        x2c = self.av(64 * KB, [128, L], BF16)
        shg = self.av(80 * KB, [128, L], BF16)
        raw = [self.av((96 + 16 * i) * KB, [128, L], BF16) for i in range(3)]
        ctmp = self.av(144 * KB, [128, L], BF16)
        ZT = self.av(96 * KB, [128, 128, 128], BF16)
        O0 = ZT.rearrange("p k c -> p (k c)").rearrange("p (c n) -> p c n", c=128)
        Yb = self.av(128 * KB, [128, 128, 2, 64], BF16)
        O1 = Yb.rearrange("p c r k -> p c (r k)")
        Ms = [self.av(o_ * KB, [128, 8, 3, 128], BF16) for o_ in (0, 6, 64, 70)]
        As = [self.av((12 + 4 * i) * KB, [128, 16, 2, 64], BF16) for i in range(2)]
        KAs = [self.av(o_ * KB, [128, 16, 2, 64], BF16) for o_ in (20, 24, 76, 176)]
        P1s = [self.av((160 + 4 * i) * KB, [128, 16, 2, 64], BF16) for i in range(2)]
        P2s = [self.av((168 + 4 * i) * KB, [128, 16, 2, 64], BF16) for i in range(2)]
        t_w = Tok()
        t_xT = [Tok(), Tok()]
        t_rawc = [[Tok() for _ in range(16)] for _ in range(3)]
        t_shgc = [Tok() for _ in range(16)]
        t_dc = [[Tok() for _ in range(16)] for _ in range(3)]
        t_g2c = [Tok() for _ in range(16)]
        t_Uv, t_Ux1 = MT("Uv"), MT("Ux1")
        t_uh, t_Uz = MT("uh"), MT("Uz")
        x1c = self.av(160 * KB, [128, L], BF16)
        dsts = [ctmp, x1c, x2c]
        Us = [(Uv, t_Uv), (Ux1, t_Ux1)]
        hl = self.dbg.get("h_lvl", 99)
        rnc = self.sb("h_rnc_%d_%d" % (s, cb), [128, 2], F32)
        wsx = self.sb("h_wsx_%d_%d" % (s, cb), [128, 6], F32)
        t_rnc, t_wsx = Tok(), Tok()
        for o_ in range(2):
            self.dma(rnc[:, o_:o_ + 1], self.s_rn[o_, cb], [], [t_rnc], "h_rn", last=(o_ == 1))
        for o_ in range(2):
            c0_ = ((1 + o_) * 8 + cb) * 3
            self.op("dve", lambda e, o_=o_, c0_=c0_: e.tensor_scalar_mul(
                out=wsx[:, 3 * o_:3 * o_ + 3], in0=self.ws[:, c0_:c0_ + 3], scalar1=rnc[:, o_:o_ + 1]),
                reads=[t_rnc, self.ctok], writes=[t_wsx])

        def wsc(ti, tap):
            if ti == 0:
                return self.ws[:, (ti * 8 + cb) * 3 + tap:(ti * 8 + cb) * 3 + tap + 1]
            return wsx[:, 3 * (ti - 1) + tap:3 * (ti - 1) + tap + 1]

        def conv_chunk(r):
            lo, hi = 512 * r, 512 * r + 512
            for ti in range(3):
                rw, dst = raw[ti], dsts[ti]
                rd = [t_rawc[ti][q] for q in (r - 1, r, r + 1) if 0 <= q < 16] + [t_wsx]
                self.op("act", lambda e, rw=rw, dst=dst, ti=ti: e.activation(
                    out=dst[:, lo:hi], in_=rw[:, lo:hi], func=AF.Copy, scale=wsc(ti, 1)),
                    reads=rd + [self.ctok], writes=[t_dc[ti][r]], after=([self.tok_uF] if ti == 2 else []))
                a0 = max(lo, 64)
                self.op("dve", lambda e, rw=rw, dst=dst, ti=ti, a0=a0: e.scalar_tensor_tensor(
                    out=dst[:, a0:hi], in0=rw[:, a0 - 64:hi - 64], scalar=wsc(ti, 0), in1=dst[:, a0:hi],
                    op0=ALU.mult, op1=ALU.add), reads=rd + [t_dc[ti][r], self.ctok], writes=[t_dc[ti][r]])
                b1 = min(hi, L - 64)
                self.op("dve", lambda e, rw=rw, dst=dst, ti=ti, b1=b1: e.scalar_tensor_tensor(
                    out=dst[:, lo:b1], in0=rw[:, lo + 64:b1 + 64], scalar=wsc(ti, 2), in1=dst[:, lo:b1],
                    op0=ALU.mult, op1=ALU.add), reads=rd + [t_dc[ti][r], self.ctok], writes=[t_dc[ti][r]])

        def post_chunk(r):
            self.op("dve", lambda e: e.tensor_tensor(out=shg[:, 512 * r:512 * r + 512], in0=shg[:, 512 * r:512 * r + 512],
                                                     in1=x2c[:, 512 * r:512 * r + 512], op=ALU.mult),
                    reads=[t_dc[2][r], t_shgc[r]], writes=[t_g2c[r]])
            if hl < 3:
                return
            for ti, (U, tU) in enumerate(Us):
                ps, pt = self.bank()
                for bi in range(4):
                    bq = 4 * r + bi
                    self.op("pe", lambda e, ps=ps, bi=bi, bq=bq, src=dsts[ti]: e.matmul(
                        ps[:, 128 * bi:128 * bi + 128], lhsT=src[:, 128 * bq:128 * bq + 128], rhs=self.ident[:],
                        start=True, stop=True), reads=[t_dc[ti][r], self.ctok], writes=pt)
                self.copy(self.ev(), U[:, :, 4 * r:4 * r + 4], ps.rearrange("p (b c) -> p c b", b=4), pt, [tU])

        pre = getattr(self, "pre", None)
        pre_ok = bool(pre) and pre[0] == s and pre[1] == cb
        if pre_ok:
            t_w = pre[2]
            t_xT[0] = pre[3]
        else:
            self.load_wblk(cb, [3, 4, 5, 6], wblk, t_w)
        self.pre = None
        for q in range(16):
            xnT, tx = xnTs[q % 2], t_xT[q % 2]
            if not (pre_ok and q == 0):
                self.dma(xnT.rearrange("p k t -> p (k t)"), self.s_xnT[q], [], [tx], "xnT%d" % (q % 2))
            for sl in range(4):
                ps, pt = self.bank()
                for kc in range(8):
                    self.op("pe", lambda e, ps=ps, kc=kc, xnT=xnT, sl=sl: e.matmul(
                        ps[:, :], lhsT=wblk[:, kc, sl, :], rhs=xnT[:, kc, :], start=(kc == 0), stop=(kc == 7)),
                        reads=[tx, t_w], writes=pt)
                if sl < 3:
                    self.copy(self.ev(), raw[sl][:, 512 * q:512 * q + 512], ps, pt, [t_rawc[sl][q]])
                else:
                    self.op("act", lambda e, ps=ps, q=q: e.activation(out=shg[:, 512 * q:512 * q + 512], in_=ps,
                                                                       func=AF.Silu), reads=pt, writes=[t_shgc[q]],
                            after=[self.tok_uH])
            if hl >= 2 and q >= 1:
                conv_chunk(q - 1)
                if 1 <= q - 2 <= 14:
                    post_chunk(q - 2)
        if hl < 2:
            return
        conv_chunk(15)
        post_chunk(14)
        for ti in range(3):
            rw, dst = raw[ti], dsts[ti]
            self.op("dve", lambda e, rw=rw, dst=dst, ti=ti: e.scalar_tensor_tensor(
                out=dst[:, 1:64], in0=rw[:, L - 64:L - 1], scalar=wsc(ti, 0), in1=dst[:, 1:64],
                op0=ALU.mult, op1=ALU.add), reads=[t_rawc[ti][15], t_dc[ti][0], self.ctok, t_wsx], writes=[t_dc[ti][0]])
            self.op("dve", lambda e, rw=rw, dst=dst, ti=ti: e.scalar_tensor_tensor(
                out=dst[:, L - 64:L - 1], in0=rw[:, 1:64], scalar=wsc(ti, 2), in1=dst[:, L - 64:L - 1],
                op0=ALU.mult, op1=ALU.add), reads=[t_rawc[ti][0], t_dc[ti][15], self.ctok, t_wsx], writes=[t_dc[ti][15]])
        post_chunk(0)
        post_chunk(15)
        t_g2 = None
        dmp = self.dbg.get("dump_H") and cb == self.dbg["dump_H"][1] and s == self.dbg["dump_H"][0]
        if dmp:
            self.dump("H_Uv", Uv.rearrange("p c b -> p (c b)"), [128, L], BF16, [t_Uv])
            self.dump("H_Ux1", Ux1.rearrange("p c b -> p (c b)"), [128, L], BF16, [t_Ux1])
            self.dump("H_gate2", shg, [128, L], BF16, t_g2c)
        if hl < 4:
            return
        self.P.barrier()
        t_M = [Tok(), Tok(), Tok(), Tok()]
        t_A = [Tok(), Tok()]
        t_KA = [Tok(), Tok(), Tok(), Tok()]
        t_P = [MT("P0"), MT("P1")]
        mi = ai = 0
        t_ZT = MT("ZT")
        t_sub = [MT("Yb%d" % i) for i in range(8)]
        t_O1 = [MT("O1_%d" % i) for i in range(8)]
        t_O0 = [MT("O0_%d" % i) for i in range(8)]
        for conv in range(2):
            Uin, t_Uin = Uv, (t_Uv if conv == 0 else t_Uz)
            for c4 in range(32):
                ps, pt = self.bank()
                for ci in range(4):
                    c = 4 * c4 + ci
                    for e_ in range(2):
                        self.op("pe", lambda e, ps=ps, ci=ci, c=c, e_=e_: e.matmul(
                            ps[64 * e_:64 * e_ + 64, 128 * ci:128 * ci + 128], lhsT=Uin[64 * e_:64 * e_ + 64, c, :],
                            rhs=self.F1d[64 * e_:64 * e_ + 64, :], start=True, stop=True),
                            reads=[t_Uin, self.ctok], writes=pt)
                self.copy(self.ev(), ZT[:, :, 4 * c4:4 * c4 + 4], ps.rearrange("p (c k) -> p k c", c=4), pt,
                          [t_ZT], after=t_O0)
            if hl < 5:
                return
            for k1q in range(8):
                Mb, tM = Ms[mi % 4], t_M[mi % 4]
                self.dma(Mb, self.T["t_M"][:, 8 * k1q:8 * k1q + 8], [], [tM], "HMs%d" % (mi % 4))
                mi += 1
                for kp in range(4):
                    ps, pt = self.bank()
                    for ki in range(2):
                        k8 = 2 * kp + ki
                        k1 = 8 * k1q + k8
                        base = 256 * ki
                        mm = [(0, k1, 0, True, False), (2, 64 + k1, 0, False, True),
                              (1, k1, 128, True, False), (0, 64 + k1, 128, False, True)]
                        for (mt, zrow, off, sta, sto) in mm:
                            self.op("pe", lambda e, ps=ps, Mb=Mb, k8=k8, mt=mt, zrow=zrow, off=off, sta=sta, sto=sto,
                                    base=base: e.matmul(
                                ps[:, base + off:base + off + 128], lhsT=Mb[:, k8, mt, :], rhs=ZT[:, zrow, :],
                                start=sta, stop=sto), reads=[tM, t_ZT], writes=pt)
                    k1a = 8 * k1q + 2 * kp
                    self.copy(self.ev(), Yb[:, :, :, k1a:k1a + 2], ps.rearrange("p (k r c) -> p c r k", k=2, r=2), pt,
                              t_sub, after=t_O1)
            if dmp and conv == 0:
                self.dump("H_Yb", Yb.rearrange("p c r k -> p (c r k)"), [128, 16384], BF16, t_sub)
            if hl < 6:
                return
            def products(sb_):
                slot = sb_ % 2
                KAb, tK = KAs[sb_ % 4], t_KA[sb_ % 4]
                P1, P2, tP = P1s[slot], P2s[slot], t_P[slot]
                self.dma(KAb.rearrange("p c h k -> p (c h k)"),
                         self.s_ka[conv, cb][:, 2048 * sb_:2048 * sb_ + 2048], [], [tK], "KAs%d" % (sb_ % 4))
                Ysub = Yb[:, 16 * sb_:16 * sb_ + 16]
                self.op("dve" if sb_ % 2 == 0 else "pool", lambda e, Ysub=Ysub, KAb=KAb, P1=P1: e.tensor_tensor(
                    out=P1[:, :, :, :], in0=Ysub[:, :, 0:1, :].broadcast_to([128, 16, 2, 64]), in1=KAb[:, :, :, :],
                    op=ALU.mult), reads=[t_sub[sb_], tK], writes=[tP])
                self.op("dve", lambda e, Ysub=Ysub, KAb=KAb, P2=P2: e.scalar_tensor_tensor(
                    out=P2[:, :, 0, :], in0=Ysub[:, :, 1, :], scalar=-1.0, in1=KAb[:, :, 1, :],
                    op0=ALU.mult, op1=ALU.mult), reads=[t_sub[sb_], tK], writes=[tP])
                self.op("dve", lambda e, Ysub=Ysub, KAb=KAb, P2=P2: e.tensor_tensor(
                    out=P2[:, :, 1, :], in0=Ysub[:, :, 1, :], in1=KAb[:, :, 0, :], op=ALU.mult),
                    reads=[t_sub[sb_], tK], writes=[tP])

            def stage1p(sb_):
                slot = sb_ % 2
                P1, P2, tP = P1s[slot], P2s[slot], t_P[slot]
                for c4 in range(8):
                    ps, pt = self.bank()
                    for ci in range(2):
                        cl = 2 * c4 + ci
                        self.op("pe", lambda e, ps=ps, ci=ci, cl=cl, P1=P1: e.matmul(
                            ps[:, 256 * ci:256 * ci + 256], lhsT=P1[:, cl, :, :], rhs=self.G[:], start=True, stop=False),
                            reads=[tP, self.ctok], writes=pt)
                        self.op("pe", lambda e, ps=ps, ci=ci, cl=cl, P2=P2: e.matmul(
                            ps[:, 256 * ci:256 * ci + 256], lhsT=P2[:, cl, :, :], rhs=self.G[:], start=False, stop=True),
                            reads=[tP, self.ctok], writes=pt)
                    c0 = 16 * sb_ + 2 * c4
                    psv = ps.rearrange("p (c h n) -> p c h n", c=2, h=2)
                    eng_o = "dve" if c4 % 3 == 2 else "act"
                    self.copy(eng_o, O0[:, c0:c0 + 2, :], psv[:, :, 0, :], pt, [t_O0[sb_]], after=[t_ZT])
                    self.copy(eng_o, O1[:, c0:c0 + 2, :], psv[:, :, 1, :], pt, [t_O1[sb_]], after=[t_sub[sb_]])

            products(0)
            for sb_ in range(8):
                if sb_ + 1 < 8:
                    products(sb_ + 1)
                stage1p(sb_)
            if hl < 7:
                return
            Os = [O0, O1]
            tOs = [t_O0, t_O1]
            if dmp and conv == 0:
                self.dump("H_O0", O0.rearrange("p c n -> p (c n)"), [128, 16384], BF16, t_O0)
                self.dump("H_O1", O1.rearrange("p c n -> p (c n)"), [128, 16384], BF16, t_O1)
            if conv == 0:
                t_U2 = Tok()
                for bq16 in range(8):
                    Ab, tA = As[ai % 2], t_A[ai % 2]
                    self.dma(Ab, self.T["t_A"][:, 16 * bq16:16 * bq16 + 16], [], [tA], "As%d" % (ai % 2))
                    ai += 1
                    for b4 in range(2):
                        ps, pt = self.bank()
                        for bi in range(4):
                            bp_ = 8 * bq16 + 4 * b4 + bi
                            for e_ in range(2):
                                b = 2 * bp_ + e_
                                hb, b64 = b // 64, b % 64
                                for g in range(2):
                                    self.op("pe", lambda e, ps=ps, bi=bi, e_=e_, g=g, hb=hb, b64=b64, Ab=Ab,
                                            bl=b - 16 * bq16: e.matmul(
                                        ps[64 * e_:64 * e_ + 64, 128 * bi:128 * bi + 128], lhsT=Ab[:, bl, g, :],
                                        rhs=Os[hb][:, :, 64 * g + b64], start=(g == 0), stop=(g == 1)),
                                        reads=[tA] + tOs[hb], writes=pt)
                        bq = 8 * bq16 + 4 * b4
                        self.op("dve", lambda e, ps=ps, bq=bq: e.tensor_tensor(
                            out=Uv[:, :, bq:bq + 4], in0=ps.rearrange("p (b c) -> p c b", b=4),
                            in1=Ux1[:, :, bq:bq + 4], op=ALU.mult), reads=pt + [t_Ux1], writes=[t_Uz], after=[t_Uv])
                if dmp:
                    self.dump("H_U2", Uv.rearrange("p c b -> p (c b)"), [128, L], BF16, [t_Uz])
            else:
                if hl < 9:
                    return
                for bq16 in range(8):
                    Ab, tA = As[ai % 2], t_A[ai % 2]
                    self.dma(Ab, self.T["t_A"][:, 16 * bq16:16 * bq16 + 16], [], [tA], "As%d" % (ai % 2))
                    ai += 1
                    for b8 in range(2):
                        ps, pt = self.bank()
                        for bi in range(8):
                            bl = 8 * b8 + bi
                            b = 16 * bq16 + bl
                            hb, b64 = b // 64, b % 64
                            for g in range(2):
                                self.op("pe", lambda e, ps=ps, bi=bi, g=g, hb=hb, b64=b64, Ab=Ab, bl=bl: e.matmul(
                                    ps[:, 64 * bi:64 * bi + 64], lhsT=Os[hb][:, :, 64 * g + b64], rhs=Ab[:, bl, g, :],
                                    start=(g == 0), stop=(g == 1)), reads=[tA] + tOs[hb], writes=pt)
                        i0 = 64 * (16 * bq16 + 8 * b8)
                        self.op("dve", lambda e, ps=ps, i0=i0: e.tensor_tensor(
                            out=shg[:, i0:i0 + 512], in0=ps, in1=shg[:, i0:i0 + 512], op=ALU.mult),
                            reads=pt + [t_g2c[i0 // 512]], writes=[t_uh])
        self.dma(self.s_u[1, cb], shg, [t_uh], [self.tok_uH], "u_stH")

    def tail(self, s):
        KB = 1024
        wt = self.av(0, [128, 8, 5120], BF16)
        xnTs = [self.av((80 + 8 * i) * KB, [128, 8, 512], BF16) for i in range(2)]
        ufs = [self.av((96 + 8 * i) * KB, [128, 8, 512], BF16) for i in range(2)]
        uhs = [self.av((112 + 8 * i) * KB, [128, 8, 512], BF16) for i in range(2)]
        xts = [self.av((128 + 16 * i) * KB, [128, 4, D], F32) for i in range(2)]
        mrg = self.av(160 * KB, [128, 8, 512], BF16)
        gf = self.av(168 * KB, [128, 512], BF16)
        gh = self.av(169 * KB, [128, 512], BF16)
        t1 = self.av(170 * KB, [128, 512], F32)
        t2 = self.av(172 * KB, [128, 512], F32)
        sqj = self.av(174 * KB, [128, 512], BF16)
        st = self.sb("tl_st%d" % s, [128, 16 * 4 * 4], F32)
        t_wt = Tok()
        self.dma(wt.rearrange("p k n -> p (k n)"), self.s_wt.rearrange("p k n -> p (k n)"), [], [t_wt], "wt")
        t_xT, t_uf, t_uh, t_x = [Tok(), Tok()], [Tok(), Tok()], [Tok(), Tok()], [Tok(), Tok()]
        t_mrg, t_gf, t_gh, t_t1, t_t2, t_sq = MT("mrg"), Tok(), Tok(), Tok(), Tok(), Tok()
        tmpo = self.av(176 * KB, [128, 512], F32)
        t_tmpo = Tok()
        xsrc = self.xs[s].rearrange("(a b) d -> b a d", b=128)
        ydst = self.y[s].rearrange("(a b) d -> b a d", b=128)
        for q in range(16):
            sl = q % 2
            xnT, uf, uh, xt = xnTs[sl], ufs[sl], uhs[sl], xts[sl]

            def loads(q2):
                s2 = q2 % 2
                self.dma(xnTs[s2].rearrange("p k t -> p (k t)"), self.s_xnT[q2], [], [t_xT[s2]], "xnT%d" % s2)
                self.dma(ufs[s2], self.s_u[0].rearrange("c p t -> p c t")[:, :, 512 * q2:512 * q2 + 512], [],
                         [t_uf[s2]], "tuf%d" % s2)
                self.dma(uhs[s2], self.s_u[1].rearrange("c p t -> p c t")[:, :, 512 * q2:512 * q2 + 512], [],
                         [t_uh[s2]], "tuh%d" % s2)
                for i in range(4):
                    for e_ in range(2):
                        b = 2 * (4 * q2 + i) + e_
                        self.dma(xts[s2][64 * e_:64 * e_ + 64, i, :], xsrc[b], [], [t_x[s2]], "tx%d" % s2,
                                 last=(i == 3 and e_ == 1))
            if q == 0:
                loads(0)
            if q + 1 < 16:
                loads(q + 1)
            for db in range(8):
                banks = [self.bank() for _ in range(4)]
                srcs = [(uf, t_uf[sl], 2048), (uh, t_uh[sl], 3072), (xnT, t_xT[sl], 0), (xnT, t_xT[sl], 1024)]
                for bi, (src, tsrc, wcol) in enumerate(srcs):
                    ps, pt = banks[bi]
                    for kc in range(8):
                        self.op("pe", lambda e, ps=ps, kc=kc, src=src, wcol=wcol, db=db: e.matmul(
                            ps[:, :], lhsT=wt[:, kc, wcol + 128 * db:wcol + 128 * db + 128], rhs=src[:, kc, :],
                            start=(kc == 0), stop=(kc == 7)), reads=[t_wt, tsrc], writes=pt)
                self.op("act", lambda e, ps=banks[2][0], db=db: e.activation(
                    out=gf[:, :], in_=ps, func=AF.Sigmoid, bias=self.bm[:, db:db + 1], scale=1.0),
                    reads=banks[2][1] + [self.ctok], writes=[t_gf])
                self.op("act", lambda e, ps=banks[3][0], db=db: e.activation(
                    out=gh[:, :], in_=ps, func=AF.Sigmoid, bias=self.bm[:, 8 + db:9 + db], scale=1.0),
                    reads=banks[3][1] + [self.ctok], writes=[t_gh])
                self.op("dve", lambda e, ps=banks[0][0]: e.tensor_tensor(out=t1[:, :], in0=ps, in1=gf[:, :], op=ALU.mult),
                        reads=banks[0][1] + [t_gf], writes=[t_t1])
                self.op("dve", lambda e, ps=banks[1][0]: e.tensor_tensor(out=t2[:, :], in0=ps, in1=gh[:, :], op=ALU.mult),
                        reads=banks[1][1] + [t_gh], writes=[t_t2])
                self.op("pool", lambda e, db=db: e.tensor_tensor(out=mrg[:, db, :], in0=t1[:, :], in1=t2[:, :], op=ALU.add),
                        reads=[t_t1, t_t2], writes=[t_mrg])
            for i in range(4):
                c0 = 16 * q + 4 * i
                pss = [self.bank() for _ in range(2)]
                t_st = Tok()
                for hf in range(2):
                    ps, pt = pss[hf]
                    for db in range(8):
                        self.op("pe", lambda e, ps=ps, db=db, i=i, hf=hf: e.matmul(
                            ps[:, :], lhsT=mrg[:, db, 128 * i:128 * i + 128],
                            rhs=wt[:, db, 4096 + 512 * hf:4096 + 512 * hf + 512], start=(db == 0), stop=(db == 7)),
                            reads=[t_mrg, t_wt], writes=pt)
                    self.op("act", lambda e, ps=ps, c0=c0, hf=hf: e.activation(
                        out=sqj[:, :], in_=ps, func=AF.Square, accum_out=st[:, c0 + hf:c0 + hf + 1]),
                        reads=pt, writes=[t_sq, t_st])
                self.op("dve", lambda e, c0=c0: e.tensor_tensor(
                    out=st[:, c0 + 2:c0 + 3], in0=st[:, c0:c0 + 1], in1=st[:, c0 + 1:c0 + 2], op=ALU.add),
                    reads=[t_st], writes=[t_st])
                self.op("act", lambda e, c0=c0: e.activation(
                    out=st[:, c0 + 3:c0 + 4], in_=st[:, c0 + 2:c0 + 3], func=AF.Sqrt, scale=1.0 / D, bias=self.epsb[:]),
                    reads=[t_st, self.ctok], writes=[t_st])
                self.op("dve", lambda e, c0=c0: e.reciprocal(out=st[:, c0 + 3:c0 + 4], in_=st[:, c0 + 3:c0 + 4]),
                        reads=[t_st], writes=[t_st])
                for hf in range(2):
                    ps, pt = pss[hf]
                    self.op("dve", lambda e, ps=ps, c0=c0, hf=hf, tmpo=tmpo: e.scalar_tensor_tensor(
                        out=tmpo[:, :], in0=ps, scalar=st[:, c0 + 3:c0 + 4],
                        in1=self.gpost[:, 512 * hf:512 * hf + 512], op0=ALU.mult, op1=ALU.mult),
                        reads=pt + [t_st, self.ctok], writes=[t_tmpo])
                    self.op("pool", lambda e, hf=hf, i=i, xt=xt, tmpo=tmpo: e.tensor_tensor(
                        out=xt[:, i, 512 * hf:512 * hf + 512], in0=xt[:, i, 512 * hf:512 * hf + 512],
                        in1=tmpo[:, :], op=ALU.add), reads=[t_tmpo, t_x[sl]], writes=[t_x[sl]])
            for i in range(4):
                for e_ in range(2):
                    b = 2 * (4 * q + i) + e_
                    self.dma(ydst[b], xt[64 * e_:64 * e_ + 64, i, :], [t_x[sl]], [Tok()], "ty%d" % sl,
                             last=(i == 3 and e_ == 1))


_CACHE = {}


def _get_nc(dbg=None):
    key = repr(sorted((dbg or {}).items()))
    if key not in _CACHE:
        k = K(dbg)
        nc = k.build()
        _CACHE[key] = (nc, k)
    return _CACHE[key]


def _weights_layout(inp):
    f32 = lambda a: np.ascontiguousarray(np.asarray(a, np.float32))
    w = {}
    w["w_in"] = f32(inp["w_in"][0])
    w["g_pre"] = f32(inp["g_pre"][0]).reshape(1, D)
    w["g_post"] = f32(inp["g_post"][0]).reshape(1, D)
    ws = f32(inp["w_short"][0])
    w["ws_t"] = np.ascontiguousarray(ws.reshape(3, 3, 8, 128).transpose(3, 1, 2, 0).reshape(128, 72))
    w["fw1"] = f32(inp["filt_w1"][0])
    w["fb1"] = f32(inp["filt_b1"][0]).reshape(64, 1)
    w["ff1"] = f32(inp["filt_freq1"][0]).reshape(64, 1)
    w["fw2"] = f32(inp["filt_w2"][0])
    w["fb2"] = f32(inp["filt_b2"][0]).reshape(64, 1)
    w["ff2"] = f32(inp["filt_freq2"][0]).reshape(64, 1)
    w["fw3"] = f32(inp["filt_w3"][0])
    w["fdecay"] = f32(inp["filt_decay"][0]).reshape(4, D)
    w["hd"] = f32(inp["hyena_d"][0])
    w["wf"] = f32(inp["w_fourier_out"][0])
    w["wh"] = f32(inp["w_hyena_out"][0])
    w["wo"] = f32(inp["w_out"][0])
    w["bm_t"] = np.ascontiguousarray(f32(inp["b_merge"][0]).reshape(16, 128).T)
    return w


def kernel(**inputs):
    nc, k = _get_nc()
    xp = np.asarray(inputs["x_prompt"], np.float32)
    xsm = np.asarray(inputs["x_sample"], np.float32)
    w = _weights_layout(inputs)
    tabs = _tables()
    in_maps = []
    for c in range(NCORES):
        second = xp[c] if c < 2 else xsm[c]
        m = {"xs": np.ascontiguousarray(np.stack([xsm[c], second], 0))}
        m.update(w)
        m.update(tabs)
        in_maps.append(m)
    res = run_bass_kernel_spmd(nc, in_maps, core_ids=list(range(NCORES)))
    y_sample = np.stack([res.results[c]["y"][0] for c in range(NCORES)], 0)
    y_prompt = np.stack([res.results[c]["y"][1] for c in range(2)], 0)
    return (np.ascontiguousarray(y_prompt, np.float32), np.ascontiguousarray(y_sample, np.float32))
```
